# Optimizing a Trainium2 kernel written in Bass

```python
import jax, jax.numpy as jnp
from jax import lax
import numpy as np

D_MODEL = 1024
BATCH = 2
SEQ = 8192
DEPTH = 4

CHUNK = 64
N_LEFT_CHUNKS = 8
BAND = (N_LEFT_CHUNKS + 1) * CHUNK
HEAD_DIM = 64
N_HEADS_A = 8
N_HEADS_B = 8
A_W = N_HEADS_A * HEAD_DIM
B_W = N_HEADS_B * HEAD_DIM
MIX_W = A_W + B_W
REL_CLIP = 256
Q_BLOCK = 128
LRU_WIDTH = D_MODEL
LRU_BLOCKS = 4
LRU_BLOCK_W = LRU_WIDTH // LRU_BLOCKS
CONV_WIDTH = 4
LRU_C = 8.0
D_FF = -(-(8 * D_MODEL) // (3 * 256)) * 256
RMS_EPS = 1e-6
N_ATTN_LAYERS = (DEPTH + 1) // 2
N_REC_LAYERS = DEPTH // 2

kernel_name = "hybrid_chunked_sb_rglru_trunk"


def rmsnorm(x, g):
    xf = x.astype(jnp.float32)
    y = xf * lax.rsqrt(jnp.mean(xf * xf, axis=-1, keepdims=True) + RMS_EPS)
    return (y * g.astype(jnp.float32)).astype(x.dtype)


def chunked_relpos_attention(q, k, v, rel_bias):
    b, s, h, dh = q.shape
    nc = s // CHUNK
    qc = q.reshape(b, nc, CHUNK, h, dh)

    def gather_band(t):
        tc = t.reshape(b, nc, CHUNK, h, dh)
        tp = jnp.pad(tc, ((0, 0), (N_LEFT_CHUNKS, 0), (0, 0), (0, 0), (0, 0)))
        return jnp.concatenate([tp[:, j:j + nc] for j in range(N_LEFT_CHUNKS + 1)], axis=2)

    kb, vb = gather_band(k), gather_band(v)
    scores = jnp.einsum('bcqhd,bckhd->bhcqk', qc, kb).astype(jnp.float32) * (dh ** -0.5)
    qpos = N_LEFT_CHUNKS * CHUNK + jnp.arange(CHUNK)
    kpos = jnp.arange(BAND)
    rel = jnp.clip(qpos[:, None] - kpos[None, :], -REL_CLIP, REL_CLIP) + REL_CLIP
    bias = rel_bias[:, rel].astype(jnp.float32)
    key_chunk = jnp.arange(nc)[:, None] - N_LEFT_CHUNKS + (kpos // CHUNK)[None, :]
    valid = key_chunk >= 0
    scores = scores + bias[None, :, None, :, :]
    scores = jnp.where(valid[None, None, :, None, :], scores, -jnp.inf)
    p = jax.nn.softmax(scores, axis=-1).astype(v.dtype)
    o = jnp.einsum('bhcqk,bckhd->bcqhd', p, vb)
    return o.reshape(b, s, h * dh)


def stick_breaking_attention(q, k, v):
    b, s, h, dh = q.shape
    nb = s // Q_BLOCK
    scale = dh ** -0.5
    kpos = jnp.arange(s)
    q_blocks = q.reshape(b, nb, Q_BLOCK, h, dh).transpose(1, 0, 2, 3, 4)

    def one_block(args):
        q_blk, blk = args
        z = jnp.einsum('bqhd,bkhd->bhqk', q_blk, k).astype(jnp.float32) * scale
        qpos = blk * Q_BLOCK + jnp.arange(Q_BLOCK)
        causal = kpos[None, :] < qpos[:, None]
        log_beta = jax.nn.log_sigmoid(z)
        log_1m_beta = jnp.where(causal, jax.nn.log_sigmoid(-z), 0.0)
        after = lax.cumsum(log_1m_beta, axis=3, reverse=True) - log_1m_beta
        w = jnp.where(causal, jnp.exp(log_beta + after), 0.0).astype(v.dtype)
        return jnp.einsum('bhqk,bkhd->bqhd', w, v)

    o = lax.map(one_block, (q_blocks, jnp.arange(nb)))
    return o.transpose(1, 0, 2, 3, 4).reshape(b, s, h * dh)


def attention_mixer(h, w_in, rel_bias, w_out):
    b, s, _ = h.shape
    proj = h @ w_in
    part_a, part_b = proj[..., :3 * A_W], proj[..., 3 * A_W:]
    qa, ka, va = [t.reshape(b, s, N_HEADS_A, HEAD_DIM) for t in jnp.split(part_a, 3, axis=-1)]
    qs, ks, vs = [t.reshape(b, s, N_HEADS_B, HEAD_DIM) for t in jnp.split(part_b, 3, axis=-1)]
    out_a = chunked_relpos_attention(qa, ka, va, rel_bias)
    out_b = stick_breaking_attention(qs, ks, vs)
    return jnp.concatenate([out_a, out_b], axis=-1) @ w_out


def recurrent_mixer(h, w_in, conv_w, conv_b, w_a, b_a, w_i, b_i, lam, w_out):
    b, s, _ = h.shape
    proj = h @ w_in
    gate, xr = jnp.split(proj, 2, axis=-1)
    gate = jax.nn.gelu(gate, approximate=True)
    xc = lax.conv_general_dilated(
        xr, conv_w, window_strides=(1,), padding=[(CONV_WIDTH - 1, 0)],
        dimension_numbers=('NWC', 'WIO', 'NWC'), feature_group_count=LRU_WIDTH) + conv_b
    xg = xc.reshape(b, s, LRU_BLOCKS, LRU_BLOCK_W)
    r = jax.nn.sigmoid(jnp.einsum('bsni,nij->bsnj', xg, w_a) + b_a).reshape(b, s, LRU_WIDTH)
    i = jax.nn.sigmoid(jnp.einsum('bsni,nij->bsnj', xg, w_i) + b_i).reshape(b, s, LRU_WIDTH)
    log_a = -LRU_C * r.astype(jnp.float32) * jax.nn.softplus(-lam.astype(jnp.float32))
    a = jnp.exp(log_a)
    mult = jnp.sqrt(-jnp.expm1(2.0 * log_a))
    u = mult * (i * xc).astype(jnp.float32)

    def combine(left, right):
        a1, b1 = left
        a2, b2 = right
        return a1 * a2, a2 * b1 + b2

    _, hs = lax.associative_scan(combine, (a, u), axis=1)
    return (hs.astype(h.dtype) * gate) @ w_out


def swiglu(h, w_gate, w_up, w_down):
    return (jax.nn.silu(h @ w_gate) * (h @ w_up)) @ w_down


def setup_inputs(seed: int = 0) -> dict:
    key = jax.random.key(seed)
    ks = jax.random.split(key, 24)
    f32 = jnp.float32

    def nrm(k, shape, fan_in):
        return jax.random.normal(k, shape, f32) * (fan_in ** -0.5)

    def gain(k, shape):
        return 1.0 + 0.05 * jax.random.normal(k, shape, f32)

    u = jax.random.uniform(ks[12], (N_REC_LAYERS, LRU_WIDTH), f32, 0.9, 0.999)
    base = u ** (1.0 / LRU_C)
    lam = jnp.log(base) - jnp.log1p(-base)
    return {
        'x': jax.random.normal(ks[0], (BATCH, SEQ, D_MODEL), f32),
        'attn_w_in': nrm(ks[1], (N_ATTN_LAYERS, D_MODEL, 3 * MIX_W), D_MODEL),
        'attn_rel_bias': 0.2 * jax.random.normal(ks[2], (N_ATTN_LAYERS, N_HEADS_A, 2 * REL_CLIP + 1), f32),
        'attn_w_out': nrm(ks[3], (N_ATTN_LAYERS, MIX_W, D_MODEL), MIX_W),
        'rg_w_in': nrm(ks[4], (N_REC_LAYERS, D_MODEL, 2 * LRU_WIDTH), D_MODEL),
        'rg_conv_w': nrm(ks[5], (N_REC_LAYERS, CONV_WIDTH, 1, LRU_WIDTH), CONV_WIDTH),
        'rg_conv_b': 0.01 * jax.random.normal(ks[6], (N_REC_LAYERS, LRU_WIDTH), f32),
        'rg_w_a': nrm(ks[7], (N_REC_LAYERS, LRU_BLOCKS, LRU_BLOCK_W, LRU_BLOCK_W), LRU_BLOCK_W),
        'rg_b_a': 0.01 * jax.random.normal(ks[8], (N_REC_LAYERS, LRU_BLOCKS, LRU_BLOCK_W), f32),
        'rg_w_i': nrm(ks[9], (N_REC_LAYERS, LRU_BLOCKS, LRU_BLOCK_W, LRU_BLOCK_W), LRU_BLOCK_W),
        'rg_b_i': 0.01 * jax.random.normal(ks[10], (N_REC_LAYERS, LRU_BLOCKS, LRU_BLOCK_W), f32),
        'rg_lambda': lam,
        'rg_w_out': nrm(ks[11], (N_REC_LAYERS, LRU_WIDTH, D_MODEL), LRU_WIDTH),
        'norm_mix_pre': gain(ks[13], (DEPTH, D_MODEL)),
        'norm_mix_post': gain(ks[14], (DEPTH, D_MODEL)),
        'norm_ffn_pre': gain(ks[15], (DEPTH, D_MODEL)),
        'norm_ffn_post': gain(ks[16], (DEPTH, D_MODEL)),
        'ffn_w_gate': nrm(ks[17], (DEPTH, D_MODEL, D_FF), D_MODEL),
        'ffn_w_up': nrm(ks[18], (DEPTH, D_MODEL, D_FF), D_MODEL),
        'ffn_w_down': nrm(ks[19], (DEPTH, D_FF, D_MODEL), D_FF),
    }


def reference(x, attn_w_in, attn_rel_bias, attn_w_out, rg_w_in, rg_conv_w, rg_conv_b,
              rg_w_a, rg_b_a, rg_w_i, rg_b_i, rg_lambda, rg_w_out,
              norm_mix_pre, norm_mix_post, norm_ffn_pre, norm_ffn_post,
              ffn_w_gate, ffn_w_up, ffn_w_down):
    for layer in range(DEPTH):
        j = layer // 2
        h = rmsnorm(x, norm_mix_pre[layer])
        if layer % 2 == 0:
            m = attention_mixer(h, attn_w_in[j], attn_rel_bias[j], attn_w_out[j])
        else:
            m = recurrent_mixer(h, rg_w_in[j], rg_conv_w[j], rg_conv_b[j], rg_w_a[j], rg_b_a[j],
                                rg_w_i[j], rg_b_i[j], rg_lambda[j], rg_w_out[j])
        x = x + rmsnorm(m, norm_mix_post[layer])
        h = rmsnorm(x, norm_ffn_pre[layer])
        f = swiglu(h, ffn_w_gate[layer], ffn_w_up[layer], ffn_w_down[layer])
        x = x + rmsnorm(f, norm_ffn_post[layer])
    return x
```

```python
import numpy as np
from contextlib import ExitStack
import concourse.bass as bass
import concourse.mybir as mybir
from concourse.bass_utils import run_bass_kernel_spmd

F32 = mybir.dt.float32
BF16 = mybir.dt.bfloat16
AF = mybir.ActivationFunctionType
ALU = mybir.AluOpType

ENGS = ("pe", "act", "dve", "pool", "sp")


class Tok:
    __slots__ = ("sem", "val", "eng")

    def __init__(self, sem, val, eng=None):
        self.sem = sem
        self.val = val
        self.eng = eng


class Chan:
    def __init__(self, sem):
        self.sem = sem
        self.total = 0


def _bbox(ap):
    t = ap.tensor
    shape = list(t.shape)
    pat = [(int(s), int(n)) for s, n in ap.ap]
    off = int(ap.offset)
    space = str(ap.space)
    if "dram" in space.lower():
        ext = sum((n - 1) * abs(s) for s, n in pat)
        return (0, 1, off, off + ext + 1)
    row = 1
    for d in shape[1:]:
        row *= int(d)
    p0 = off // row
    f0 = off % row
    s0, n0 = pat[0]
    if s0 != 0 and s0 % row == 0:
        pstep = s0 // row
        p1 = p0 + (n0 - 1) * pstep + 1
        rest = pat[1:]
    elif s0 == 0:
        p1 = p0 + 1
        rest = pat[1:]
    else:
        p1 = p0 + 1
        rest = pat
    ext = sum((n - 1) * abs(s) for s, n in rest)
    return (p0, p1, f0, f0 + ext + 1)


class FW:
    def __init__(self, nc, sync_same_engine=True):
        self.nc = nc
        self.ops = {e: [] for e in ENGS}
        self.sem = {}
        self.cnt = {e: 0 for e in ENGS}
        self.waited = {e: {} for e in ENGS}
        self.acc = {}
        self.sync_same = sync_same_engine
        self._stack = []
        self.n_instr = 0
        self.n_wait = 0
        for e in ENGS:
            self.sem[e] = self._sem("c_" + e)
        self.chans = {}

    def _sem(self, name):
        cm = self.nc.semaphore(name)
        s = cm.__enter__()
        self._stack.append(cm)
        return s

    def chan(self, name):
        if name not in self.chans:
            self.chans[name] = Chan(self._sem("d_" + name))
        return self.chans[name]

    def _deps(self, reads, writes):
        deps = []
        for ap, is_w in [(a, False) for a in reads] + [(a, True) for a in writes]:
            name = ap.tensor.name
            bb = _bbox(ap)
            for rec in self.acc.get(name, ()):
                rb, tok, rw = rec
                if not (is_w or rw):
                    continue
                if rb[0] < bb[1] and bb[0] < rb[1] and rb[2] < bb[3] and bb[2] < rb[3]:
                    deps.append((tok, rw, is_w))
        return deps

    def _record(self, reads, writes, tok):
        for ap in writes:
            name = ap.tensor.name
            bb = _bbox(ap)
            lst = self.acc.setdefault(name, [])
            lst[:] = [r for r in lst if not (bb[0] <= r[0][0] and r[0][1] <= bb[1]
                                             and bb[2] <= r[0][2] and r[0][3] <= bb[3])]
            lst.append([bb, tok, True])
        for ap in reads:
            name = ap.tensor.name
            bb = _bbox(ap)
            lst = self.acc.setdefault(name, [])
            lst[:] = [r for r in lst if not ((not r[2]) and r[1].sem is tok.sem
                                             and bb[0] <= r[0][0] and r[0][1] <= bb[1]
                                             and bb[2] <= r[0][2] and r[0][3] <= bb[3])]
            lst.append([bb, tok, False])

    def _emit_waits(self, e, deps, is_dma=False):
        need = {}
        for tok, rw, is_w in deps:
            if tok.eng == e and not is_dma:
                if e == "pe":
                    continue
                if not self.sync_same:
                    continue
                if not rw:
                    continue
            k = id(tok.sem)
            if k not in need or need[k][1] < tok.val:
                need[k] = (tok.sem, tok.val)
        for k, (sem, val) in need.items():
            if self.waited[e].get(k, 0) >= val:
                continue
            self.waited[e][k] = val
            self.n_wait += 1
            self.ops[e].append(("wait", sem, val))

    def op(self, e, fn, reads, writes):
        deps = self._deps(reads, writes)
        self._emit_waits(e, deps)
        self.cnt[e] += 1
        tok = Tok(self.sem[e], self.cnt[e], e)
        self.ops[e].append(("op", fn, self.sem[e]))
        self._record(reads, writes, tok)
        self.n_instr += 1
        return tok

    def dma(self, q, out, in_, chan, rd=None, wr=None, **kw):
        ch = self.chan(chan)
        r = (list(rd) if isinstance(rd, (list, tuple)) else [rd]) if rd is not None else [in_]
        w = [wr if wr is not None else out]
        deps = self._deps(r, w)
        if ch.total > 0:
            deps.append((Tok(ch.sem, ch.total, None), True, True))
        self._emit_waits(q, deps, is_dma=True)
        ch.total += 16
        tok = Tok(ch.sem, ch.total, None)
        self.ops[q].append(("dma", out, in_, ch.sem, kw))
        self._record(r, w, tok)
        self.n_instr += 1
        return tok

    def collective(self, kind, ins, outs, groups, chan):
        ch = self.chan(chan)
        deps = self._deps(ins, outs)
        self._emit_waits("pool", deps, is_dma=True)
        ch.total += 1
        tok = Tok(ch.sem, ch.total, None)
        self.ops["pool"].append(("cc", kind, ins, outs, groups, ch.sem))
        self._record(ins, outs, tok)
        return tok

    def wait_all(self, e, aps):
        deps = self._deps([], aps)
        self._emit_waits(e, deps, is_dma=True)

    def _replay(self, e, h):
        for item in self.ops[e]:
            k = item[0]
            if k == "wait":
                h.wait_ge(item[1], item[2])
            elif k == "op":
                item[1](h).then_inc(item[2], 1)
            elif k == "dma":
                o = item[1](h) if callable(item[1]) else item[1]
                i = item[2](h) if callable(item[2]) else item[2]
                h.dma_start(out=o, in_=i, **item[4]).then_inc(item[3], 16)
            elif k == "cc":
                _, kind, ins, outs, groups, sem = item
                h.collective_compute(kind, ALU.bypass, replica_groups=groups,
                                     ins=[a.opt() for a in ins],
                                     outs=[a.opt() for a in outs]).then_inc(sem, 1)

    def finish(self):
        nc = self.nc
        with nc.Block() as block:
            @block.tensor
            def _(h):
                self._replay("pe", h)

            @block.scalar
            def _(h):
                self._replay("act", h)

            @block.vector
            def _(h):
                self._replay("dve", h)

            @block.gpsimd
            def _(h):
                self._replay("pool", h)

            @block.sync
            def _(h):
                self._replay("sp", h)
        for cm in reversed(self._stack):
            cm.__exit__(None, None, None)
        self._stack = []


D = 1024
KC = 8
S = 8192
NT = 2048
TT = 512
DFF = 2816
FC = 22
DEPTH = 4
EPS = 1e-6
GROUPS = [[0, 1, 2, 3], [4, 5, 6, 7]]

_SEG_R = {"const": 816, "attn": 4096, "ffn": 8448, "rec": 3072}


def seg_layout(plan):
    keys = [("const", 0)]
    for st in plan:
        keys.append((st[0], st[1]))
    rows = [_SEG_R[k] for k, _ in keys]
    tot = sum(rows) // 4
    acc, split = 0, tot
    for i, r in enumerate(rows):
        if acc + r // 4 > 6400:
            split = acc
            break
        acc += r // 4
    if split == tot:
        split = rows[0] // 4
    return keys, rows, tot, split

BA_SIZE = 43008
FA_SIZE = 8192


class Prog:
    def __init__(self, plan, load_x=True, store_x=True):
        self.plan = plan
        global SEG_KEYS, SEG_ROWS, SEG_INDEX, WSH_ROWS, WSH_SPLIT
        SEG_KEYS, SEG_ROWS, WSH_ROWS, WSH_SPLIT = seg_layout(plan)
        SEG_INDEX = {k: i for i, k in enumerate(SEG_KEYS)}
        nc = self.nc = bass.Bass("TRN2", target_bir_lowering=False)
        self.fw = FW(nc)
        self.es = ExitStack()
        dt = nc.dram_tensor

        def ext(name, shape, dtype=F32):
            return dt(name, shape, dtype, kind="ExternalInput").ap()

        self.d_xT = ext("xT", [D, NT])
        self.d_gains = ext("gains", [128, 128])
        self.d_wsh = [ext("wsh0", [WSH_SPLIT, 1024]), ext("wsh1", [WSH_ROWS - WSH_SPLIT, 1024])]
        self.d_bext = ext("biasext", [2, 2, 1536])
        self.d_jmat = ext("jmat", [128, 128])
        self.d_rwa = ext("r_wa", [2, 128, 512])
        self.d_rwi = ext("r_wi", [2, 128, 512])
        self.d_rsm = ext("r_small", [2, 128, 16])
        self.d_wshb = dt("wsh_bf", [WSH_ROWS, 1024], BF16).ap()
        self.d_wall = [dt("wall%d" % i, [SEG_ROWS[i], 1024], BF16).ap() for i in range(len(SEG_ROWS))]
        self.d_yT = dt("yT", [D, NT], F32, kind="ExternalOutput").ap()
        self.d_qkv_s = dt("qkv_s", [3072, 2048], BF16).ap()
        self.d_qkv_r = dt("qkv_r", [4 * 3072, 2048], BF16).ap()
        self.d_myqkv = dt("myqkv", [3072, 2048], BF16).ap()
        self.d_myrec = dt("myrec", [2048, 2048], BF16).ap()
        self.d_att_s = dt("att_s", [1024, 2048], BF16).ap()
        self.d_att_r = dt("att_r", [4096, 2048], BF16).ap()
        self.d_rec_s = dt("rec_s", [2048, 2048], BF16).ap()
        self.d_rec_r = dt("rec_r", [4 * 2048, 2048], BF16).ap()
        self.d_rec2_s = dt("rec2_s", [1024, 2048], BF16).ap()
        self.d_rec2_r = dt("rec2_r", [4096, 2048], BF16).ap()

        sb = lambda name, shape, dtype: self.es.enter_context(nc.sbuf_tensor(name, shape, dtype))
        self.xT = sb("xT_sb", [128, KC, NT], F32)
        self.gains = sb("gains_sb", [128, 128], F32)
        self.ones_f = sb("ones_f", [128, 128], F32)
        self.cst = sb("cst_sb", [128, 384], BF16)
        self.amask = sb("amask_sb", [128, 8, 512], BF16)
        self.bmask = sb("bmask_sb", [128, 4, 512], BF16)
        self.BA = sb("BA", [128, BA_SIZE], BF16)
        self.FA = sb("FA", [128, FA_SIZE], F32)
        self.ps = [self.es.enter_context(nc.psum_tensor("ps%d" % i, [128, 512], F32)) for i in range(8)]
        self.ones_b = self.cst[:, 0:128]
        self.tri_b = self.cst[:, 128:256]
        self.sl_b = self.cst[:, 256:384]
        self.evac_rr = 0

        fw = self.fw
        fw.dma("sp", self.gains[:], self.d_gains, "c0")
        self.jmat = sb("jmat_sb", [128, 128], F32)
        fw.dma("sp", self.jmat[:], self.d_jmat, "c4")
        self.wstage = sb("wstage", [128, 2 * 2048], BF16)
        self.sqb = sb("sqb", [128, 1024], BF16)
        self.ones_bb = sb("ones_bb", [128, 128], BF16)
        fw.op("dve", lambda h: h.memset(self.ones_bb[:], 1.0), [], [self.ones_bb[:]])
        self.seg_done = 0
        self.wst_i = 0
        self.step_i = 0
        self.prefetch_weights(2)
        cseg = self.d_wall[0]
        fw.dma("pool", self.cst[:], self.wtile(cseg, 0, 384), "c1")
        fw.dma("pool", self.amask[:], self.wtile(cseg, 128 * 384, 4096).rearrange("p (a b) -> p a b", a=8), "c2")
        fw.dma("pool", self.bmask[:], self.wtile(cseg, 128 * (384 + 4096), 2048).rearrange("p (a b) -> p a b", a=4), "c3")
        fw.op("dve", lambda h: h.memset(self.ones_f[:], 1.0), [], [self.ones_f[:]])
        if load_x:
            for c in range(KC):
                fw.dma("sp", self.xT[:, c, :], self.d_xT[c * 128:(c + 1) * 128, :], "x%d" % c)
        for si_, step in enumerate(plan):
            self.step_i = si_
            kind = step[0]
            if kind == "attn":
                self.attn_layer(step[1], step[2])
            elif kind == "rec":
                self.rec_layer(step[1], step[2])
            elif kind == "ffn":
                self.ffn_layer(step[1])
            else:
                raise ValueError(kind)
        if store_x:
            for c in range(KC):
                fw.dma("sp", self.d_yT[c * 128:(c + 1) * 128, :], self.xT[:, c, :], "x%d" % c)
            fw.wait_all("sp", [self.d_yT])
        fw.finish()
        self.es.close()

    def mm(self, out, lhsT, rhs, start, stop, skip=False):
        rd = [lhsT, rhs] + ([] if start else [out])
        return self.fw.op("pe", lambda h: h.matmul(out, lhsT, rhs, start=start, stop=stop,
                                                   skip_group_check=skip), rd, [out])

    def act(self, out, in_, func, scale=None, bias=None, eng="act"):
        kw = {}
        rd = [in_]
        if scale is not None:
            kw["scale"] = scale
            if not isinstance(scale, (int, float)):
                rd.append(scale)
        if bias is not None:
            kw["bias"] = bias
            if not isinstance(bias, (int, float)):
                rd.append(bias)
        return self.fw.op("act", lambda h: h.activation(out, in_, func, **kw), rd, [out])

    def tt(self, out, in0, in1, op, eng="dve"):
        return self.fw.op(eng, lambda h: h.tensor_tensor(out, in0, in1, op), [in0, in1], [out])

    def stt(self, out, in0, scalar, in1, op0, op1):
        rd = [in0, in1] + ([] if isinstance(scalar, (int, float)) else [scalar])
        return self.fw.op("dve", lambda h: h.scalar_tensor_tensor(out, in0, scalar, in1, op0, op1), rd, [out])

    def ts(self, out, in0, s1, s2, op0, op1=None, eng="dve"):
        rd = [in0] + [s for s in (s1, s2) if s is not None and not isinstance(s, (int, float))]
        if op1 is None:
            return self.fw.op(eng, lambda h: h.tensor_scalar(out, in0, s1, None, op0), rd, [out])
        return self.fw.op(eng, lambda h: h.tensor_scalar(out, in0, s1, s2, op0, op1), rd, [out])

    def copy(self, out, in_, eng="dve"):
        if eng == "act":
            return self.act(out, in_, AF.Copy)
        return self.fw.op(eng, lambda h: h.tensor_copy(out, in_), [in_], [out])

    def evac(self, out, in_):
        self.evac_rr ^= 1
        return self.copy(out, in_, eng="act" if self.evac_rr else "dve")

    def recip(self, out, in_):
        return self.fw.op("dve", lambda h: h.reciprocal(out, in_), [in_], [out])

    def gidx(self, kind, layer):
        return (kind * 4 + layer) * 8

    def ba(self, off, n):
        return self.BA[:, off:off + n]

    def fa(self, off, n):
        return self.FA[:, off:off + n]

    def dyn_rank(self, h):
        if getattr(self, "_rank_val", None) is None:
            self._rank_val = h.partition_id() % 4
            self._rank_col = self._rank_val * NT
        return self._rank_val

    def dyn_col(self, h):
        self.dyn_rank(h)
        return self._rank_col

    def dyn_val(self, h, mult, add):
        if not hasattr(self, "_dyn_cache"):
            self._dyn_cache = {}
        key = (mult, add)
        if key not in self._dyn_cache:
            self._dyn_cache[key] = self.dyn_rank(h) * mult + add
        return self._dyn_cache[key]

    def gather(self, send, recv, chunk):
        R = int(send.shape[0])
        k = 0
        while k < R:
            c = min(chunk, R - k)
            self.fw.collective("AllGather", [send[k:k + c, :]], [recv[4 * k:4 * k + 4 * c, :]], GROUPS, "cc")
            k += c

    def fetch_rows(self, gathered, dst, nblk, kind, chan):
        def src(h):
            v = gathered.rearrange("(h k r p) c -> h k r p c", h=4, k=nblk, r=4)
            w = v[bass.ds(self.dyn_rank(h), 1), kind]
            return w.rearrange("o r p c -> p (o r) c")
        rd = [gathered[(hh * nblk + kind) * 512:(hh * nblk + kind + 1) * 512, :] for hh in range(4)]
        self.fw.dma("sp", dst, src, chan, rd=rd)

    def fetch_cols(self, gathered, aT):
        for s_ in range(2):
            def src(h, s_=s_):
                v = gathered.rearrange("(r s h p) c -> r s h p c", r=4, s=2, h=4)
                w = v[bass.ds(self.dyn_rank(h), 1), s_]
                return w.rearrange("o h p c -> p (o h) c")
            dst = aT.rearrange("p (h s) c -> p h s c", s=2)[:, :, s_, :]
            self.fw.dma("sp", dst, src, "fc%d" % s_, rd=gathered)

    def wtile(self, seg, off, width, k=None):
        ap = bass.AP(tensor=seg.tensor, offset=int(seg.offset) + off, ap=[[width, 128], [1, width]])
        if k is not None:
            ap = ap.rearrange("p (k m) -> p k m", k=k)
        return ap

    def prefetch_weights(self, upto):
        fw = self.fw
        NB = 2
        upto = min(upto, len(SEG_ROWS))
        while self.seg_done < upto:
            si = self.seg_done
            o = sum(SEG_ROWS[:si]) // 4
            end = o + SEG_ROWS[si] // 4
            row = o
            while row < end:
                n = min(256, end - row)
                npart = n // 2
                i = self.wst_i
                self.wst_i += 1
                st = self.wstage[0:npart, 2048 * (i % NB):2048 * (i % NB + 1)].rearrange("p (a c) -> p a c", a=2)
                if row < WSH_SPLIT:
                    src = self.d_wsh[0][row:row + n, :].rearrange("(p a) c -> p a c", a=2)
                else:
                    src = self.d_wsh[1][row - WSH_SPLIT:row - WSH_SPLIT + n, :].rearrange("(p a) c -> p a c", a=2)
                dst = self.d_wshb[row:row + n, :].rearrange("(p a) c -> p a c", a=2)
                fw.dma("pool", st, src, "wp%d" % (i % NB))
                fw.dma("sp", dst, st, "ws%d" % (i % NB))
                row += n
            self.gather(self.d_wshb[o:end, :], self.d_wall[si], 512)
            self.seg_done += 1

    def seg_of(self, kind, idx):
        return self.d_wall[SEG_INDEX[(kind, idx)]]

    def prenorm(self, g0, tok0, ntiles, hT, hoff):
        sq = [self.sqb[:, 0:512], self.sqb[:, 512:1024]]
        rs = self.fa(1024, 512)
        ss = self.ps[7]
        for ti in range(ntiles):
            t0 = tok0 + ti * TT
            for c in range(KC):
                s = sq[c % 2]
                self.act(s, self.xT[:, c, t0:t0 + TT], AF.Square)
                self.mm(ss[:], self.ones_bb[:], s, c == 0, c == KC - 1)
            self.act(rs, ss[:], AF.Sqrt, scale=1.0 / D, bias=self.eps_ap)
            self.recip(rs, rs)
            for c in range(KC):
                self.stt(hT[:, c, hoff + ti * TT: hoff + (ti + 1) * TT], self.xT[:, c, t0:t0 + TT],
                         self.gains[:, g0 + c:g0 + c + 1], rs, ALU.mult, ALU.mult)

    def postnorm_residual(self, g0, t0, produce):
        mbuf = self.FA[:, 1536:1536 + 4096].rearrange("p (m t) -> p m t", m=8)
        sq = [self.sqb[:, 0:512], self.sqb[:, 512:1024]]
        rs = self.fa(1024, 512)
        tb = [self.fa(5632, 512), self.fa(6144, 512)]
        ss = self.ps[7]
        pend = None
        for m in range(KC):
            pm = self.ps[4 + (m % 3)]
            produce(m, pm)
            if pend is not None:
                pend()
            self.copy(mbuf[:, m, :], pm[:], eng="act")
            s = sq[m % 2]
            self.tt(s, mbuf[:, m, :], mbuf[:, m, :], ALU.mult)

            def _p(s=s, m=m):
                self.mm(ss[:], self.ones_bb[:], s, m == 0, m == KC - 1)
            pend = _p
        pend()
        self.act(rs, ss[:], AF.Sqrt, scale=1.0 / D, bias=self.eps_ap)
        self.recip(rs, rs)
        for m in range(KC):
            t = tb[m % 2]
            self.stt(t, mbuf[:, m, :], self.gains[:, g0 + m:g0 + m + 1], rs, ALU.mult, ALU.mult)
            self.tt(self.xT[:, m, t0:t0 + TT], self.xT[:, m, t0:t0 + TT], t, ALU.add, eng="pool")

    @property
    def eps_ap(self):
        if not hasattr(self, "_eps"):
            t = self.es.enter_context(self.nc.sbuf_tensor("eps_sb", [128, 1], F32))
            self.fw.op("dve", lambda h: h.memset(t[:], EPS), [], [t[:]])
            self._eps = t
        return self._eps[:]

    def ffn_layer(self, L):
        fw = self.fw
        self.prefetch_weights(self.step_i + 4)
        hTh = self.BA[:, 0:8192].rearrange("p (c t) -> p c t", c=8)
        hid = self.BA[:, 8192:8192 + 22528].rearrange("p (f t) -> p f t", f=FC)
        wgu = [[self.BA[:, 30720 + (2 * i + k) * 1024: 30720 + (2 * i + k + 1) * 1024].rearrange("p (k m) -> p k m", k=8)
                for k in range(2)] for i in range(3)]
        wd = [self.BA[:, 36864 + i * DFF: 36864 + (i + 1) * DFF].rearrange("p (k m) -> p k m", k=FC) for i in range(2)]
        sg = [self.fa(6656, 512), self.fa(7168, 512)]
        g_pre = self.gidx(2, L)
        g_post = self.gidx(3, L)
        seg = self.seg_of("ffn", L)
        wdi = 0
        for th in range(2):
            tok0 = th * 1024
            self.prenorm(g_pre, tok0, 2, hTh, 0)
            cnt = 0
            for f in range(FC):
                wg_t, wu_t = wgu[f % 3]
                fw.dma("pool", wg_t, self.wtile(seg, f * 131072, 1024, 8), "wg%d" % (f % 3))
                fw.dma("pool", wu_t, self.wtile(seg, (FC + f) * 131072, 1024, 8), "wu%d" % (f % 3))
                for tt in range(2):
                    pg = self.ps[2 * (cnt % 2)]
                    pu = self.ps[2 * (cnt % 2) + 1]
                    cnt += 1
                    rhs = lambda kc: hTh[:, kc, tt * TT:(tt + 1) * TT]
                    for kc in range(KC):
                        self.mm(pg[:], wg_t[:, kc, :], rhs(kc), kc == 0, kc == KC - 1)
                    for kc in range(KC):
                        self.mm(pu[:], wu_t[:, kc, :], rhs(kc), kc == 0, kc == KC - 1)
                    s = sg[cnt % 2]
                    self.act(s, pg[:], AF.Silu)
                    self.tt(hid[:, f, tt * TT:(tt + 1) * TT], s, pu[:], ALU.mult)
            for tt in range(2):
                def produce(m, pm, tt=tt):
                    nonlocal wdi
                    wt = wd[wdi % 2]
                    fw.dma("sp", wt, self.wtile(seg, 2 * FC * 131072 + m * 128 * DFF, DFF, FC), "wd%d" % (wdi % 2))
                    wdi += 1
                    for kc in range(FC):
                        self.mm(pm[:], wt[:, kc, :], hid[:, kc, tt * TT:(tt + 1) * TT], kc == 0, kc == FC - 1)
                self.postnorm_residual(g_post, tok0 + tt * TT, produce)

    def attn_layer(self, j, L):
        fw = self.fw
        BA = self.BA
        seg = self.seg_of("attn", j)
        hT = BA[:, 0:16384].rearrange("p (c t) -> p c t", c=8)
        wqk = [BA[:, 16384 + i * 1024:16384 + (i + 1) * 1024].rearrange("p (k m) -> p k m", k=8) for i in range(3)]
        wv = [BA[:, 19456 + i * 4096:19456 + (i + 1) * 4096].rearrange("p (k m) -> p k m", k=8) for i in range(2)]
        ost = [BA[:, 27648 + i * 2048:27648 + (i + 1) * 2048] for i in range(2)]
        vst = BA[:, 31744:31744 + 8192].rearrange("p (t f) -> p t f", t=16)
        self.prenorm(self.gidx(0, L), 0, 4, hT, 0)
        def cc_chunk(hp, kind):
            a = (hp * 6 + kind) * 128
            self.gather(self.d_qkv_s[a:a + 128, :], self.d_qkv_r[4 * a:4 * a + 512, :], 128)

        vst2 = BA[:, 31744:31744 + 8192].rearrange("p (g t f) -> p g t f", g=4, t=16)
        for vb in range(2):
            wt = wv[vb]
            fw.dma("pool", wt, self.wtile(seg, 16 * 131072 + vb * 524288, 4096, 8), "wv%d" % vb)
            for tk in range(16):
                pv = self.ps[4 + tk % 3]
                for kc in range(KC):
                    self.mm(pv[:], hT[:, kc, tk * 128:(tk + 1) * 128], wt[:, kc, :], kc == 0, kc == KC - 1)
                self.evac(vst2[:, :, tk, :], pv[:].rearrange("p (g f) -> p g f", g=4))
            for u in range(2):
                hp = 2 * vb + u
                for which in range(2):
                    r0 = hp * 768 + 512 + which * 128
                    fw.dma("sp", self.d_qkv_s[r0:r0 + 128, :].rearrange("p (t f) -> p t f", t=16),
                           vst2[:, 2 * u + which, :, :], "vs%d" % which)
        cnt = 0
        pending = [(hp, 4) for hp in range(4)]
        order = [(kind, hp) for kind in range(4) for hp in range(4)]
        for i, (kind, hp) in enumerate(order):
            mt = hp * 4 + kind
            wt = wqk[i % 3]
            fw.dma("pool", wt, self.wtile(seg, mt * 131072, 1024, 8), "wqk%d" % (i % 3))
            o = ost[i % 2]
            for tt in range(4):
                pz = self.ps[cnt % 4]
                cnt += 1
                for kc in range(KC):
                    self.mm(pz[:], wt[:, kc, :], hT[:, kc, tt * TT:(tt + 1) * TT], kc == 0, kc == KC - 1)
                self.evac(o[:, tt * TT:(tt + 1) * TT], pz[:])
            r0 = hp * 768 + kind * 128
            fw.dma("sp", self.d_qkv_s[r0:r0 + 128, :], o, "qs%d" % (i % 2))
            while pending:
                cc_chunk(*pending.pop(0))
            pending.append((hp, kind))
        while pending:
            cc_chunk(*pending.pop(0))
        for hp in range(4):
            cc_chunk(hp, 5)
        self.prefetch_weights(self.step_i + 3)
        if ATT_STOP == 1:
            fw.wait_all("sp", [self.d_qkv_r])
            return

        Q = BA[:, 0:8192]
        K = BA[:, 8192:16384]
        V = BA[:, 16384:24576].rearrange("p (t f) -> p t f", t=64)
        EB = BA[:, 24576:32768].rearrange("p (e q) -> p e q", e=16)
        pT = [BA[:, 32768 + i * 512:32768 + (i + 1) * 512] for i in range(3)]
        ob = [BA[:, 34304 + i * 512:34304 + (i + 1) * 512] for i in range(2)]
        spb = [BA[:, 35328 + i * 512:35328 + (i + 1) * 512] for i in range(3)]
        wb = [BA[:, 36864 + i * 512:36864 + (i + 1) * 512] for i in range(3)]
        ef = [self.fa(i * 512, 512) for i in range(3)]
        ecf = [self.fa(1536 + i * 512, 512) for i in range(3)]
        stg = [self.fa(3072 + i * 512, 512) for i in range(2)]
        rden = self.fa(4096, 512)

        def load_qkv(which):
            self.fetch_rows(self.d_qkv_r, Q.rearrange("p (r t) -> p r t", r=4), 6, 2 * which, "lq0")
            self.fetch_rows(self.d_qkv_r, K.rearrange("p (r t) -> p r t", r=4), 6, 2 * which + 1, "lq1")
            self.fetch_rows(self.d_qkv_r, BA[:, 16384:24576].rearrange("p (r t) -> p r t", r=4), 6, 4 + which, "lq2")

        load_qkv(0)
        for u in range(2):
            for jj in range(8):
                s = stg[jj % 2]
                s2 = ef[jj % 2]
                src = bass.AP(tensor=self.d_bext.tensor, offset=(j * 2 + u) * 1536 + 896 - 128 * jj,
                              ap=[[1, 128], [1, 512]])
                fw.dma("sp", s, src, "tb%d" % (jj % 2))
                pz = self.ps[jj % 2]
                self.mm(pz[:], self.jmat[:], s, True, True)
                self.act(s2, pz[:], AF.Exp)
                self.tt(EB[:, u * 8 + jj, :], s2, self.amask[:, jj, :], ALU.mult)
        for u in range(2):
            pr = slice(64 * u, 64 * u + 64)
            blocks = []
            for qi in range(16):
                jts = [(jj, 4 * qi - 4 + jj) for jj in range(8) if 4 * qi - 4 + jj >= 0]
                for n, (jj, jt) in enumerate(jts):
                    blocks.append((qi, jj, jt, n == 0, n == len(jts) - 1))

            def st1(i):
                qi, jj, jt, first, last = blocks[i]
                pz = self.ps[i % 3]
                self.mm(pz[:], K[pr, jt * 128:(jt + 1) * 128], Q[pr, qi * TT:(qi + 1) * TT], True, True)
                self.act(ef[i % 3], pz[:], AF.Exp, scale=0.125)

            def st2(i):
                qi, jj, jt, first, last = blocks[i]
                po = self.ps[3 + (qi % 2)]
                pden = self.ps[5 + (qi % 2)]
                p = pT[i % 3]
                self.tt(p, ef[i % 3], EB[:, u * 8 + jj, :], ALU.mult)
                self.mm(po[0:64, :], V[:, jt, 64 * u:64 * u + 64], p, first, last)
                self.mm(pden[0:64, :], self.ones_b[:, 0:64], p, first, last)
                if last:
                    self.recip(rden[0:64, :], pden[0:64, :])
                    o = ob[qi % 2]
                    self.tt(o[0:64, :], po[0:64, :], rden[0:64, :], ALU.mult)
                    ar0 = (qi // 4) * 256 + 64 * u
                    fw.dma("sp", self.d_att_s[ar0:ar0 + 64, (qi % 4) * TT:(qi % 4 + 1) * TT], o[0:64, :],
                           "ao%d" % (qi % 2))

            st1(0)
            for i in range(len(blocks)):
                if i + 1 < len(blocks):
                    st1(i + 1)
                st2(i)
        for r_ in range(4):
            a = r_ * 256
            self.gather(self.d_att_s[a:a + 128, :], self.d_att_r[4 * a:4 * a + 512, :], 128)
        if ATT_STOP == 2:
            fw.wait_all("sp", [self.d_att_s])
            return
        load_qkv(1)
        NBUF = 4
        ef = [self.fa(i * 512, 512) for i in range(NBUF)]
        ecf = [self.fa(2048 + i * 512, 512) for i in range(NBUF)]
        spb = [BA[:, 35328 + i * 512:35328 + (i + 1) * 512] for i in range(NBUF)]
        wb = [BA[:, 37376 + i * 512:37376 + (i + 1) * 512] for i in range(NBUF)]
        ob2 = [BA[:, 39424 + i * 512:39424 + (i + 1) * 512] for i in range(2)]
        cnt = 0
        for qi in range(16):
            jts = list(range(4 * qi + 3, -1, -1))
            nb = len(jts)
            po = self.ps[6 + (qi % 2)]
            Ts = [self.ps[4], self.ps[5]]
            slots = {}

            def stageA(n, u):
                nonlocal cnt
                jt = jts[n]
                pr = slice(64 * u, 64 * u + 64)
                b = cnt % NBUF
                cnt += 1
                pz = self.ps[b]
                slots[(n, u)] = b
                self.mm(pz[:], K[pr, jt * 128:(jt + 1) * 128], Q[pr, qi * TT:(qi + 1) * TT], True, True)
                self.act(ef[b], pz[:], AF.Exp, scale=0.125)
                dj = jt - 4 * qi
                if dj >= 0:
                    self.tt(ef[b], ef[b], self.bmask[:, dj, :], ALU.mult)
                self.act(spb[b], ef[b], AF.Ln, bias=1.0)

            def stageB1(n, u):
                b = slots[(n, u)]
                self.mm(Ts[u][:], self.tri_b, spb[b], n == 0, True, skip=(n != 0))
                self.act(ecf[b], Ts[u][:], AF.Exp, scale=-1.0)

            def stageB2(n, u):
                b = slots[(n, u)]
                jt = jts[n]
                self.mm(Ts[u][:], self.sl_b, spb[b], False, True, skip=True)
                self.tt(wb[b], ef[b], ecf[b], ALU.mult)
                self.mm(po[64 * u:64 * u + 64, :], V[:, jt, 64 * u:64 * u + 64], wb[b], n == 0, n == nb - 1)

            for u in range(2):
                stageA(0, u)
            for n in range(nb):
                for u in range(2):
                    stageB1(n, u)
                if n + 1 < nb:
                    for u in range(2):
                        stageA(n + 1, u)
                for u in range(2):
                    stageB2(n, u)
            o = ob2[qi % 2]
            self.evac(o[:], po[:])
            ar0 = (qi // 4) * 256 + 128
            fw.dma("sp", self.d_att_s[ar0:ar0 + 128, (qi % 4) * TT:(qi % 4 + 1) * TT], o[:], "ao%d" % (qi % 2))
            if qi % 4 == 3:
                self.gather(self.d_att_s[ar0:ar0 + 128, :], self.d_att_r[4 * ar0:4 * ar0 + 512, :], 128)
        self.prefetch_weights(self.step_i + 4)
        if ATT_STOP == 3:
            fw.wait_all("sp", [self.d_att_r])
            return

        aT = BA[:, 0:16384].rearrange("p (c t) -> p c t", c=8)
        wo = [BA[:, 16384 + m * 1024:16384 + (m + 1) * 1024].rearrange("p (k m) -> p k m", k=8) for m in range(8)]
        self.fetch_cols(self.d_att_r, aT)
        for m in range(8):
            fw.dma("pool", wo[m], self.wtile(seg, (16 + 8 + m) * 131072, 1024, 8), "wo%d" % (m % 4))
        g_post = self.gidx(1, L)
        for tt in range(4):
            def produce(m, pm, tt=tt):
                for kc in range(KC):
                    self.mm(pm[:], wo[m][:, kc, :], aT[:, kc, tt * TT:(tt + 1) * TT], kc == 0, kc == KC - 1)
            self.postnorm_residual(g_post, tt * TT, produce)

    def rec_layer(self, j, L):
        fw = self.fw
        BA = self.BA
        seg = self.seg_of("rec", j)
        hT = BA[:, 0:16384].rearrange("p (c t) -> p c t", c=8)
        win = [BA[:, 16384 + i * 1024:16384 + (i + 1) * 1024].rearrange("p (k m) -> p k m", k=8) for i in range(3)]
        ost = [BA[:, 27648 + i * 2048:27648 + (i + 1) * 2048] for i in range(2)]
        self.prenorm(self.gidx(0, L), 0, 4, hT, 0)
        cnt = 0
        for mt in range(16):
            n, kind, half = mt // 4, (mt // 2) % 2, mt % 2
            wt = win[mt % 3]
            fw.dma("pool", wt, self.wtile(seg, mt * 131072, 1024, 8), "wqk%d" % (mt % 3))
            o = ost[mt % 2]
            for tt in range(4):
                pz = self.ps[cnt % 4]
                cnt += 1
                for kc in range(KC):
                    self.mm(pz[:], wt[:, kc, :], hT[:, kc, tt * TT:(tt + 1) * TT], kc == 0, kc == KC - 1)
                if kind == 0:
                    self.act(o[:, tt * TT:(tt + 1) * TT], pz[:], AF.Gelu_apprx_tanh)
                else:
                    self.copy(o[:, tt * TT:(tt + 1) * TT], pz[:], eng="dve")
            r0 = n * 512 + kind * 256 + half * 128
            fw.dma("sp", self.d_rec_s[r0:r0 + 128, :], o, "qs%d" % (mt % 2))
            if mt >= 1:
                a = rprev
                self.gather(self.d_rec_s[a:a + 128, :], self.d_rec_r[4 * a:4 * a + 512, :], 128)
            rprev = r0
        self.gather(self.d_rec_s[rprev:rprev + 128, :], self.d_rec_r[4 * rprev:4 * rprev + 512, :], 128)
        self.prefetch_weights(self.step_i + 3)

        G = BA[:, 0:16384].rearrange("p (c t) -> p c t", c=2)
        XR = BA[:, 16384:16384 + 16400].rearrange("p (c t) -> p c t", c=2)
        PAD = 8
        xcb = [BA[:, 32784 + i * 1024:32784 + (i + 1) * 1024].rearrange("p (c t) -> p c t", c=2) for i in range(2)]
        wa = BA[:, 34832:34832 + 512].rearrange("p (k m) -> p k m", k=2)
        wi = BA[:, 35344:35344 + 512].rearrange("p (k m) -> p k m", k=2)
        ob = [BA[:, 35856 + i * 512:35856 + (i + 1) * 512] for i in range(4)]
        xc = [self.FA[:, i * 1024:(i + 1) * 1024].rearrange("p (c t) -> p c t", c=2) for i in range(2)]
        rb = [self.fa(2048 + i * 512, 512) for i in range(2)]
        ib = [self.fa(3072 + i * 512, 512) for i in range(2)]
        a2b = [self.fa(4096 + i * 512, 512) for i in range(2)]
        hsb = [[self.fa(5120 + (2 * hf + i) * 512, 512) for i in range(2)] for hf in range(2)]
        sm = self.fa(7168, 16)
        cv = self.fa(7184, 4)
        fw.dma("sp", sm, self.d_rsm[j], "c0")
        fw.dma("pool", wa, self.d_rwa[j].rearrange("p (k m) -> p k m", k=2), "c1")
        fw.dma("pool", wi, self.d_rwi[j].rearrange("p (k m) -> p k m", k=2), "c2")
        self.act(cv[:, 0:2], sm[:, 14:16], AF.Exp, scale=-1.0)
        self.act(cv[:, 0:2], cv[:, 0:2], AF.Ln, bias=1.0)
        self.ts(cv[:, 2:4], cv[:, 0:2], -16.0, None, ALU.mult)
        self.ts(cv[:, 0:2], cv[:, 0:2], -8.0, None, ALU.mult)
        for hf in range(2):
            fw.op("dve", lambda h, hf=hf: h.memset(XR[:, hf, 0:PAD], 0.0), [], [XR[:, hf, 0:PAD]])
        for kind in range(2):
            for hf in range(2):
                dstt = (G[:, hf, :] if kind == 0 else XR[:, hf, PAD:PAD + S]).rearrange("p (r t) -> p r t", r=4)
                self.fetch_rows(self.d_rec_r, dstt, 4, kind * 2 + hf, "lq%d" % (2 * kind + hf))
        for ti in range(16):
            t0 = ti * TT
            x_c = xc[ti % 2]
            x_b = xcb[ti % 2]
            for hf in range(2):
                base = PAD + t0 - 3
                self.ts(x_c[:, hf, :], XR[:, hf, base:base + TT], sm[:, 4 * hf:4 * hf + 1], sm[:, 8 + hf:9 + hf],
                        ALU.mult, ALU.add)
                for jw in range(1, 4):
                    self.stt(x_c[:, hf, :], XR[:, hf, base + jw:base + jw + TT], sm[:, 4 * hf + jw:4 * hf + jw + 1],
                             x_c[:, hf, :], ALU.mult, ALU.add)
                self.copy(x_b[:, hf, :], x_c[:, hf, :], eng="pool")
            for jt in range(2):
                prr = self.ps[2 * jt]
                pii = self.ps[2 * jt + 1]
                for ic in range(2):
                    self.mm(prr[:], wa[:, ic, jt * 128:(jt + 1) * 128], x_b[:, ic, :], ic == 0, ic == 1)
                for ic in range(2):
                    self.mm(pii[:], wi[:, ic, jt * 128:(jt + 1) * 128], x_b[:, ic, :], ic == 0, ic == 1)
                r_ = rb[jt]
                i_ = ib[jt]
                a2 = a2b[jt]
                hs = hsb[jt][ti % 2]
                hprev = hsb[jt][(ti + 1) % 2]
                self.act(r_, prr[:], AF.Sigmoid, bias=sm[:, 10 + jt:11 + jt])
                self.act(i_, pii[:], AF.Sigmoid, bias=sm[:, 12 + jt:13 + jt])
                self.act(a2, r_, AF.Exp, scale=cv[:, 2 + jt:3 + jt])
                self.act(r_, r_, AF.Exp, scale=cv[:, jt:jt + 1])
                self.act(a2, a2, AF.Sqrt, scale=-1.0, bias=1.0)
                self.tt(i_, i_, x_c[:, jt, :], ALU.mult)
                self.tt(i_, i_, a2, ALU.mult)
                init = 0.0 if ti == 0 else hprev[:, TT - 1:TT]
                rd = [r_, i_] + ([] if ti == 0 else [init])
                fw.op("dve", lambda h, hs=hs, r_=r_, i_=i_, init=init: h.tensor_tensor_scan(
                    hs, r_, i_, init, ALU.mult, ALU.add), rd, [hs])
                o = ob[(2 * ti + jt) % 4]
                self.tt(o, hs, G[:, jt, t0:t0 + TT], ALU.mult, eng="pool")
                rr0 = (ti // 4) * 256 + jt * 128
                fw.dma("sp", self.d_rec2_s[rr0:rr0 + 128, (ti % 4) * TT:(ti % 4 + 1) * TT], o, "ao%d" % (jt))
            if ti % 4 == 3:
                g0 = (ti // 4) * 256
                self.gather(self.d_rec2_s[g0:g0 + 256, :], self.d_rec2_r[4 * g0:4 * g0 + 1024, :], 128)
        self.prefetch_weights(self.step_i + 4)

        aT = BA[:, 0:16384].rearrange("p (c t) -> p c t", c=8)
        wo = [BA[:, 16384 + m * 1024:16384 + (m + 1) * 1024].rearrange("p (k m) -> p k m", k=8) for m in range(8)]
        self.fetch_cols(self.d_rec2_r, aT)
        for m in range(8):
            fw.dma("pool", wo[m], self.wtile(seg, (16 + m) * 131072, 1024, 8), "wo%d" % (m % 4))
        g_post = self.gidx(1, L)
        for tt in range(4):
            def produce(m, pm, tt=tt):
                for kc in range(KC):
                    self.mm(pm[:], wo[m][:, kc, :], aT[:, kc, tt * TT:(tt + 1) * TT], kc == 0, kc == KC - 1)
            self.postnorm_residual(g_post, tt * TT, produce)


def _tile_w(w, cols):
    K = w.shape[0]
    kc = K // 128
    out = []
    for c in cols:
        blk = w[:, c]
        m = blk.shape[1]
        out.append(blk.reshape(kc, 128, m).transpose(1, 0, 2).reshape(128, kc * m))
    return np.ascontiguousarray(np.stack(out, 0))


def _consts():
    k = np.arange(128)[:, None]
    q = np.arange(512)[None, :]
    cst = np.zeros((128, 384), np.float32)
    cst[:, 0:128] = 1.0
    kk = np.arange(128)[:, None]
    kp = np.arange(128)[None, :]
    cst[:, 128:256] = (kk >= kp).astype(np.float32)
    cst[:, 256:384] = (kk < kp).astype(np.float32)
    am = np.zeros((128, 8, 512), np.float32)
    for jj in range(8):
        kc = 2 * jj + k // 64
        qc = q // 64
        am[:, jj, :] = ((kc >= qc) & (kc <= qc + 8)).astype(np.float32)
    bm = np.zeros((128, 4, 512), np.float32)
    for dj in range(4):
        bm[:, dj, :] = ((k + 128 * dj) < q).astype(np.float32)
    return cst, am.reshape(128, 4096), bm.reshape(128, 2048)


def _toeplitz_idx():
    k = np.arange(128)[:, None]
    q = np.arange(512)[None, :]
    idx = np.zeros((8, 128, 512), np.int64)
    for jj in range(8):
        idx[jj] = np.clip(512 - 128 * jj + q - k, -256, 256) + 256
    return idx


def prepare_inputs(inp, plan=None):
    plan = FULL_PLAN if plan is None else plan
    SEG_KEYS, SEG_ROWS, WSH_ROWS, WSH_SPLIT = seg_layout(plan)
    f = lambda a: np.asarray(a, dtype=np.float32)
    x = f(inp["x"])
    shared = {}
    G = np.stack([f(inp["norm_mix_pre"]), f(inp["norm_mix_post"]), f(inp["norm_ffn_pre"]), f(inp["norm_ffn_post"])], 0)
    shared["gains"] = np.ascontiguousarray(G.reshape(4, 4, 8, 128).transpose(3, 0, 1, 2).reshape(128, 128))
    shared["jmat"] = np.ascontiguousarray(np.eye(128, dtype=np.float32)[::-1])
    cst, am, bm = _consts()
    ar = np.arange(128)
    segs = {}
    segs[("const", 0)] = np.concatenate([cst.ravel(), am.ravel(), bm.ravel()])
    w_in = f(inp["attn_w_in"])
    w_out = f(inp["attn_w_out"])
    kind_base = [0, 512, 1536, 2048]
    cols = [kind_base[kind] + 128 * hp + ar for hp in range(4) for kind in range(4)]
    vcols = []
    for vb in range(2):
        vcols.append(np.concatenate([1024 + 128 * (2 * vb) + ar, 2560 + 128 * (2 * vb) + ar,
                                     1024 + 128 * (2 * vb + 1) + ar, 2560 + 128 * (2 * vb + 1) + ar]))
    rows = np.concatenate([np.concatenate([128 * hp + ar, 512 + 128 * hp + ar]) for hp in range(4)])
    mcols = [128 * m + ar for m in range(8)]
    for j in range(2):
        if ("attn", j) not in SEG_KEYS:
            continue
        segs[("attn", j)] = np.concatenate([_tile_w(w_in[j], cols).ravel(), _tile_w(w_in[j], vcols).ravel(),
                                            _tile_w(w_out[j][rows, :], mcols).ravel()])
    rw_in = f(inp["rg_w_in"])
    rw_out = f(inp["rg_w_out"])
    rcols = [kind * 1024 + n * 256 + half * 128 + ar for n in range(4) for kind in range(2) for half in range(2)]
    for j in range(2):
        if ("rec", j) not in SEG_KEYS:
            continue
        segs[("rec", j)] = np.concatenate([_tile_w(rw_in[j], rcols).ravel(), _tile_w(rw_out[j], mcols).ravel()])
    wg, wu, wd = f(inp["ffn_w_gate"]), f(inp["ffn_w_up"]), f(inp["ffn_w_down"])
    fcols = [128 * t + ar for t in range(FC)]
    for l in range(4):
        if ("ffn", l) not in SEG_KEYS:
            continue
        segs[("ffn", l)] = np.concatenate([_tile_w(wg[l], fcols).ravel(), _tile_w(wu[l], fcols).ravel(),
                                           _tile_w(wd[l], mcols).ravel()])
    shards = [[] for _ in range(8)]
    for si, key in enumerate(SEG_KEYS):
        a = segs[key]
        assert a.size == SEG_ROWS[si] * 1024, (key, a.size)
        a = a.reshape(SEG_ROWS[si], 1024)
        rs = SEG_ROWS[si] // 4
        parts = [[] for _ in range(4)]
        k = 0
        while k < rs:
            cch = min(512, rs - k)
            for r in range(4):
                parts[r].append(a[4 * k + r * cch:4 * k + (r + 1) * cch])
            k += cch
        for c in range(8):
            shards[c].append(np.concatenate(parts[c % 4], 0).ravel())
    rel_bias = f(inp["attn_rel_bias"])
    eidx = np.clip(np.arange(1536) - 511, -256, 256) + 256
    w_a, w_i = f(inp["rg_w_a"]), f(inp["rg_w_i"])
    conv_w, conv_b = f(inp["rg_conv_w"]), f(inp["rg_conv_b"])
    b_a, b_i, lam = f(inp["rg_b_a"]), f(inp["rg_b_i"]), f(inp["rg_lambda"])
    maps = []
    for c in range(8):
        b, r = c // 4, c % 4
        m = dict(shared)
        m["xT"] = np.ascontiguousarray(x[b, r * NT:(r + 1) * NT, :].T)
        wfull = np.concatenate(shards[c]).reshape(WSH_ROWS, 1024)
        m["wsh0"] = np.ascontiguousarray(wfull[:WSH_SPLIT])
        m["wsh1"] = np.ascontiguousarray(wfull[WSH_SPLIT:])
        m["biasext"] = np.ascontiguousarray(
            np.stack([np.stack([rel_bias[j, 2 * r + u][eidx] for u in range(2)], 0) for j in range(2)], 0))
        n = r
        m["r_wa"] = np.stack([_tile_w(w_a[j, n], [np.arange(256)])[0] for j in range(2)], 0)
        m["r_wi"] = np.stack([_tile_w(w_i[j, n], [np.arange(256)])[0] for j in range(2)], 0)
        sm = np.zeros((2, 128, 16), np.float32)
        for j in range(2):
            for hf in range(2):
                ch = n * 256 + hf * 128 + ar
                for jw in range(4):
                    sm[j, :, 4 * hf + jw] = conv_w[j, jw, 0, ch]
                sm[j, :, 8 + hf] = conv_b[j, ch]
                sm[j, :, 10 + hf] = b_a[j].reshape(-1)[ch]
                sm[j, :, 12 + hf] = b_i[j].reshape(-1)[ch]
                sm[j, :, 14 + hf] = lam[j, ch]
        m["r_small"] = sm
        maps.append(m)
    return maps


import os as _os
ATT_STOP = int(_os.environ.get("ATT_STOP", "0"))
FULL_PLAN = [("attn", 0, 0), ("ffn", 0), ("rec", 0, 1), ("ffn", 1), ("attn", 1, 2), ("ffn", 2), ("rec", 1, 3), ("ffn", 3)]
_PROG_CACHE = {}


def run_plan(plan, maps):
    key = tuple(plan)
    if key not in _PROG_CACHE:
        _PROG_CACHE[key] = Prog(plan)
    prog = _PROG_CACHE[key]
    res = run_bass_kernel_spmd(prog.nc, maps, core_ids=list(range(8)))
    return [r["yT"] for r in res.results]


def assemble(ys):
    out = np.zeros((2, S, D), np.float32)
    for c in range(8):
        b, r = c // 4, c % 4
        out[b, r * NT:(r + 1) * NT, :] = ys[c].T
    return out


def kernel(**inputs):
    maps = prepare_inputs(inputs)
    ys = run_plan(FULL_PLAN, maps)
    return assemble(ys)
```

```python
import numpy as np
from contextlib import ExitStack
import concourse.bass as bass
import concourse.mybir as mybir
from concourse.bass_utils import run_bass_kernel_spmd

F32 = mybir.dt.float32
BF16 = mybir.dt.bfloat16
AF = mybir.ActivationFunctionType
ALU = mybir.AluOpType

ENGS = ("pe", "act", "dve", "pool", "sp")


class Tok:
    __slots__ = ("sem", "val", "eng")

    def __init__(self, sem, val, eng=None):
        self.sem = sem
        self.val = val
        self.eng = eng


class Chan:
    def __init__(self, sem):
        self.sem = sem
        self.total = 0


def _bbox(ap):
    t = ap.tensor
    shape = list(t.shape)
    pat = [(int(s), int(n)) for s, n in ap.ap]
    off = int(ap.offset)
    space = str(ap.space)
    if "dram" in space.lower():
        ext = sum((n - 1) * abs(s) for s, n in pat)
        return (0, 1, off, off + ext + 1)
    row = 1
    for d in shape[1:]:
        row *= int(d)
    p0 = off // row
    f0 = off % row
    s0, n0 = pat[0]
    if s0 != 0 and s0 % row == 0:
        pstep = s0 // row
        p1 = p0 + (n0 - 1) * pstep + 1
        rest = pat[1:]
    elif s0 == 0:
        p1 = p0 + 1
        rest = pat[1:]
    else:
        p1 = p0 + 1
        rest = pat
    ext = sum((n - 1) * abs(s) for s, n in rest)
    return (p0, p1, f0, f0 + ext + 1)


class FW:
    def __init__(self, nc, sync_same_engine=True):
        self.nc = nc
        self.ops = {e: [] for e in ENGS}
        self.sem = {}
        self.cnt = {e: 0 for e in ENGS}
        self.waited = {e: {} for e in ENGS}
        self.acc = {}
        self.sync_same = sync_same_engine
        self._stack = []
        self.n_instr = 0
        self.n_wait = 0
        for e in ENGS:
            self.sem[e] = self._sem("c_" + e)
        self.chans = {}

    def _sem(self, name):
        cm = self.nc.semaphore(name)
        s = cm.__enter__()
        self._stack.append(cm)
        return s

    def chan(self, name):
        if name not in self.chans:
            self.chans[name] = Chan(self._sem("d_" + name))
        return self.chans[name]

    def _deps(self, reads, writes):
        deps = []
        for ap, is_w in [(a, False) for a in reads] + [(a, True) for a in writes]:
            name = ap.tensor.name
            bb = _bbox(ap)
            for rec in self.acc.get(name, ()):
                rb, tok, rw = rec
                if not (is_w or rw):
                    continue
                if rb[0] < bb[1] and bb[0] < rb[1] and rb[2] < bb[3] and bb[2] < rb[3]:
                    deps.append((tok, rw, is_w))
        return deps

    def _record(self, reads, writes, tok):
        for ap in writes:
            name = ap.tensor.name
            bb = _bbox(ap)
            lst = self.acc.setdefault(name, [])
            lst[:] = [r for r in lst if not (bb[0] <= r[0][0] and r[0][1] <= bb[1]
                                             and bb[2] <= r[0][2] and r[0][3] <= bb[3])]
            lst.append([bb, tok, True])
        for ap in reads:
            name = ap.tensor.name
            bb = _bbox(ap)
            lst = self.acc.setdefault(name, [])
            lst[:] = [r for r in lst if not ((not r[2]) and r[1].sem is tok.sem
                                             and bb[0] <= r[0][0] and r[0][1] <= bb[1]
                                             and bb[2] <= r[0][2] and r[0][3] <= bb[3])]
            lst.append([bb, tok, False])

    def _emit_waits(self, e, deps, is_dma=False):
        need = {}
        for tok, rw, is_w in deps:
            if tok.eng == e and not is_dma:
                if e == "pe":
                    continue
                if not self.sync_same:
                    continue
                if not rw:
                    continue
            k = id(tok.sem)
            if k not in need or need[k][1] < tok.val:
                need[k] = (tok.sem, tok.val)
        for k, (sem, val) in need.items():
            if self.waited[e].get(k, 0) >= val:
                continue
            self.waited[e][k] = val
            self.n_wait += 1
            self.ops[e].append(("wait", sem, val))

    def op(self, e, fn, reads, writes):
        deps = self._deps(reads, writes)
        self._emit_waits(e, deps)
        self.cnt[e] += 1
        tok = Tok(self.sem[e], self.cnt[e], e)
        self.ops[e].append(("op", fn, self.sem[e]))
        self._record(reads, writes, tok)
        self.n_instr += 1
        return tok

    def dma(self, q, out, in_, chan, rd=None, wr=None, **kw):
        ch = self.chan(chan)
        r = (list(rd) if isinstance(rd, (list, tuple)) else [rd]) if rd is not None else [in_]
        w = [wr if wr is not None else out]
        deps = self._deps(r, w)
        if ch.total > 0:
            deps.append((Tok(ch.sem, ch.total, None), True, True))
        self._emit_waits(q, deps, is_dma=True)
        ch.total += 16
        tok = Tok(ch.sem, ch.total, None)
        self.ops[q].append(("dma", out, in_, ch.sem, kw))
        self._record(r, w, tok)
        self.n_instr += 1
        return tok

    def collective(self, kind, ins, outs, groups, chan):
        ch = self.chan(chan)
        deps = self._deps(ins, outs)
        self._emit_waits("pool", deps, is_dma=True)
        ch.total += 1
        tok = Tok(ch.sem, ch.total, None)
        self.ops["pool"].append(("cc", kind, ins, outs, groups, ch.sem))
        self._record(ins, outs, tok)
        return tok

    def wait_all(self, e, aps):
        deps = self._deps([], aps)
        self._emit_waits(e, deps, is_dma=True)

    def _replay(self, e, h):
        for item in self.ops[e]:
            k = item[0]
            if k == "wait":
                h.wait_ge(item[1], item[2])
            elif k == "op":
                item[1](h).then_inc(item[2], 1)
            elif k == "dma":
                o = item[1](h) if callable(item[1]) else item[1]
                i = item[2](h) if callable(item[2]) else item[2]
                h.dma_start(out=o, in_=i, **item[4]).then_inc(item[3], 16)
            elif k == "cc":
                _, kind, ins, outs, groups, sem = item
                h.collective_compute(kind, ALU.bypass, replica_groups=groups,
                                     ins=[a.opt() for a in ins],
                                     outs=[a.opt() for a in outs]).then_inc(sem, 1)

    def finish(self):
        nc = self.nc
        with nc.Block() as block:
            @block.tensor
            def _(h):
                self._replay("pe", h)

            @block.scalar
            def _(h):
                self._replay("act", h)

            @block.vector
            def _(h):
                self._replay("dve", h)

            @block.gpsimd
            def _(h):
                self._replay("pool", h)

            @block.sync
            def _(h):
                self._replay("sp", h)
        for cm in reversed(self._stack):
            cm.__exit__(None, None, None)
        self._stack = []


D = 1024
KC = 8
S = 8192
NT = 2048
TT = 512
DFF = 2816
FC = 22
DEPTH = 4
EPS = 1e-6
GROUPS = [[0, 1, 2, 3], [4, 5, 6, 7]]

_SEG_R = {"const": 816, "attn": 4096, "ffn": 8448, "rec": 3072}


def seg_layout(plan):
    keys = [("const", 0)]
    for st in plan:
        keys.append((st[0], st[1]))
    rows = [_SEG_R[k] for k, _ in keys]
    tot = sum(rows) // 4
    acc, split = 0, tot
    for i, r in enumerate(rows):
        if acc + r // 4 > 6400:
            split = acc
            break
        acc += r // 4
    if split == tot:
        split = rows[0] // 4
    return keys, rows, tot, split

BA_SIZE = 43008
FA_SIZE = 8192


class Prog:
    def __init__(self, plan, load_x=True, store_x=True):
        self.plan = plan
        global SEG_KEYS, SEG_ROWS, SEG_INDEX, WSH_ROWS, WSH_SPLIT
        SEG_KEYS, SEG_ROWS, WSH_ROWS, WSH_SPLIT = seg_layout(plan)
        SEG_INDEX = {k: i for i, k in enumerate(SEG_KEYS)}
        nc = self.nc = bass.Bass("TRN2", target_bir_lowering=False)
        self.fw = FW(nc)
        self.es = ExitStack()
        dt = nc.dram_tensor

        def ext(name, shape, dtype=F32):
            return dt(name, shape, dtype, kind="ExternalInput").ap()

        self.d_xT = ext("xT", [D, NT])
        self.d_gains = ext("gains", [128, 128])
        self.d_wsh = [ext("wsh0", [WSH_SPLIT, 1024]), ext("wsh1", [WSH_ROWS - WSH_SPLIT, 1024])]
        self.d_bext = ext("biasext", [2, 2, 1536])
        self.d_jmat = ext("jmat", [128, 128])
        self.d_rwa = ext("r_wa", [2, 128, 512])
        self.d_rwi = ext("r_wi", [2, 128, 512])
        self.d_rsm = ext("r_small", [2, 128, 16])
        self.d_wshb = dt("wsh_bf", [WSH_ROWS, 1024], BF16).ap()
        self.d_wall = [dt("wall%d" % i, [SEG_ROWS[i], 1024], BF16).ap() for i in range(len(SEG_ROWS))]
        self.d_yT = dt("yT", [D, NT], F32, kind="ExternalOutput").ap()
        self.d_qkv_s = dt("qkv_s", [3072, 2048], BF16).ap()
        self.d_qkv_r = dt("qkv_r", [4 * 3072, 2048], BF16).ap()
        self.d_myqkv = dt("myqkv", [3072, 2048], BF16).ap()
        self.d_myrec = dt("myrec", [2048, 2048], BF16).ap()
        self.d_att_s = dt("att_s", [1024, 2048], BF16).ap()
        self.d_att_r = dt("att_r", [4096, 2048], BF16).ap()
        self.d_rec_s = dt("rec_s", [2048, 2048], BF16).ap()
        self.d_rec_r = dt("rec_r", [4 * 2048, 2048], BF16).ap()
        self.d_rec2_s = dt("rec2_s", [1024, 2048], BF16).ap()
        self.d_rec2_r = dt("rec2_r", [4096, 2048], BF16).ap()

        sb = lambda name, shape, dtype: self.es.enter_context(nc.sbuf_tensor(name, shape, dtype))
        self.xT = sb("xT_sb", [128, KC, NT], F32)
        self.gains = sb("gains_sb", [128, 128], F32)
        self.ones_f = sb("ones_f", [128, 128], F32)
        self.cst = sb("cst_sb", [128, 384], BF16)
        self.amask = sb("amask_sb", [128, 8, 512], BF16)
        self.bmask = sb("bmask_sb", [128, 4, 512], BF16)
        self.BA = sb("BA", [128, BA_SIZE], BF16)
        self.FA = sb("FA", [128, FA_SIZE], F32)
        self.ps = [self.es.enter_context(nc.psum_tensor("ps%d" % i, [128, 512], F32)) for i in range(8)]
        self.ones_b = self.cst[:, 0:128]
        self.tri_b = self.cst[:, 128:256]
        self.sl_b = self.cst[:, 256:384]
        self.evac_rr = 0

        fw = self.fw
        fw.dma("sp", self.gains[:], self.d_gains, "c0")
        self.jmat = sb("jmat_sb", [128, 128], F32)
        fw.dma("sp", self.jmat[:], self.d_jmat, "c4")
        self.wstage = sb("wstage", [128, 2 * 2048], BF16)
        self.sqb = sb("sqb", [128, 1024], BF16)
        self.ones_bb = sb("ones_bb", [128, 128], BF16)
        fw.op("dve", lambda h: h.memset(self.ones_bb[:], 1.0), [], [self.ones_bb[:]])
        self.seg_done = 0
        self.wst_i = 0
        self.step_i = 0
        self.prefetch_weights(2)
        cseg = self.d_wall[0]
        fw.dma("pool", self.cst[:], self.wtile(cseg, 0, 384), "c1")
        fw.dma("pool", self.amask[:], self.wtile(cseg, 128 * 384, 4096).rearrange("p (a b) -> p a b", a=8), "c2")
        fw.dma("pool", self.bmask[:], self.wtile(cseg, 128 * (384 + 4096), 2048).rearrange("p (a b) -> p a b", a=4), "c3")
        fw.op("dve", lambda h: h.memset(self.ones_f[:], 1.0), [], [self.ones_f[:]])
        if load_x:
            for c in range(KC):
                fw.dma("sp", self.xT[:, c, :], self.d_xT[c * 128:(c + 1) * 128, :], "x%d" % c)
        for si_, step in enumerate(plan):
            self.step_i = si_
            kind = step[0]
            if kind == "attn":
                self.attn_layer(step[1], step[2])
            elif kind == "rec":
                self.rec_layer(step[1], step[2])
            elif kind == "ffn":
                self.ffn_layer(step[1])
            else:
                raise ValueError(kind)
        if store_x:
            for c in range(KC):
                fw.dma("sp", self.d_yT[c * 128:(c + 1) * 128, :], self.xT[:, c, :], "x%d" % c)
            fw.wait_all("sp", [self.d_yT])
        fw.finish()
        self.es.close()

    def mm(self, out, lhsT, rhs, start, stop, skip=False):
        rd = [lhsT, rhs] + ([] if start else [out])
        return self.fw.op("pe", lambda h: h.matmul(out, lhsT, rhs, start=start, stop=stop,
                                                   skip_group_check=skip), rd, [out])

    def act(self, out, in_, func, scale=None, bias=None, eng="act"):
        kw = {}
        rd = [in_]
        if scale is not None:
            kw["scale"] = scale
            if not isinstance(scale, (int, float)):
                rd.append(scale)
        if bias is not None:
            kw["bias"] = bias
            if not isinstance(bias, (int, float)):
                rd.append(bias)
        return self.fw.op("act", lambda h: h.activation(out, in_, func, **kw), rd, [out])

    def tt(self, out, in0, in1, op, eng="dve"):
        return self.fw.op(eng, lambda h: h.tensor_tensor(out, in0, in1, op), [in0, in1], [out])

    def stt(self, out, in0, scalar, in1, op0, op1):
        rd = [in0, in1] + ([] if isinstance(scalar, (int, float)) else [scalar])
        return self.fw.op("dve", lambda h: h.scalar_tensor_tensor(out, in0, scalar, in1, op0, op1), rd, [out])

    def ts(self, out, in0, s1, s2, op0, op1=None, eng="dve"):
        rd = [in0] + [s for s in (s1, s2) if s is not None and not isinstance(s, (int, float))]
        if op1 is None:
            return self.fw.op(eng, lambda h: h.tensor_scalar(out, in0, s1, None, op0), rd, [out])
        return self.fw.op(eng, lambda h: h.tensor_scalar(out, in0, s1, s2, op0, op1), rd, [out])

    def copy(self, out, in_, eng="dve"):
        if eng == "act":
            return self.act(out, in_, AF.Copy)
        return self.fw.op(eng, lambda h: h.tensor_copy(out, in_), [in_], [out])

    def evac(self, out, in_):
        self.evac_rr ^= 1
        return self.copy(out, in_, eng="act" if self.evac_rr else "dve")

    def recip(self, out, in_):
        return self.fw.op("dve", lambda h: h.reciprocal(out, in_), [in_], [out])

    def gidx(self, kind, layer):
        return (kind * 4 + layer) * 8

    def ba(self, off, n):
        return self.BA[:, off:off + n]

    def fa(self, off, n):
        return self.FA[:, off:off + n]

    def dyn_rank(self, h):
        if getattr(self, "_rank_val", None) is None:
            self._rank_val = h.partition_id() % 4
            self._rank_col = self._rank_val * NT
        return self._rank_val

    def dyn_col(self, h):
        self.dyn_rank(h)
        return self._rank_col

    def dyn_val(self, h, mult, add):
        if not hasattr(self, "_dyn_cache"):
            self._dyn_cache = {}
        key = (mult, add)
        if key not in self._dyn_cache:
            self._dyn_cache[key] = self.dyn_rank(h) * mult + add
        return self._dyn_cache[key]

    def gather(self, send, recv, chunk):
        R = int(send.shape[0])
        k = 0
        while k < R:
            c = min(chunk, R - k)
            self.fw.collective("AllGather", [send[k:k + c, :]], [recv[4 * k:4 * k + 4 * c, :]], GROUPS, "cc")
            k += c

    def fetch_rows(self, gathered, dst, nblk, kind, chan):
        def src(h):
            v = gathered.rearrange("(h k r p) c -> h k r p c", h=4, k=nblk, r=4)
            w = v[bass.ds(self.dyn_rank(h), 1), kind]
            return w.rearrange("o r p c -> p (o r) c")
        rd = [gathered[(hh * nblk + kind) * 512:(hh * nblk + kind + 1) * 512, :] for hh in range(4)]
        self.fw.dma("sp", dst, src, chan, rd=rd)

    def fetch_cols(self, gathered, aT):
        for s_ in range(2):
            def src(h, s_=s_):
                v = gathered.rearrange("(r s h p) c -> r s h p c", r=4, s=2, h=4)
                w = v[bass.ds(self.dyn_rank(h), 1), s_]
                return w.rearrange("o h p c -> p (o h) c")
            dst = aT.rearrange("p (h s) c -> p h s c", s=2)[:, :, s_, :]
            self.fw.dma("sp", dst, src, "fc%d" % s_, rd=gathered)

    def wtile(self, seg, off, width, k=None):
        ap = bass.AP(tensor=seg.tensor, offset=int(seg.offset) + off, ap=[[width, 128], [1, width]])
        if k is not None:
            ap = ap.rearrange("p (k m) -> p k m", k=k)
        return ap

    def prefetch_weights(self, upto):
        fw = self.fw
        NB = 2
        upto = min(upto, len(SEG_ROWS))
        while self.seg_done < upto:
            si = self.seg_done
            o = sum(SEG_ROWS[:si]) // 4
            end = o + SEG_ROWS[si] // 4
            row = o
            while row < end:
                n = min(256, end - row)
                npart = n // 2
                i = self.wst_i
                self.wst_i += 1
                st = self.wstage[0:npart, 2048 * (i % NB):2048 * (i % NB + 1)].rearrange("p (a c) -> p a c", a=2)
                if row < WSH_SPLIT:
                    src = self.d_wsh[0][row:row + n, :].rearrange("(p a) c -> p a c", a=2)
                else:
                    src = self.d_wsh[1][row - WSH_SPLIT:row - WSH_SPLIT + n, :].rearrange("(p a) c -> p a c", a=2)
                dst = self.d_wshb[row:row + n, :].rearrange("(p a) c -> p a c", a=2)
                fw.dma("pool", st, src, "wp%d" % (i % NB))
                fw.dma("sp", dst, st, "ws%d" % (i % NB))
                row += n
            self.gather(self.d_wshb[o:end, :], self.d_wall[si], 256)
            self.seg_done += 1

    def seg_of(self, kind, idx):
        return self.d_wall[SEG_INDEX[(kind, idx)]]

    def prenorm(self, g0, tok0, ntiles, hT, hoff):
        sq = [self.sqb[:, 0:512], self.sqb[:, 512:1024]]
        rs = self.fa(1024, 512)
        ss = self.ps[7]
        for ti in range(ntiles):
            t0 = tok0 + ti * TT
            for c in range(KC):
                s = sq[c % 2]
                self.act(s, self.xT[:, c, t0:t0 + TT], AF.Square)
                self.mm(ss[:], self.ones_bb[:], s, c == 0, c == KC - 1)
            self.act(rs, ss[:], AF.Sqrt, scale=1.0 / D, bias=self.eps_ap)
            self.recip(rs, rs)
            for c in range(KC):
                self.stt(hT[:, c, hoff + ti * TT: hoff + (ti + 1) * TT], self.xT[:, c, t0:t0 + TT],
                         self.gains[:, g0 + c:g0 + c + 1], rs, ALU.mult, ALU.mult)

    def postnorm_residual(self, g0, t0, produce):
        mbuf = self.FA[:, 1536:1536 + 4096].rearrange("p (m t) -> p m t", m=8)
        sq = [self.sqb[:, 0:512], self.sqb[:, 512:1024]]
        rs = self.fa(1024, 512)
        tb = [self.fa(5632, 512), self.fa(6144, 512)]
        ss = self.ps[7]
        pend = None
        for m in range(KC):
            pm = self.ps[4 + (m % 3)]
            produce(m, pm)
            if pend is not None:
                pend()
            self.copy(mbuf[:, m, :], pm[:], eng="act")
            s = sq[m % 2]
            self.tt(s, mbuf[:, m, :], mbuf[:, m, :], ALU.mult)

            def _p(s=s, m=m):
                self.mm(ss[:], self.ones_bb[:], s, m == 0, m == KC - 1)
            pend = _p
        pend()
        self.act(rs, ss[:], AF.Sqrt, scale=1.0 / D, bias=self.eps_ap)
        self.recip(rs, rs)
        for m in range(KC):
            t = tb[m % 2]
            self.stt(t, mbuf[:, m, :], self.gains[:, g0 + m:g0 + m + 1], rs, ALU.mult, ALU.mult)
            self.tt(self.xT[:, m, t0:t0 + TT], self.xT[:, m, t0:t0 + TT], t, ALU.add, eng="pool")

    @property
    def eps_ap(self):
        if not hasattr(self, "_eps"):
            t = self.es.enter_context(self.nc.sbuf_tensor("eps_sb", [128, 1], F32))
            self.fw.op("dve", lambda h: h.memset(t[:], EPS), [], [t[:]])
            self._eps = t
        return self._eps[:]

    def ffn_layer(self, L):
        fw = self.fw
        self.prefetch_weights(self.step_i + 4)
        hTh = self.BA[:, 0:8192].rearrange("p (c t) -> p c t", c=8)
        hid = self.BA[:, 8192:8192 + 22528].rearrange("p (f t) -> p f t", f=FC)
        wgu = [[self.BA[:, 30720 + (2 * i + k) * 1024: 30720 + (2 * i + k + 1) * 1024].rearrange("p (k m) -> p k m", k=8)
                for k in range(2)] for i in range(3)]
        wd = [self.BA[:, 36864 + i * DFF: 36864 + (i + 1) * DFF].rearrange("p (k m) -> p k m", k=FC) for i in range(2)]
        sg = [self.fa(6656, 512), self.fa(7168, 512)]
        g_pre = self.gidx(2, L)
        g_post = self.gidx(3, L)
        seg = self.seg_of("ffn", L)
        wdi = 0
        for th in range(2):
            tok0 = th * 1024
            self.prenorm(g_pre, tok0, 2, hTh, 0)
            cnt = 0
            for f in range(FC):
                wg_t, wu_t = wgu[f % 3]
                fw.dma("pool", wg_t, self.wtile(seg, f * 131072, 1024, 8), "wg%d" % (f % 3))
                fw.dma("pool", wu_t, self.wtile(seg, (FC + f) * 131072, 1024, 8), "wu%d" % (f % 3))
                for tt in range(2):
                    pg = self.ps[2 * (cnt % 2)]
                    pu = self.ps[2 * (cnt % 2) + 1]
                    cnt += 1
                    rhs = lambda kc: hTh[:, kc, tt * TT:(tt + 1) * TT]
                    for kc in range(KC):
                        self.mm(pg[:], wg_t[:, kc, :], rhs(kc), kc == 0, kc == KC - 1)
                    for kc in range(KC):
                        self.mm(pu[:], wu_t[:, kc, :], rhs(kc), kc == 0, kc == KC - 1)
                    s = sg[cnt % 2]
                    self.act(s, pg[:], AF.Silu)
                    self.tt(hid[:, f, tt * TT:(tt + 1) * TT], s, pu[:], ALU.mult)
            for tt in range(2):
                def produce(m, pm, tt=tt):
                    nonlocal wdi
                    wt = wd[wdi % 2]
                    fw.dma("sp", wt, self.wtile(seg, 2 * FC * 131072 + m * 128 * DFF, DFF, FC), "wd%d" % (wdi % 2))
                    wdi += 1
                    for kc in range(FC):
                        self.mm(pm[:], wt[:, kc, :], hid[:, kc, tt * TT:(tt + 1) * TT], kc == 0, kc == FC - 1)
                self.postnorm_residual(g_post, tok0 + tt * TT, produce)

    def attn_layer(self, j, L):
        fw = self.fw
        BA = self.BA
        seg = self.seg_of("attn", j)
        hT = BA[:, 0:16384].rearrange("p (c t) -> p c t", c=8)
        wqk = [BA[:, 16384 + i * 1024:16384 + (i + 1) * 1024].rearrange("p (k m) -> p k m", k=8) for i in range(3)]
        wv = [BA[:, 19456 + i * 4096:19456 + (i + 1) * 4096].rearrange("p (k m) -> p k m", k=8) for i in range(2)]
        ost = [BA[:, 27648 + i * 2048:27648 + (i + 1) * 2048] for i in range(2)]
        vst = BA[:, 31744:31744 + 8192].rearrange("p (t f) -> p t f", t=16)
        self.prenorm(self.gidx(0, L), 0, 4, hT, 0)
        def cc_chunk(hp, kind):
            a = (hp * 6 + kind) * 128
            self.gather(self.d_qkv_s[a:a + 128, :], self.d_qkv_r[4 * a:4 * a + 512, :], 128)

        vst2 = BA[:, 31744:31744 + 8192].rearrange("p (g t f) -> p g t f", g=4, t=16)
        for vb in range(2):
            wt = wv[vb]
            fw.dma("pool", wt, self.wtile(seg, 16 * 131072 + vb * 524288, 4096, 8), "wv%d" % vb)
            for tk in range(16):
                pv = self.ps[4 + tk % 3]
                for kc in range(KC):
                    self.mm(pv[:], hT[:, kc, tk * 128:(tk + 1) * 128], wt[:, kc, :], kc == 0, kc == KC - 1)
                self.evac(vst2[:, :, tk, :], pv[:].rearrange("p (g f) -> p g f", g=4))
            for u in range(2):
                hp = 2 * vb + u
                for which in range(2):
                    r0 = hp * 768 + 512 + which * 128
                    fw.dma("sp", self.d_qkv_s[r0:r0 + 128, :].rearrange("p (t f) -> p t f", t=16),
                           vst2[:, 2 * u + which, :, :], "vs%d" % which)
        cnt = 0
        pending = [(hp, 4) for hp in range(4)]
        order = [(kind, hp) for kind in range(4) for hp in range(4)]
        for i, (kind, hp) in enumerate(order):
            mt = hp * 4 + kind
            wt = wqk[i % 3]
            fw.dma("pool", wt, self.wtile(seg, mt * 131072, 1024, 8), "wqk%d" % (i % 3))
            o = ost[i % 2]
            for tt in range(4):
                pz = self.ps[cnt % 4]
                cnt += 1
                for kc in range(KC):
                    self.mm(pz[:], wt[:, kc, :], hT[:, kc, tt * TT:(tt + 1) * TT], kc == 0, kc == KC - 1)
                self.evac(o[:, tt * TT:(tt + 1) * TT], pz[:])
            r0 = hp * 768 + kind * 128
            fw.dma("sp", self.d_qkv_s[r0:r0 + 128, :], o, "qs%d" % (i % 2))
            while pending:
                cc_chunk(*pending.pop(0))
            pending.append((hp, kind))
        while pending:
            cc_chunk(*pending.pop(0))
        for hp in range(4):
            cc_chunk(hp, 5)
        self.prefetch_weights(self.step_i + 3)
        if ATT_STOP == 1:
            fw.wait_all("sp", [self.d_qkv_r])
            return

        Q = BA[:, 0:8192]
        K = BA[:, 8192:16384]
        V = BA[:, 16384:24576].rearrange("p (t f) -> p t f", t=64)
        EB = BA[:, 24576:32768].rearrange("p (e q) -> p e q", e=16)
        pT = [BA[:, 32768 + i * 512:32768 + (i + 1) * 512] for i in range(3)]
        ob = [BA[:, 34304 + i * 512:34304 + (i + 1) * 512] for i in range(2)]
        spb = [BA[:, 35328 + i * 512:35328 + (i + 1) * 512] for i in range(3)]
        wb = [BA[:, 36864 + i * 512:36864 + (i + 1) * 512] for i in range(3)]
        ef = [self.fa(i * 512, 512) for i in range(3)]
        ecf = [self.fa(1536 + i * 512, 512) for i in range(3)]
        stg = [self.fa(3072 + i * 512, 512) for i in range(2)]
        rden = self.fa(4096, 512)

        def load_qkv(which):
            self.fetch_rows(self.d_qkv_r, Q.rearrange("p (r t) -> p r t", r=4), 6, 2 * which, "lq0")
            self.fetch_rows(self.d_qkv_r, K.rearrange("p (r t) -> p r t", r=4), 6, 2 * which + 1, "lq1")
            self.fetch_rows(self.d_qkv_r, BA[:, 16384:24576].rearrange("p (r t) -> p r t", r=4), 6, 4 + which, "lq2")

        load_qkv(0)
        for u in range(2):
            for jj in range(8):
                s = stg[jj % 2]
                s2 = ef[jj % 2]
                src = bass.AP(tensor=self.d_bext.tensor, offset=(j * 2 + u) * 1536 + 896 - 128 * jj,
                              ap=[[1, 128], [1, 512]])
                fw.dma("sp", s, src, "tb%d" % (jj % 2))
                pz = self.ps[jj % 2]
                self.mm(pz[:], self.jmat[:], s, True, True)
                self.act(s2, pz[:], AF.Exp)
                self.tt(EB[:, u * 8 + jj, :], s2, self.amask[:, jj, :], ALU.mult)
        for u in range(2):
            pr = slice(64 * u, 64 * u + 64)
            blocks = []
            for qi in range(16):
                jts = [(jj, 4 * qi - 4 + jj) for jj in range(8) if 4 * qi - 4 + jj >= 0]
                for n, (jj, jt) in enumerate(jts):
                    blocks.append((qi, jj, jt, n == 0, n == len(jts) - 1))

            def st1(i):
                qi, jj, jt, first, last = blocks[i]
                pz = self.ps[i % 3]
                self.mm(pz[:], K[pr, jt * 128:(jt + 1) * 128], Q[pr, qi * TT:(qi + 1) * TT], True, True)
                self.act(ef[i % 3], pz[:], AF.Exp, scale=0.125)

            def st2(i):
                qi, jj, jt, first, last = blocks[i]
                po = self.ps[3 + (qi % 2)]
                pden = self.ps[5 + (qi % 2)]
                p = pT[i % 3]
                self.tt(p, ef[i % 3], EB[:, u * 8 + jj, :], ALU.mult)
                self.mm(po[0:64, :], V[:, jt, 64 * u:64 * u + 64], p, first, last)
                self.mm(pden[0:64, :], self.ones_b[:, 0:64], p, first, last)
                if last:
                    self.recip(rden[0:64, :], pden[0:64, :])
                    o = ob[qi % 2]
                    self.tt(o[0:64, :], po[0:64, :], rden[0:64, :], ALU.mult)
                    ar0 = (qi // 4) * 256 + 64 * u
                    fw.dma("sp", self.d_att_s[ar0:ar0 + 64, (qi % 4) * TT:(qi % 4 + 1) * TT], o[0:64, :],
                           "ao%d" % (qi % 2))

            st1(0)
            for i in range(len(blocks)):
                if i + 1 < len(blocks):
                    st1(i + 1)
                st2(i)
        for r_ in range(4):
            a = r_ * 256
            self.gather(self.d_att_s[a:a + 128, :], self.d_att_r[4 * a:4 * a + 512, :], 128)
        if ATT_STOP == 2:
            fw.wait_all("sp", [self.d_att_s])
            return
        load_qkv(1)
        NBUF = 4
        ef = [self.fa(i * 512, 512) for i in range(NBUF)]
        ecf = [self.fa(2048 + i * 512, 512) for i in range(NBUF)]
        spb = [BA[:, 35328 + i * 512:35328 + (i + 1) * 512] for i in range(NBUF)]
        wb = [BA[:, 37376 + i * 512:37376 + (i + 1) * 512] for i in range(NBUF)]
        ob2 = [BA[:, 39424 + i * 512:39424 + (i + 1) * 512] for i in range(2)]
        cnt = 0
        for qi in range(16):
            jts = list(range(4 * qi + 3, -1, -1))
            nb = len(jts)
            po = self.ps[6 + (qi % 2)]
            Ts = [self.ps[4], self.ps[5]]
            slots = {}

            def stageA(n, u):
                nonlocal cnt
                jt = jts[n]
                pr = slice(64 * u, 64 * u + 64)
                b = cnt % NBUF
                cnt += 1
                pz = self.ps[b]
                slots[(n, u)] = b
                self.mm(pz[:], K[pr, jt * 128:(jt + 1) * 128], Q[pr, qi * TT:(qi + 1) * TT], True, True)
                self.act(ef[b], pz[:], AF.Exp, scale=0.125)
                dj = jt - 4 * qi
                if dj >= 0:
                    self.tt(ef[b], ef[b], self.bmask[:, dj, :], ALU.mult)

            def stageA2(n, u):
                b = slots[(n, u)]
                self.act(spb[b], ef[b], AF.Ln, bias=1.0)

            def stageB1(n, u):
                b = slots[(n, u)]
                self.mm(Ts[u][:], self.tri_b, spb[b], n == 0, True, skip=(n != 0))
                self.act(ecf[b], Ts[u][:], AF.Exp, scale=-1.0)

            def stageB2(n, u):
                b = slots[(n, u)]
                jt = jts[n]
                self.mm(Ts[u][:], self.sl_b, spb[b], False, True, skip=True)
                self.tt(wb[b], ef[b], ecf[b], ALU.mult)
                self.mm(po[64 * u:64 * u + 64, :], V[:, jt, 64 * u:64 * u + 64], wb[b], n == 0, n == nb - 1)

            for u in range(2):
                stageA(0, u)
            for u in range(2):
                stageA2(0, u)
            for n in range(nb):
                for u in range(2):
                    stageB1(n, u)
                if n + 1 < nb:
                    for u in range(2):
                        stageA(n + 1, u)
                    for u in range(2):
                        stageA2(n + 1, u)
                for u in range(2):
                    stageB2(n, u)
            o = ob2[qi % 2]
            self.evac(o[:], po[:])
            ar0 = (qi // 4) * 256 + 128
            fw.dma("sp", self.d_att_s[ar0:ar0 + 128, (qi % 4) * TT:(qi % 4 + 1) * TT], o[:], "ao%d" % (qi % 2))
            if qi % 4 == 3:
                self.gather(self.d_att_s[ar0:ar0 + 128, :], self.d_att_r[4 * ar0:4 * ar0 + 512, :], 128)
        self.prefetch_weights(self.step_i + 4)
        if ATT_STOP == 3:
            fw.wait_all("sp", [self.d_att_r])
            return

        aT = BA[:, 0:16384].rearrange("p (c t) -> p c t", c=8)
        wo = [BA[:, 16384 + m * 1024:16384 + (m + 1) * 1024].rearrange("p (k m) -> p k m", k=8) for m in range(8)]
        self.fetch_cols(self.d_att_r, aT)
        for m in range(8):
            fw.dma("pool", wo[m], self.wtile(seg, (16 + 8 + m) * 131072, 1024, 8), "wo%d" % (m % 4))
        g_post = self.gidx(1, L)
        for tt in range(4):
            def produce(m, pm, tt=tt):
                for kc in range(KC):
                    self.mm(pm[:], wo[m][:, kc, :], aT[:, kc, tt * TT:(tt + 1) * TT], kc == 0, kc == KC - 1)
            self.postnorm_residual(g_post, tt * TT, produce)

    def rec_layer(self, j, L):
        fw = self.fw
        BA = self.BA
        seg = self.seg_of("rec", j)
        hT = BA[:, 0:16384].rearrange("p (c t) -> p c t", c=8)
        win = [BA[:, 16384 + i * 1024:16384 + (i + 1) * 1024].rearrange("p (k m) -> p k m", k=8) for i in range(3)]
        ost = [BA[:, 27648 + i * 2048:27648 + (i + 1) * 2048] for i in range(2)]
        self.prenorm(self.gidx(0, L), 0, 4, hT, 0)
        cnt = 0
        for mt in range(16):
            n, kind, half = mt // 4, (mt // 2) % 2, mt % 2
            wt = win[mt % 3]
            fw.dma("pool", wt, self.wtile(seg, mt * 131072, 1024, 8), "wqk%d" % (mt % 3))
            o = ost[mt % 2]
            for tt in range(4):
                pz = self.ps[cnt % 4]
                cnt += 1
                for kc in range(KC):
                    self.mm(pz[:], wt[:, kc, :], hT[:, kc, tt * TT:(tt + 1) * TT], kc == 0, kc == KC - 1)
                if kind == 0:
                    self.act(o[:, tt * TT:(tt + 1) * TT], pz[:], AF.Gelu_apprx_tanh)
                else:
                    self.copy(o[:, tt * TT:(tt + 1) * TT], pz[:], eng="dve")
            r0 = n * 512 + kind * 256 + half * 128
            fw.dma("sp", self.d_rec_s[r0:r0 + 128, :], o, "qs%d" % (mt % 2))
            if mt >= 1:
                a = rprev
                self.gather(self.d_rec_s[a:a + 128, :], self.d_rec_r[4 * a:4 * a + 512, :], 128)
            rprev = r0
        self.gather(self.d_rec_s[rprev:rprev + 128, :], self.d_rec_r[4 * rprev:4 * rprev + 512, :], 128)
        self.prefetch_weights(self.step_i + 3)

        G = BA[:, 0:16384].rearrange("p (c t) -> p c t", c=2)
        XR = BA[:, 16384:16384 + 16400].rearrange("p (c t) -> p c t", c=2)
        PAD = 8
        xcb = [BA[:, 32784 + i * 1024:32784 + (i + 1) * 1024].rearrange("p (c t) -> p c t", c=2) for i in range(2)]
        wa = BA[:, 34832:34832 + 512].rearrange("p (k m) -> p k m", k=2)
        wi = BA[:, 35344:35344 + 512].rearrange("p (k m) -> p k m", k=2)
        ob = [BA[:, 35856 + i * 512:35856 + (i + 1) * 512] for i in range(4)]
        xc = [self.FA[:, i * 1024:(i + 1) * 1024].rearrange("p (c t) -> p c t", c=2) for i in range(2)]
        rb = [self.fa(2048 + i * 512, 512) for i in range(2)]
        ib = [self.fa(3072 + i * 512, 512) for i in range(2)]
        a2b = [self.fa(4096 + i * 512, 512) for i in range(2)]
        hsb = [[self.fa(5120 + (2 * hf + i) * 512, 512) for i in range(2)] for hf in range(2)]
        sm = self.fa(7168, 16)
        cv = self.fa(7184, 4)
        fw.dma("sp", sm, self.d_rsm[j], "c0")
        fw.dma("pool", wa, self.d_rwa[j].rearrange("p (k m) -> p k m", k=2), "c1")
        fw.dma("pool", wi, self.d_rwi[j].rearrange("p (k m) -> p k m", k=2), "c2")
        self.act(cv[:, 0:2], sm[:, 14:16], AF.Exp, scale=-1.0)
        self.act(cv[:, 0:2], cv[:, 0:2], AF.Ln, bias=1.0)
        self.ts(cv[:, 2:4], cv[:, 0:2], -16.0, None, ALU.mult)
        self.ts(cv[:, 0:2], cv[:, 0:2], -8.0, None, ALU.mult)
        for hf in range(2):
            fw.op("dve", lambda h, hf=hf: h.memset(XR[:, hf, 0:PAD], 0.0), [], [XR[:, hf, 0:PAD]])
        for kind in range(2):
            for hf in range(2):
                dstt = (G[:, hf, :] if kind == 0 else XR[:, hf, PAD:PAD + S]).rearrange("p (r t) -> p r t", r=4)
                self.fetch_rows(self.d_rec_r, dstt, 4, kind * 2 + hf, "lq%d" % (2 * kind + hf))
        for ti in range(16):
            t0 = ti * TT
            x_c = xc[ti % 2]
            x_b = xcb[ti % 2]
            for hf in range(2):
                base = PAD + t0 - 3
                self.ts(x_c[:, hf, :], XR[:, hf, base:base + TT], sm[:, 4 * hf:4 * hf + 1], sm[:, 8 + hf:9 + hf],
                        ALU.mult, ALU.add)
                for jw in range(1, 4):
                    self.stt(x_c[:, hf, :], XR[:, hf, base + jw:base + jw + TT], sm[:, 4 * hf + jw:4 * hf + jw + 1],
                             x_c[:, hf, :], ALU.mult, ALU.add)
                self.copy(x_b[:, hf, :], x_c[:, hf, :], eng="pool")
            for jt in range(2):
                prr = self.ps[2 * jt]
                pii = self.ps[2 * jt + 1]
                for ic in range(2):
                    self.mm(prr[:], wa[:, ic, jt * 128:(jt + 1) * 128], x_b[:, ic, :], ic == 0, ic == 1)
                for ic in range(2):
                    self.mm(pii[:], wi[:, ic, jt * 128:(jt + 1) * 128], x_b[:, ic, :], ic == 0, ic == 1)
                r_ = rb[jt]
                i_ = ib[jt]
                a2 = a2b[jt]
                hs = hsb[jt][ti % 2]
                hprev = hsb[jt][(ti + 1) % 2]
                self.act(r_, prr[:], AF.Sigmoid, bias=sm[:, 10 + jt:11 + jt])
                self.act(i_, pii[:], AF.Sigmoid, bias=sm[:, 12 + jt:13 + jt])
                self.act(a2, r_, AF.Exp, scale=cv[:, 2 + jt:3 + jt])
                self.act(r_, r_, AF.Exp, scale=cv[:, jt:jt + 1])
                self.act(a2, a2, AF.Sqrt, scale=-1.0, bias=1.0)
                self.tt(i_, i_, x_c[:, jt, :], ALU.mult)
                self.tt(i_, i_, a2, ALU.mult)
                init = 0.0 if ti == 0 else hprev[:, TT - 1:TT]
                rd = [r_, i_] + ([] if ti == 0 else [init])
                fw.op("dve", lambda h, hs=hs, r_=r_, i_=i_, init=init: h.tensor_tensor_scan(
                    hs, r_, i_, init, ALU.mult, ALU.add), rd, [hs])
                o = ob[(2 * ti + jt) % 4]
                self.tt(o, hs, G[:, jt, t0:t0 + TT], ALU.mult, eng="pool")
                rr0 = (ti // 4) * 256 + jt * 128
                fw.dma("sp", self.d_rec2_s[rr0:rr0 + 128, (ti % 4) * TT:(ti % 4 + 1) * TT], o, "ao%d" % (jt))
            if ti % 4 == 3:
                g0 = (ti // 4) * 256
                self.gather(self.d_rec2_s[g0:g0 + 256, :], self.d_rec2_r[4 * g0:4 * g0 + 1024, :], 128)
        self.prefetch_weights(self.step_i + 4)

        aT = BA[:, 0:16384].rearrange("p (c t) -> p c t", c=8)
        wo = [BA[:, 16384 + m * 1024:16384 + (m + 1) * 1024].rearrange("p (k m) -> p k m", k=8) for m in range(8)]
        self.fetch_cols(self.d_rec2_r, aT)
        for m in range(8):
            fw.dma("pool", wo[m], self.wtile(seg, (16 + m) * 131072, 1024, 8), "wo%d" % (m % 4))
        g_post = self.gidx(1, L)
        for tt in range(4):
            def produce(m, pm, tt=tt):
                for kc in range(KC):
                    self.mm(pm[:], wo[m][:, kc, :], aT[:, kc, tt * TT:(tt + 1) * TT], kc == 0, kc == KC - 1)
            self.postnorm_residual(g_post, tt * TT, produce)


def _tile_w(w, cols):
    K = w.shape[0]
    kc = K // 128
    out = []
    for c in cols:
        blk = w[:, c]
        m = blk.shape[1]
        out.append(blk.reshape(kc, 128, m).transpose(1, 0, 2).reshape(128, kc * m))
    return np.ascontiguousarray(np.stack(out, 0))


def _consts():
    k = np.arange(128)[:, None]
    q = np.arange(512)[None, :]
    cst = np.zeros((128, 384), np.float32)
    cst[:, 0:128] = 1.0
    kk = np.arange(128)[:, None]
    kp = np.arange(128)[None, :]
    cst[:, 128:256] = (kk >= kp).astype(np.float32)
    cst[:, 256:384] = (kk < kp).astype(np.float32)
    am = np.zeros((128, 8, 512), np.float32)
    for jj in range(8):
        kc = 2 * jj + k // 64
        qc = q // 64
        am[:, jj, :] = ((kc >= qc) & (kc <= qc + 8)).astype(np.float32)
    bm = np.zeros((128, 4, 512), np.float32)
    for dj in range(4):
        bm[:, dj, :] = ((k + 128 * dj) < q).astype(np.float32)
    return cst, am.reshape(128, 4096), bm.reshape(128, 2048)


def _toeplitz_idx():
    k = np.arange(128)[:, None]
    q = np.arange(512)[None, :]
    idx = np.zeros((8, 128, 512), np.int64)
    for jj in range(8):
        idx[jj] = np.clip(512 - 128 * jj + q - k, -256, 256) + 256
    return idx


def prepare_inputs(inp, plan=None):
    plan = FULL_PLAN if plan is None else plan
    SEG_KEYS, SEG_ROWS, WSH_ROWS, WSH_SPLIT = seg_layout(plan)
    f = lambda a: np.asarray(a, dtype=np.float32)
    x = f(inp["x"])
    shared = {}
    G = np.stack([f(inp["norm_mix_pre"]), f(inp["norm_mix_post"]), f(inp["norm_ffn_pre"]), f(inp["norm_ffn_post"])], 0)
    shared["gains"] = np.ascontiguousarray(G.reshape(4, 4, 8, 128).transpose(3, 0, 1, 2).reshape(128, 128))
    shared["jmat"] = np.ascontiguousarray(np.eye(128, dtype=np.float32)[::-1])
    cst, am, bm = _consts()
    ar = np.arange(128)
    segs = {}
    segs[("const", 0)] = np.concatenate([cst.ravel(), am.ravel(), bm.ravel()])
    w_in = f(inp["attn_w_in"])
    w_out = f(inp["attn_w_out"])
    kind_base = [0, 512, 1536, 2048]
    cols = [kind_base[kind] + 128 * hp + ar for hp in range(4) for kind in range(4)]
    vcols = []
    for vb in range(2):
        vcols.append(np.concatenate([1024 + 128 * (2 * vb) + ar, 2560 + 128 * (2 * vb) + ar,
                                     1024 + 128 * (2 * vb + 1) + ar, 2560 + 128 * (2 * vb + 1) + ar]))
    rows = np.concatenate([np.concatenate([128 * hp + ar, 512 + 128 * hp + ar]) for hp in range(4)])
    mcols = [128 * m + ar for m in range(8)]
    for j in range(2):
        if ("attn", j) not in SEG_KEYS:
            continue
        segs[("attn", j)] = np.concatenate([_tile_w(w_in[j], cols).ravel(), _tile_w(w_in[j], vcols).ravel(),
                                            _tile_w(w_out[j][rows, :], mcols).ravel()])
    rw_in = f(inp["rg_w_in"])
    rw_out = f(inp["rg_w_out"])
    rcols = [kind * 1024 + n * 256 + half * 128 + ar for n in range(4) for kind in range(2) for half in range(2)]
    for j in range(2):
        if ("rec", j) not in SEG_KEYS:
            continue
        segs[("rec", j)] = np.concatenate([_tile_w(rw_in[j], rcols).ravel(), _tile_w(rw_out[j], mcols).ravel()])
    wg, wu, wd = f(inp["ffn_w_gate"]), f(inp["ffn_w_up"]), f(inp["ffn_w_down"])
    fcols = [128 * t + ar for t in range(FC)]
    for l in range(4):
        if ("ffn", l) not in SEG_KEYS:
            continue
        segs[("ffn", l)] = np.concatenate([_tile_w(wg[l], fcols).ravel(), _tile_w(wu[l], fcols).ravel(),
                                           _tile_w(wd[l], mcols).ravel()])
    shards = [[] for _ in range(8)]
    for si, key in enumerate(SEG_KEYS):
        a = segs[key]
        assert a.size == SEG_ROWS[si] * 1024, (key, a.size)
        a = a.reshape(SEG_ROWS[si], 1024)
        rs = SEG_ROWS[si] // 4
        parts = [[] for _ in range(4)]
        k = 0
        while k < rs:
            cch = min(256, rs - k)
            for r in range(4):
                parts[r].append(a[4 * k + r * cch:4 * k + (r + 1) * cch])
            k += cch
        for c in range(8):
            shards[c].append(np.concatenate(parts[c % 4], 0).ravel())
    rel_bias = f(inp["attn_rel_bias"])
    eidx = np.clip(np.arange(1536) - 511, -256, 256) + 256
    w_a, w_i = f(inp["rg_w_a"]), f(inp["rg_w_i"])
    conv_w, conv_b = f(inp["rg_conv_w"]), f(inp["rg_conv_b"])
    b_a, b_i, lam = f(inp["rg_b_a"]), f(inp["rg_b_i"]), f(inp["rg_lambda"])
    maps = []
    for c in range(8):
        b, r = c // 4, c % 4
        m = dict(shared)
        m["xT"] = np.ascontiguousarray(x[b, r * NT:(r + 1) * NT, :].T)
        wfull = np.concatenate(shards[c]).reshape(WSH_ROWS, 1024)
        m["wsh0"] = np.ascontiguousarray(wfull[:WSH_SPLIT])
        m["wsh1"] = np.ascontiguousarray(wfull[WSH_SPLIT:])
        m["biasext"] = np.ascontiguousarray(
            np.stack([np.stack([rel_bias[j, 2 * r + u][eidx] for u in range(2)], 0) for j in range(2)], 0))
        n = r
        m["r_wa"] = np.stack([_tile_w(w_a[j, n], [np.arange(256)])[0] for j in range(2)], 0)
        m["r_wi"] = np.stack([_tile_w(w_i[j, n], [np.arange(256)])[0] for j in range(2)], 0)
        sm = np.zeros((2, 128, 16), np.float32)
        for j in range(2):
            for hf in range(2):
                ch = n * 256 + hf * 128 + ar
                for jw in range(4):
                    sm[j, :, 4 * hf + jw] = conv_w[j, jw, 0, ch]
                sm[j, :, 8 + hf] = conv_b[j, ch]
                sm[j, :, 10 + hf] = b_a[j].reshape(-1)[ch]
                sm[j, :, 12 + hf] = b_i[j].reshape(-1)[ch]
                sm[j, :, 14 + hf] = lam[j, ch]
        m["r_small"] = sm
        maps.append(m)
    return maps


import os as _os
ATT_STOP = int(_os.environ.get("ATT_STOP", "0"))
FULL_PLAN = [("attn", 0, 0), ("ffn", 0), ("rec", 0, 1), ("ffn", 1), ("attn", 1, 2), ("ffn", 2), ("rec", 1, 3), ("ffn", 3)]
_PROG_CACHE = {}


def run_plan(plan, maps):
    key = tuple(plan)
    if key not in _PROG_CACHE:
        _PROG_CACHE[key] = Prog(plan)
    prog = _PROG_CACHE[key]
    res = run_bass_kernel_spmd(prog.nc, maps, core_ids=list(range(8)))
    return [r["yT"] for r in res.results]


def assemble(ys):
    out = np.zeros((2, S, D), np.float32)
    for c in range(8):
        b, r = c // 4, c % 4
        out[b, r * NT:(r + 1) * NT, :] = ys[c].T
    return out


def kernel(**inputs):
    maps = prepare_inputs(inputs)
    ys = run_plan(FULL_PLAN, maps)
    return assemble(ys)
```

```python
import numpy as np
from contextlib import ExitStack
import concourse.bass as bass
import concourse.mybir as mybir
from concourse.bass_utils import run_bass_kernel_spmd

F32 = mybir.dt.float32
BF16 = mybir.dt.bfloat16
AF = mybir.ActivationFunctionType
ALU = mybir.AluOpType

ENGS = ("pe", "act", "dve", "pool", "sp")


class Tok:
    __slots__ = ("sem", "val", "eng")

    def __init__(self, sem, val, eng=None):
        self.sem = sem
        self.val = val
        self.eng = eng


class Chan:
    def __init__(self, sem):
        self.sem = sem
        self.total = 0


def _bbox(ap):
    t = ap.tensor
    shape = list(t.shape)
    pat = [(int(s), int(n)) for s, n in ap.ap]
    off = int(ap.offset)
    space = str(ap.space)
    if "dram" in space.lower():
        ext = sum((n - 1) * abs(s) for s, n in pat)
        return (0, 1, off, off + ext + 1)
    row = 1
    for d in shape[1:]:
        row *= int(d)
    p0 = off // row
    f0 = off % row
    s0, n0 = pat[0]
    if s0 != 0 and s0 % row == 0:
        pstep = s0 // row
        p1 = p0 + (n0 - 1) * pstep + 1
        rest = pat[1:]
    elif s0 == 0:
        p1 = p0 + 1
        rest = pat[1:]
    else:
        p1 = p0 + 1
        rest = pat
    ext = sum((n - 1) * abs(s) for s, n in rest)
    return (p0, p1, f0, f0 + ext + 1)


class FW:
    def __init__(self, nc, sync_same_engine=True):
        self.nc = nc
        self.ops = {e: [] for e in ENGS}
        self.sem = {}
        self.cnt = {e: 0 for e in ENGS}
        self.waited = {e: {} for e in ENGS}
        self.acc = {}
        self.sync_same = sync_same_engine
        self._stack = []
        self.n_instr = 0
        self.n_wait = 0
        for e in ENGS:
            self.sem[e] = self._sem("c_" + e)
        self.chans = {}

    def _sem(self, name):
        cm = self.nc.semaphore(name)
        s = cm.__enter__()
        self._stack.append(cm)
        return s

    def chan(self, name):
        if name not in self.chans:
            self.chans[name] = Chan(self._sem("d_" + name))
        return self.chans[name]

    def _deps(self, reads, writes):
        deps = []
        for ap, is_w in [(a, False) for a in reads] + [(a, True) for a in writes]:
            name = ap.tensor.name
            bb = _bbox(ap)
            for rec in self.acc.get(name, ()):
                rb, tok, rw = rec
                if not (is_w or rw):
                    continue
                if rb[0] < bb[1] and bb[0] < rb[1] and rb[2] < bb[3] and bb[2] < rb[3]:
                    deps.append((tok, rw, is_w))
        return deps

    def _record(self, reads, writes, tok):
        for ap in writes:
            name = ap.tensor.name
            bb = _bbox(ap)
            lst = self.acc.setdefault(name, [])
            lst[:] = [r for r in lst if not (bb[0] <= r[0][0] and r[0][1] <= bb[1]
                                             and bb[2] <= r[0][2] and r[0][3] <= bb[3])]
            lst.append([bb, tok, True])
        for ap in reads:
            name = ap.tensor.name
            bb = _bbox(ap)
            lst = self.acc.setdefault(name, [])
            lst[:] = [r for r in lst if not ((not r[2]) and r[1].sem is tok.sem
                                             and bb[0] <= r[0][0] and r[0][1] <= bb[1]
                                             and bb[2] <= r[0][2] and r[0][3] <= bb[3])]
            lst.append([bb, tok, False])

    def _emit_waits(self, e, deps, is_dma=False):
        need = {}
        for tok, rw, is_w in deps:
            if tok.eng == e and not is_dma:
                if e == "pe":
                    continue
                if not self.sync_same:
                    continue
                if not rw:
                    continue
            k = id(tok.sem)
            if k not in need or need[k][1] < tok.val:
                need[k] = (tok.sem, tok.val)
        for k, (sem, val) in need.items():
            if self.waited[e].get(k, 0) >= val:
                continue
            self.waited[e][k] = val
            self.n_wait += 1
            self.ops[e].append(("wait", sem, val))

    def op(self, e, fn, reads, writes):
        deps = self._deps(reads, writes)
        self._emit_waits(e, deps)
        self.cnt[e] += 1
        tok = Tok(self.sem[e], self.cnt[e], e)
        self.ops[e].append(("op", fn, self.sem[e]))
        self._record(reads, writes, tok)
        self.n_instr += 1
        return tok

    def dma(self, q, out, in_, chan, rd=None, wr=None, **kw):
        ch = self.chan(chan)
        r = (list(rd) if isinstance(rd, (list, tuple)) else [rd]) if rd is not None else [in_]
        w = [wr if wr is not None else out]
        deps = self._deps(r, w)
        if ch.total > 0:
            deps.append((Tok(ch.sem, ch.total, None), True, True))
        self._emit_waits(q, deps, is_dma=True)
        ch.total += 16
        tok = Tok(ch.sem, ch.total, None)
        self.ops[q].append(("dma", out, in_, ch.sem, kw))
        self._record(r, w, tok)
        self.n_instr += 1
        return tok

    def collective(self, kind, ins, outs, groups, chan):
        ch = self.chan(chan)
        deps = self._deps(ins, outs)
        self._emit_waits("pool", deps, is_dma=True)
        ch.total += 1
        tok = Tok(ch.sem, ch.total, None)
        self.ops["pool"].append(("cc", kind, ins, outs, groups, ch.sem))
        self._record(ins, outs, tok)
        return tok

    def wait_all(self, e, aps):
        deps = self._deps([], aps)
        self._emit_waits(e, deps, is_dma=True)

    def _replay(self, e, h):
        for item in self.ops[e]:
            k = item[0]
            if k == "wait":
                h.wait_ge(item[1], item[2])
            elif k == "op":
                item[1](h).then_inc(item[2], 1)
            elif k == "dma":
                o = item[1](h) if callable(item[1]) else item[1]
                i = item[2](h) if callable(item[2]) else item[2]
                h.dma_start(out=o, in_=i, **item[4]).then_inc(item[3], 16)
            elif k == "cc":
                _, kind, ins, outs, groups, sem = item
                h.collective_compute(kind, ALU.bypass, replica_groups=groups,
                                     ins=[a.opt() for a in ins],
                                     outs=[a.opt() for a in outs]).then_inc(sem, 1)

    def finish(self):
        nc = self.nc
        with nc.Block() as block:
            @block.tensor
            def _(h):
                self._replay("pe", h)

            @block.scalar
            def _(h):
                self._replay("act", h)

            @block.vector
            def _(h):
                self._replay("dve", h)

            @block.gpsimd
            def _(h):
                self._replay("pool", h)

            @block.sync
            def _(h):
                self._replay("sp", h)
        for cm in reversed(self._stack):
            cm.__exit__(None, None, None)
        self._stack = []


D = 1024
KC = 8
S = 8192
NT = 2048
TT = 512
DFF = 2816
FC = 22
DEPTH = 4
EPS = 1e-6
GROUPS = [[0, 1, 2, 3], [4, 5, 6, 7]]

_SEG_R = {"const": 816, "attn": 4096, "ffn": 8448, "rec": 3072}


def seg_layout(plan):
    keys = [("const", 0)]
    for st in plan:
        keys.append((st[0], st[1]))
    rows = [_SEG_R[k] for k, _ in keys]
    tot = sum(rows) // 4
    acc, split = 0, tot
    for i, r in enumerate(rows):
        if acc + r // 4 > 6400:
            split = acc
            break
        acc += r // 4
    if split == tot:
        split = rows[0] // 4
    return keys, rows, tot, split

BA_SIZE = 43008
FA_SIZE = 8192


class Prog:
    def __init__(self, plan, load_x=True, store_x=True):
        self.plan = plan
        global SEG_KEYS, SEG_ROWS, SEG_INDEX, WSH_ROWS, WSH_SPLIT
        SEG_KEYS, SEG_ROWS, WSH_ROWS, WSH_SPLIT = seg_layout(plan)
        SEG_INDEX = {k: i for i, k in enumerate(SEG_KEYS)}
        nc = self.nc = bass.Bass("TRN2", target_bir_lowering=False)
        self.fw = FW(nc)
        self.es = ExitStack()
        dt = nc.dram_tensor

        def ext(name, shape, dtype=F32):
            return dt(name, shape, dtype, kind="ExternalInput").ap()

        self.d_xT = ext("xT", [D, NT])
        self.d_gains = ext("gains", [128, 128])
        self.d_wsh = [ext("wsh0", [WSH_SPLIT, 1024]), ext("wsh1", [WSH_ROWS - WSH_SPLIT, 1024])]
        self.d_bext = ext("biasext", [2, 2, 1536])
        self.d_jmat = ext("jmat", [128, 128])
        self.d_rwa = ext("r_wa", [2, 128, 512])
        self.d_rwi = ext("r_wi", [2, 128, 512])
        self.d_rsm = ext("r_small", [2, 128, 16])
        self.d_wshb = dt("wsh_bf", [WSH_ROWS, 1024], BF16).ap()
        self.d_wall = [dt("wall%d" % i, [SEG_ROWS[i], 1024], BF16).ap() for i in range(len(SEG_ROWS))]
        self.d_yT = dt("yT", [D, NT], F32, kind="ExternalOutput").ap()
        self.d_qkv_s = dt("qkv_s", [3072, 2048], BF16).ap()
        self.d_qkv_r = dt("qkv_r", [4 * 3072, 2048], BF16).ap()
        self.d_myqkv = dt("myqkv", [3072, 2048], BF16).ap()
        self.d_myrec = dt("myrec", [2048, 2048], BF16).ap()
        self.d_att_s = dt("att_s", [1024, 2048], BF16).ap()
        self.d_att_r = dt("att_r", [4096, 2048], BF16).ap()
        self.d_rec_s = dt("rec_s", [2048, 2048], BF16).ap()
        self.d_rec_r = dt("rec_r", [4 * 2048, 2048], BF16).ap()
        self.d_rec2_s = dt("rec2_s", [1024, 2048], BF16).ap()
        self.d_rec2_r = dt("rec2_r", [4096, 2048], BF16).ap()

        sb = lambda name, shape, dtype: self.es.enter_context(nc.sbuf_tensor(name, shape, dtype))
        self.xT = sb("xT_sb", [128, KC, NT], F32)
        self.gains = sb("gains_sb", [128, 128], F32)
        self.ones_f = sb("ones_f", [128, 128], F32)
        self.cst = sb("cst_sb", [128, 384], BF16)
        self.amask = sb("amask_sb", [128, 8, 512], BF16)
        self.bmask = sb("bmask_sb", [128, 4, 512], BF16)
        self.BA = sb("BA", [128, BA_SIZE], BF16)
        self.FA = sb("FA", [128, FA_SIZE], F32)
        self.ps = [self.es.enter_context(nc.psum_tensor("ps%d" % i, [128, 512], F32)) for i in range(8)]
        self.ones_b = self.cst[:, 0:128]
        self.tri_b = self.cst[:, 128:256]
        self.sl_b = self.cst[:, 256:384]
        self.evac_rr = 0

        fw = self.fw
        fw.dma("sp", self.gains[:], self.d_gains, "c0")
        self.jmat = sb("jmat_sb", [128, 128], F32)
        fw.dma("sp", self.jmat[:], self.d_jmat, "c4")
        self.wstage = sb("wstage", [128, 2 * 2048], BF16)
        self.sqb = sb("sqb", [128, 1024], BF16)
        self.ones_bb = sb("ones_bb", [128, 128], BF16)
        fw.op("dve", lambda h: h.memset(self.ones_bb[:], 1.0), [], [self.ones_bb[:]])
        self.seg_done = 0
        self.wst_i = 0
        self.step_i = 0
        self.prefetch_weights(2)
        cseg = self.d_wall[0]
        fw.dma("pool", self.cst[:], self.wtile(cseg, 0, 384), "c1")
        fw.dma("pool", self.amask[:], self.wtile(cseg, 128 * 384, 4096).rearrange("p (a b) -> p a b", a=8), "c2")
        fw.dma("pool", self.bmask[:], self.wtile(cseg, 128 * (384 + 4096), 2048).rearrange("p (a b) -> p a b", a=4), "c3")
        fw.op("dve", lambda h: h.memset(self.ones_f[:], 1.0), [], [self.ones_f[:]])
        if load_x:
            for c in range(KC):
                fw.dma("sp", self.xT[:, c, :], self.d_xT[c * 128:(c + 1) * 128, :], "x%d" % c)
        for si_, step in enumerate(plan):
            self.step_i = si_
            kind = step[0]
            if kind == "attn":
                self.attn_layer(step[1], step[2])
            elif kind == "rec":
                self.rec_layer(step[1], step[2])
            elif kind == "ffn":
                self.ffn_layer(step[1])
            else:
                raise ValueError(kind)
        if store_x:
            for c in range(KC):
                fw.dma("sp", self.d_yT[c * 128:(c + 1) * 128, :], self.xT[:, c, :], "x%d" % c)
            fw.wait_all("sp", [self.d_yT])
        fw.finish()
        self.es.close()

    def mm(self, out, lhsT, rhs, start, stop, skip=False):
        rd = [lhsT, rhs] + ([] if start else [out])
        return self.fw.op("pe", lambda h: h.matmul(out, lhsT, rhs, start=start, stop=stop,
                                                   skip_group_check=skip), rd, [out])

    def act(self, out, in_, func, scale=None, bias=None, eng="act"):
        kw = {}
        rd = [in_]
        if scale is not None:
            kw["scale"] = scale
            if not isinstance(scale, (int, float)):
                rd.append(scale)
        if bias is not None:
            kw["bias"] = bias
            if not isinstance(bias, (int, float)):
                rd.append(bias)
        return self.fw.op("act", lambda h: h.activation(out, in_, func, **kw), rd, [out])

    def tt(self, out, in0, in1, op, eng="dve"):
        return self.fw.op(eng, lambda h: h.tensor_tensor(out, in0, in1, op), [in0, in1], [out])

    def stt(self, out, in0, scalar, in1, op0, op1):
        rd = [in0, in1] + ([] if isinstance(scalar, (int, float)) else [scalar])
        return self.fw.op("dve", lambda h: h.scalar_tensor_tensor(out, in0, scalar, in1, op0, op1), rd, [out])

    def ts(self, out, in0, s1, s2, op0, op1=None, eng="dve"):
        rd = [in0] + [s for s in (s1, s2) if s is not None and not isinstance(s, (int, float))]
        if op1 is None:
            return self.fw.op(eng, lambda h: h.tensor_scalar(out, in0, s1, None, op0), rd, [out])
        return self.fw.op(eng, lambda h: h.tensor_scalar(out, in0, s1, s2, op0, op1), rd, [out])

    def copy(self, out, in_, eng="dve"):
        if eng == "act":
            return self.act(out, in_, AF.Copy)
        return self.fw.op(eng, lambda h: h.tensor_copy(out, in_), [in_], [out])

    def evac(self, out, in_):
        self.evac_rr ^= 1
        return self.copy(out, in_, eng="act" if self.evac_rr else "dve")

    def recip(self, out, in_):
        return self.fw.op("dve", lambda h: h.reciprocal(out, in_), [in_], [out])

    def gidx(self, kind, layer):
        return (kind * 4 + layer) * 8

    def ba(self, off, n):
        return self.BA[:, off:off + n]

    def fa(self, off, n):
        return self.FA[:, off:off + n]

    def dyn_rank(self, h):
        if getattr(self, "_rank_val", None) is None:
            self._rank_val = h.partition_id() % 4
            self._rank_col = self._rank_val * NT
        return self._rank_val

    def dyn_col(self, h):
        self.dyn_rank(h)
        return self._rank_col

    def dyn_val(self, h, mult, add):
        if not hasattr(self, "_dyn_cache"):
            self._dyn_cache = {}
        key = (mult, add)
        if key not in self._dyn_cache:
            self._dyn_cache[key] = self.dyn_rank(h) * mult + add
        return self._dyn_cache[key]

    def gather(self, send, recv, chunk):
        R = int(send.shape[0])
        k = 0
        while k < R:
            c = min(chunk, R - k)
            self.fw.collective("AllGather", [send[k:k + c, :]], [recv[4 * k:4 * k + 4 * c, :]], GROUPS, "cc")
            k += c

    def fetch_rows(self, gathered, dst, nblk, kind, chan):
        def src(h):
            v = gathered.rearrange("(h k r p) c -> h k r p c", h=4, k=nblk, r=4)
            w = v[bass.ds(self.dyn_rank(h), 1), kind]
            return w.rearrange("o r p c -> p (o r) c")
        rd = [gathered[(hh * nblk + kind) * 512:(hh * nblk + kind + 1) * 512, :] for hh in range(4)]
        self.fw.dma("sp", dst, src, chan, rd=rd)

    def fetch_cols(self, gathered, aT):
        for s_ in range(2):
            def src(h, s_=s_):
                v = gathered.rearrange("(r s h p) c -> r s h p c", r=4, s=2, h=4)
                w = v[bass.ds(self.dyn_rank(h), 1), s_]
                return w.rearrange("o h p c -> p (o h) c")
            dst = aT.rearrange("p (h s) c -> p h s c", s=2)[:, :, s_, :]
            self.fw.dma("sp", dst, src, "fc%d" % s_, rd=gathered)

    def wtile(self, seg, off, width, k=None):
        ap = bass.AP(tensor=seg.tensor, offset=int(seg.offset) + off, ap=[[width, 128], [1, width]])
        if k is not None:
            ap = ap.rearrange("p (k m) -> p k m", k=k)
        return ap

    def prefetch_weights(self, upto):
        fw = self.fw
        NB = 2
        upto = min(upto, len(SEG_ROWS))
        while self.seg_done < upto:
            si = self.seg_done
            o = sum(SEG_ROWS[:si]) // 4
            end = o + SEG_ROWS[si] // 4
            row = o
            while row < end:
                n = min(256, end - row)
                npart = n // 2
                i = self.wst_i
                self.wst_i += 1
                st = self.wstage[0:npart, 2048 * (i % NB):2048 * (i % NB + 1)].rearrange("p (a c) -> p a c", a=2)
                if row < WSH_SPLIT:
                    src = self.d_wsh[0][row:row + n, :].rearrange("(p a) c -> p a c", a=2)
                else:
                    src = self.d_wsh[1][row - WSH_SPLIT:row - WSH_SPLIT + n, :].rearrange("(p a) c -> p a c", a=2)
                dst = self.d_wshb[row:row + n, :].rearrange("(p a) c -> p a c", a=2)
                fw.dma("pool", st, src, "wp%d" % (i % NB))
                fw.dma("sp", dst, st, "ws%d" % (i % NB))
                row += n
            self.gather(self.d_wshb[o:end, :], self.d_wall[si], 256)
            self.seg_done += 1

    def seg_of(self, kind, idx):
        return self.d_wall[SEG_INDEX[(kind, idx)]]

    def prenorm(self, g0, tok0, ntiles, hT, hoff):
        sq = [self.sqb[:, 0:512], self.sqb[:, 512:1024]]
        rs = self.fa(1024, 512)
        ss = self.ps[7]
        for ti in range(ntiles):
            t0 = tok0 + ti * TT
            for c in range(KC):
                s = sq[c % 2]
                self.act(s, self.xT[:, c, t0:t0 + TT], AF.Square)
                self.mm(ss[:], self.ones_bb[:], s, c == 0, c == KC - 1)
            self.act(rs, ss[:], AF.Sqrt, scale=1.0 / D, bias=self.eps_ap)
            self.recip(rs, rs)
            for c in range(KC):
                self.stt(hT[:, c, hoff + ti * TT: hoff + (ti + 1) * TT], self.xT[:, c, t0:t0 + TT],
                         self.gains[:, g0 + c:g0 + c + 1], rs, ALU.mult, ALU.mult)

    def postnorm_residual(self, g0, t0, produce):
        mbuf = self.FA[:, 1536:1536 + 4096].rearrange("p (m t) -> p m t", m=8)
        sq = [self.sqb[:, 0:512], self.sqb[:, 512:1024]]
        rs = self.fa(1024, 512)
        tb = [self.fa(5632, 512), self.fa(6144, 512)]
        ss = self.ps[7]
        pend = None
        for m in range(KC):
            pm = self.ps[4 + (m % 3)]
            produce(m, pm)
            if pend is not None:
                pend()
            self.copy(mbuf[:, m, :], pm[:], eng="act")
            s = sq[m % 2]
            self.tt(s, mbuf[:, m, :], mbuf[:, m, :], ALU.mult)

            def _p(s=s, m=m):
                self.mm(ss[:], self.ones_bb[:], s, m == 0, m == KC - 1)
            pend = _p
        pend()
        self.act(rs, ss[:], AF.Sqrt, scale=1.0 / D, bias=self.eps_ap)
        self.recip(rs, rs)
        for m in range(KC):
            t = tb[m % 2]
            self.stt(t, mbuf[:, m, :], self.gains[:, g0 + m:g0 + m + 1], rs, ALU.mult, ALU.mult)
            self.tt(self.xT[:, m, t0:t0 + TT], self.xT[:, m, t0:t0 + TT], t, ALU.add, eng="pool")

    @property
    def eps_ap(self):
        if not hasattr(self, "_eps"):
            t = self.es.enter_context(self.nc.sbuf_tensor("eps_sb", [128, 1], F32))
            self.fw.op("dve", lambda h: h.memset(t[:], EPS), [], [t[:]])
            self._eps = t
        return self._eps[:]

    def ffn_layer(self, L):
        fw = self.fw
        self.prefetch_weights(self.step_i + 4)
        hTh = self.BA[:, 0:8192].rearrange("p (c t) -> p c t", c=8)
        hid = self.BA[:, 8192:8192 + 22528].rearrange("p (f t) -> p f t", f=FC)
        wgu = [[self.BA[:, 30720 + (2 * i + k) * 1024: 30720 + (2 * i + k + 1) * 1024].rearrange("p (k m) -> p k m", k=8)
                for k in range(2)] for i in range(3)]
        wd = [self.BA[:, 36864 + i * DFF: 36864 + (i + 1) * DFF].rearrange("p (k m) -> p k m", k=FC) for i in range(2)]
        sg = [self.fa(6656, 512), self.fa(7168, 512)]
        g_pre = self.gidx(2, L)
        g_post = self.gidx(3, L)
        seg = self.seg_of("ffn", L)
        wdi = 0
        for th in range(2):
            tok0 = th * 1024
            self.prenorm(g_pre, tok0, 2, hTh, 0)
            cnt = 0
            for f in range(FC):
                wg_t, wu_t = wgu[f % 3]
                fw.dma("pool", wg_t, self.wtile(seg, f * 131072, 1024, 8), "wg%d" % (f % 3))
                fw.dma("pool", wu_t, self.wtile(seg, (FC + f) * 131072, 1024, 8), "wu%d" % (f % 3))
                for tt in range(2):
                    pg = self.ps[2 * (cnt % 2)]
                    pu = self.ps[2 * (cnt % 2) + 1]
                    cnt += 1
                    rhs = lambda kc: hTh[:, kc, tt * TT:(tt + 1) * TT]
                    for kc in range(KC):
                        self.mm(pg[:], wg_t[:, kc, :], rhs(kc), kc == 0, kc == KC - 1)
                    for kc in range(KC):
                        self.mm(pu[:], wu_t[:, kc, :], rhs(kc), kc == 0, kc == KC - 1)
                    s = sg[cnt % 2]
                    self.act(s, pg[:], AF.Silu)
                    self.tt(hid[:, f, tt * TT:(tt + 1) * TT], s, pu[:], ALU.mult)
            for tt in range(2):
                def produce(m, pm, tt=tt):
                    nonlocal wdi
                    wt = wd[wdi % 2]
                    fw.dma("sp", wt, self.wtile(seg, 2 * FC * 131072 + m * 128 * DFF, DFF, FC), "wd%d" % (wdi % 2))
                    wdi += 1
                    for kc in range(FC):
                        self.mm(pm[:], wt[:, kc, :], hid[:, kc, tt * TT:(tt + 1) * TT], kc == 0, kc == FC - 1)
                self.postnorm_residual(g_post, tok0 + tt * TT, produce)

    def attn_layer(self, j, L):
        fw = self.fw
        BA = self.BA
        seg = self.seg_of("attn", j)
        hT = BA[:, 0:16384].rearrange("p (c t) -> p c t", c=8)
        wqk = [BA[:, 16384 + i * 1024:16384 + (i + 1) * 1024].rearrange("p (k m) -> p k m", k=8) for i in range(3)]
        wv = [BA[:, 19456 + i * 4096:19456 + (i + 1) * 4096].rearrange("p (k m) -> p k m", k=8) for i in range(2)]
        ost = [BA[:, 27648 + i * 2048:27648 + (i + 1) * 2048] for i in range(2)]
        vst = BA[:, 31744:31744 + 8192].rearrange("p (t f) -> p t f", t=16)
        self.prenorm(self.gidx(0, L), 0, 4, hT, 0)
        def cc_chunk(hp, kind):
            a = (hp * 6 + kind) * 128
            self.gather(self.d_qkv_s[a:a + 128, :], self.d_qkv_r[4 * a:4 * a + 512, :], 128)

        vst2 = BA[:, 31744:31744 + 8192].rearrange("p (g t f) -> p g t f", g=4, t=16)
        for vb in range(2):
            wt = wv[vb]
            fw.dma("pool", wt, self.wtile(seg, 16 * 131072 + vb * 524288, 4096, 8), "wv%d" % vb)
            for tk in range(16):
                pv = self.ps[4 + tk % 3]
                for kc in range(KC):
                    self.mm(pv[:], hT[:, kc, tk * 128:(tk + 1) * 128], wt[:, kc, :], kc == 0, kc == KC - 1)
                self.evac(vst2[:, :, tk, :], pv[:].rearrange("p (g f) -> p g f", g=4))
            for u in range(2):
                hp = 2 * vb + u
                for which in range(2):
                    r0 = hp * 768 + 512 + which * 128
                    fw.dma("sp", self.d_qkv_s[r0:r0 + 128, :].rearrange("p (t f) -> p t f", t=16),
                           vst2[:, 2 * u + which, :, :], "vs%d" % which)
        cnt = 0
        pending = [(hp, 4) for hp in range(4)]
        order = [(kind, hp) for kind in range(4) for hp in range(4)]
        for i, (kind, hp) in enumerate(order):
            mt = hp * 4 + kind
            wt = wqk[i % 3]
            fw.dma("pool", wt, self.wtile(seg, mt * 131072, 1024, 8), "wqk%d" % (i % 3))
            o = ost[i % 2]
            for tt in range(4):
                pz = self.ps[cnt % 4]
                cnt += 1
                for kc in range(KC):
                    self.mm(pz[:], wt[:, kc, :], hT[:, kc, tt * TT:(tt + 1) * TT], kc == 0, kc == KC - 1)
                self.evac(o[:, tt * TT:(tt + 1) * TT], pz[:])
            r0 = hp * 768 + kind * 128
            fw.dma("sp", self.d_qkv_s[r0:r0 + 128, :], o, "qs%d" % (i % 2))
            while pending:
                cc_chunk(*pending.pop(0))
            pending.append((hp, kind))
        while pending:
            cc_chunk(*pending.pop(0))
        for hp in range(4):
            cc_chunk(hp, 5)
        self.prefetch_weights(self.step_i + 3)
        if ATT_STOP == 1:
            fw.wait_all("sp", [self.d_qkv_r])
            return

        Q = BA[:, 0:8192]
        K = BA[:, 8192:16384]
        V = BA[:, 16384:24576].rearrange("p (t f) -> p t f", t=64)
        EB = BA[:, 24576:32768].rearrange("p (e q) -> p e q", e=16)
        pT = [BA[:, 32768 + i * 512:32768 + (i + 1) * 512] for i in range(3)]
        ob = [BA[:, 34304 + i * 512:34304 + (i + 1) * 512] for i in range(2)]
        spb = [BA[:, 35328 + i * 512:35328 + (i + 1) * 512] for i in range(3)]
        wb = [BA[:, 36864 + i * 512:36864 + (i + 1) * 512] for i in range(3)]
        ef = [self.fa(i * 512, 512) for i in range(3)]
        ecf = [self.fa(1536 + i * 512, 512) for i in range(3)]
        stg = [self.fa(3072 + i * 512, 512) for i in range(2)]
        rden = self.fa(4096, 512)

        def load_qkv(which):
            self.fetch_rows(self.d_qkv_r, Q.rearrange("p (r t) -> p r t", r=4), 6, 2 * which, "lq0")
            self.fetch_rows(self.d_qkv_r, K.rearrange("p (r t) -> p r t", r=4), 6, 2 * which + 1, "lq1")
            self.fetch_rows(self.d_qkv_r, BA[:, 16384:24576].rearrange("p (r t) -> p r t", r=4), 6, 4 + which, "lq2")

        load_qkv(0)
        for u in range(2):
            for jj in range(8):
                s = stg[jj % 2]
                s2 = ef[jj % 2]
                src = bass.AP(tensor=self.d_bext.tensor, offset=(j * 2 + u) * 1536 + 896 - 128 * jj,
                              ap=[[1, 128], [1, 512]])
                fw.dma("sp", s, src, "tb%d" % (jj % 2))
                pz = self.ps[jj % 2]
                self.mm(pz[:], self.jmat[:], s, True, True)
                self.act(s2, pz[:], AF.Exp)
                self.tt(EB[:, u * 8 + jj, :], s2, self.amask[:, jj, :], ALU.mult)
        for u in range(2):
            pr = slice(64 * u, 64 * u + 64)
            blocks = []
            for qi in range(16):
                jts = [(jj, 4 * qi - 4 + jj) for jj in range(8) if 4 * qi - 4 + jj >= 0]
                for n, (jj, jt) in enumerate(jts):
                    blocks.append((qi, jj, jt, n == 0, n == len(jts) - 1))

            def st1(i):
                qi, jj, jt, first, last = blocks[i]
                pz = self.ps[i % 3]
                self.mm(pz[:], K[pr, jt * 128:(jt + 1) * 128], Q[pr, qi * TT:(qi + 1) * TT], True, True)
                self.act(ef[i % 3], pz[:], AF.Exp, scale=0.125)

            def st2(i):
                qi, jj, jt, first, last = blocks[i]
                po = self.ps[3 + (qi % 2)]
                pden = self.ps[5 + (qi % 2)]
                p = pT[i % 3]
                self.tt(p, ef[i % 3], EB[:, u * 8 + jj, :], ALU.mult)
                self.mm(po[0:64, :], V[:, jt, 64 * u:64 * u + 64], p, first, last)
                self.mm(pden[0:64, :], self.ones_b[:, 0:64], p, first, last)
                if last:
                    self.recip(rden[0:64, :], pden[0:64, :])
                    o = ob[qi % 2]
                    self.tt(o[0:64, :], po[0:64, :], rden[0:64, :], ALU.mult)
                    ar0 = (qi // 4) * 256 + 64 * u
                    fw.dma("sp", self.d_att_s[ar0:ar0 + 64, (qi % 4) * TT:(qi % 4 + 1) * TT], o[0:64, :],
                           "ao%d" % (qi % 2))

            st1(0)
            for i in range(len(blocks)):
                if i + 1 < len(blocks):
                    st1(i + 1)
                st2(i)
        for r_ in range(4):
            a = r_ * 256
            self.gather(self.d_att_s[a:a + 128, :], self.d_att_r[4 * a:4 * a + 512, :], 128)
        if ATT_STOP == 2:
            fw.wait_all("sp", [self.d_att_s])
            return
        load_qkv(1)
        NBUF = 4
        ef = [self.fa(i * 512, 512) for i in range(NBUF)]
        ecf = [self.fa(2048 + i * 512, 512) for i in range(NBUF)]
        spb = [BA[:, 35328 + i * 512:35328 + (i + 1) * 512] for i in range(NBUF)]
        wb = [BA[:, 37376 + i * 512:37376 + (i + 1) * 512] for i in range(NBUF)]
        ob2 = [BA[:, 39424 + i * 512:39424 + (i + 1) * 512] for i in range(2)]
        cnt = 0
        for qi in range(16):
            jts = list(range(4 * qi + 3, -1, -1))
            nb = len(jts)
            po = self.ps[6 + (qi % 2)]
            Ts = [self.ps[4], self.ps[5]]
            slots = {}

            def stageA(n, u):
                nonlocal cnt
                jt = jts[n]
                pr = slice(64 * u, 64 * u + 64)
                b = cnt % NBUF
                cnt += 1
                pz = self.ps[b]
                slots[(n, u)] = b
                self.mm(pz[:], K[pr, jt * 128:(jt + 1) * 128], Q[pr, qi * TT:(qi + 1) * TT], True, True)
                self.act(ef[b], pz[:], AF.Exp, scale=0.125)
                dj = jt - 4 * qi
                if dj >= 0:
                    self.tt(ef[b], ef[b], self.bmask[:, dj, :], ALU.mult)

            def stageA2(n, u):
                b = slots[(n, u)]
                self.act(spb[b], ef[b], AF.Ln, bias=1.0)

            def stageB1(n, u):
                b = slots[(n, u)]
                self.mm(Ts[u][:], self.tri_b, spb[b], n == 0, True, skip=(n != 0))
                self.act(ecf[b], Ts[u][:], AF.Exp, scale=-1.0)

            def stageB2(n, u):
                b = slots[(n, u)]
                jt = jts[n]
                self.mm(Ts[u][:], self.sl_b, spb[b], False, True, skip=True)
                self.tt(wb[b], ef[b], ecf[b], ALU.mult)
                self.mm(po[64 * u:64 * u + 64, :], V[:, jt, 64 * u:64 * u + 64], wb[b], n == 0, n == nb - 1)

            for u in range(2):
                stageA(0, u)
            for u in range(2):
                stageA2(0, u)
            for n in range(nb):
                for u in range(2):
                    stageB1(n, u)
                if n + 1 < nb:
                    for u in range(2):
                        stageA(n + 1, u)
                    for u in range(2):
                        stageA2(n + 1, u)
                for u in range(2):
                    stageB2(n, u)
            o = ob2[qi % 2]
            self.evac(o[:], po[:])
            ar0 = (qi // 4) * 256 + 128
            fw.dma("sp", self.d_att_s[ar0:ar0 + 128, (qi % 4) * TT:(qi % 4 + 1) * TT], o[:], "ao%d" % (qi % 2))
            if qi % 4 == 3:
                self.gather(self.d_att_s[ar0:ar0 + 128, :], self.d_att_r[4 * ar0:4 * ar0 + 512, :], 128)
        self.prefetch_weights(self.step_i + 4)
        if ATT_STOP == 3:
            fw.wait_all("sp", [self.d_att_r])
            return

        aT = BA[:, 0:16384].rearrange("p (c t) -> p c t", c=8)
        wo = [BA[:, 16384 + m * 1024:16384 + (m + 1) * 1024].rearrange("p (k m) -> p k m", k=8) for m in range(8)]
        self.fetch_cols(self.d_att_r, aT)
        for m in range(8):
            fw.dma("pool", wo[m], self.wtile(seg, (16 + 8 + m) * 131072, 1024, 8), "wo%d" % (m % 4))
        g_post = self.gidx(1, L)
        for tt in range(4):
            def produce(m, pm, tt=tt):
                for kc in range(KC):
                    self.mm(pm[:], wo[m][:, kc, :], aT[:, kc, tt * TT:(tt + 1) * TT], kc == 0, kc == KC - 1)
            self.postnorm_residual(g_post, tt * TT, produce)

    def rec_layer(self, j, L):
        fw = self.fw
        BA = self.BA
        seg = self.seg_of("rec", j)
        hT = BA[:, 0:16384].rearrange("p (c t) -> p c t", c=8)
        win = [BA[:, 16384 + i * 1024:16384 + (i + 1) * 1024].rearrange("p (k m) -> p k m", k=8) for i in range(3)]
        ost = [BA[:, 27648 + i * 2048:27648 + (i + 1) * 2048] for i in range(2)]
        self.prenorm(self.gidx(0, L), 0, 4, hT, 0)
        cnt = 0
        for mt in range(16):
            n, kind, half = mt // 4, (mt // 2) % 2, mt % 2
            wt = win[mt % 3]
            fw.dma("pool", wt, self.wtile(seg, mt * 131072, 1024, 8), "wqk%d" % (mt % 3))
            o = ost[mt % 2]
            for tt in range(4):
                pz = self.ps[cnt % 4]
                cnt += 1
                for kc in range(KC):
                    self.mm(pz[:], wt[:, kc, :], hT[:, kc, tt * TT:(tt + 1) * TT], kc == 0, kc == KC - 1)
                if kind == 0:
                    self.act(o[:, tt * TT:(tt + 1) * TT], pz[:], AF.Gelu_apprx_tanh)
                else:
                    self.copy(o[:, tt * TT:(tt + 1) * TT], pz[:], eng="dve")
            r0 = n * 512 + kind * 256 + half * 128
            fw.dma("sp", self.d_rec_s[r0:r0 + 128, :], o, "qs%d" % (mt % 2))
            if mt >= 1:
                a = rprev
                self.gather(self.d_rec_s[a:a + 128, :], self.d_rec_r[4 * a:4 * a + 512, :], 128)
            rprev = r0
        self.gather(self.d_rec_s[rprev:rprev + 128, :], self.d_rec_r[4 * rprev:4 * rprev + 512, :], 128)
        self.prefetch_weights(self.step_i + 3)

        G = BA[:, 0:16384].rearrange("p (c t) -> p c t", c=2)
        XR = BA[:, 16384:16384 + 16400].rearrange("p (c t) -> p c t", c=2)
        PAD = 8
        xcb = [BA[:, 32784 + i * 1024:32784 + (i + 1) * 1024].rearrange("p (c t) -> p c t", c=2) for i in range(2)]
        wa = BA[:, 34832:34832 + 512].rearrange("p (k m) -> p k m", k=2)
        wi = BA[:, 35344:35344 + 512].rearrange("p (k m) -> p k m", k=2)
        ob = [BA[:, 35856 + i * 512:35856 + (i + 1) * 512] for i in range(4)]
        xc = [self.FA[:, i * 1024:(i + 1) * 1024].rearrange("p (c t) -> p c t", c=2) for i in range(2)]
        rb = [self.fa(2048 + i * 512, 512) for i in range(2)]
        ib = [self.fa(3072 + i * 512, 512) for i in range(2)]
        a2b = [self.fa(4096 + i * 512, 512) for i in range(2)]
        hsb = [[self.fa(5120 + (2 * hf + i) * 512, 512) for i in range(2)] for hf in range(2)]
        sm = self.fa(7168, 16)
        cv = self.fa(7184, 4)
        fw.dma("sp", sm, self.d_rsm[j], "c0")
        fw.dma("pool", wa, self.d_rwa[j].rearrange("p (k m) -> p k m", k=2), "c1")
        fw.dma("pool", wi, self.d_rwi[j].rearrange("p (k m) -> p k m", k=2), "c2")
        self.act(cv[:, 0:2], sm[:, 14:16], AF.Exp, scale=-1.0)
        self.act(cv[:, 0:2], cv[:, 0:2], AF.Ln, bias=1.0)
        self.ts(cv[:, 2:4], cv[:, 0:2], -16.0, None, ALU.mult)
        self.ts(cv[:, 0:2], cv[:, 0:2], -8.0, None, ALU.mult)
        for hf in range(2):
            fw.op("dve", lambda h, hf=hf: h.memset(XR[:, hf, 0:PAD], 0.0), [], [XR[:, hf, 0:PAD]])
        for kind in range(2):
            for hf in range(2):
                dstt = (G[:, hf, :] if kind == 0 else XR[:, hf, PAD:PAD + S]).rearrange("p (r t) -> p r t", r=4)
                self.fetch_rows(self.d_rec_r, dstt, 4, kind * 2 + hf, "lq%d" % (2 * kind + hf))
        for ti in range(16):
            t0 = ti * TT
            x_c = xc[ti % 2]
            x_b = xcb[ti % 2]
            base = PAD + t0 - 3
            for hf in range(2):
                self.ts(x_c[:, hf, :], XR[:, hf, base:base + TT], sm[:, 4 * hf:4 * hf + 1], sm[:, 8 + hf:9 + hf],
                        ALU.mult, ALU.add)
            for jw in range(1, 4):
                for hf in range(2):
                    self.stt(x_c[:, hf, :], XR[:, hf, base + jw:base + jw + TT], sm[:, 4 * hf + jw:4 * hf + jw + 1],
                             x_c[:, hf, :], ALU.mult, ALU.add)
            for hf in range(2):
                self.copy(x_b[:, hf, :], x_c[:, hf, :], eng="pool")
            for jt in range(2):
                for ic in range(2):
                    self.mm(self.ps[2 * jt][:], wa[:, ic, jt * 128:(jt + 1) * 128], x_b[:, ic, :], ic == 0, ic == 1)
                for ic in range(2):
                    self.mm(self.ps[2 * jt + 1][:], wi[:, ic, jt * 128:(jt + 1) * 128], x_b[:, ic, :], ic == 0, ic == 1)
            for jt in range(2):
                self.act(rb[jt], self.ps[2 * jt][:], AF.Sigmoid, bias=sm[:, 10 + jt:11 + jt])
            for jt in range(2):
                self.act(ib[jt], self.ps[2 * jt + 1][:], AF.Sigmoid, bias=sm[:, 12 + jt:13 + jt])
            for jt in range(2):
                self.act(a2b[jt], rb[jt], AF.Exp, scale=cv[:, 2 + jt:3 + jt])
            for jt in range(2):
                self.act(rb[jt], rb[jt], AF.Exp, scale=cv[:, jt:jt + 1])
            for jt in range(2):
                self.act(a2b[jt], a2b[jt], AF.Sqrt, scale=-1.0, bias=1.0)
            for jt in range(2):
                self.tt(ib[jt], ib[jt], x_c[:, jt, :], ALU.mult)
            for jt in range(2):
                self.tt(ib[jt], ib[jt], a2b[jt], ALU.mult)
            for jt in range(2):
                r_, i_ = rb[jt], ib[jt]
                hs = hsb[jt][ti % 2]
                hprev = hsb[jt][(ti + 1) % 2]
                init = 0.0 if ti == 0 else hprev[:, TT - 1:TT]
                rd = [r_, i_] + ([] if ti == 0 else [init])
                fw.op("dve", lambda h, hs=hs, r_=r_, i_=i_, init=init: h.tensor_tensor_scan(
                    hs, r_, i_, init, ALU.mult, ALU.add), rd, [hs])
            for jt in range(2):
                hs = hsb[jt][ti % 2]
                o = ob[(2 * ti + jt) % 4]
                self.tt(o, hs, G[:, jt, t0:t0 + TT], ALU.mult, eng="pool")
                rr0 = (ti // 4) * 256 + jt * 128
                fw.dma("sp", self.d_rec2_s[rr0:rr0 + 128, (ti % 4) * TT:(ti % 4 + 1) * TT], o, "ao%d" % (jt))
            if ti % 4 == 3:
                g0 = (ti // 4) * 256
                self.gather(self.d_rec2_s[g0:g0 + 256, :], self.d_rec2_r[4 * g0:4 * g0 + 1024, :], 128)
        self.prefetch_weights(self.step_i + 4)

        aT = BA[:, 0:16384].rearrange("p (c t) -> p c t", c=8)
        wo = [BA[:, 16384 + m * 1024:16384 + (m + 1) * 1024].rearrange("p (k m) -> p k m", k=8) for m in range(8)]
        self.fetch_cols(self.d_rec2_r, aT)
        for m in range(8):
            fw.dma("pool", wo[m], self.wtile(seg, (16 + m) * 131072, 1024, 8), "wo%d" % (m % 4))
        g_post = self.gidx(1, L)
        for tt in range(4):
            def produce(m, pm, tt=tt):
                for kc in range(KC):
                    self.mm(pm[:], wo[m][:, kc, :], aT[:, kc, tt * TT:(tt + 1) * TT], kc == 0, kc == KC - 1)
            self.postnorm_residual(g_post, tt * TT, produce)


def _tile_w(w, cols):
    K = w.shape[0]
    kc = K // 128
    out = []
    for c in cols:
        blk = w[:, c]
        m = blk.shape[1]
        out.append(blk.reshape(kc, 128, m).transpose(1, 0, 2).reshape(128, kc * m))
    return np.ascontiguousarray(np.stack(out, 0))


def _consts():
    k = np.arange(128)[:, None]
    q = np.arange(512)[None, :]
    cst = np.zeros((128, 384), np.float32)
    cst[:, 0:128] = 1.0
    kk = np.arange(128)[:, None]
    kp = np.arange(128)[None, :]
    cst[:, 128:256] = (kk >= kp).astype(np.float32)
    cst[:, 256:384] = (kk < kp).astype(np.float32)
    am = np.zeros((128, 8, 512), np.float32)
    for jj in range(8):
        kc = 2 * jj + k // 64
        qc = q // 64
        am[:, jj, :] = ((kc >= qc) & (kc <= qc + 8)).astype(np.float32)
    bm = np.zeros((128, 4, 512), np.float32)
    for dj in range(4):
        bm[:, dj, :] = ((k + 128 * dj) < q).astype(np.float32)
    return cst, am.reshape(128, 4096), bm.reshape(128, 2048)


def _toeplitz_idx():
    k = np.arange(128)[:, None]
    q = np.arange(512)[None, :]
    idx = np.zeros((8, 128, 512), np.int64)
    for jj in range(8):
        idx[jj] = np.clip(512 - 128 * jj + q - k, -256, 256) + 256
    return idx


def prepare_inputs(inp, plan=None):
    plan = FULL_PLAN if plan is None else plan
    SEG_KEYS, SEG_ROWS, WSH_ROWS, WSH_SPLIT = seg_layout(plan)
    f = lambda a: np.asarray(a, dtype=np.float32)
    x = f(inp["x"])
    shared = {}
    G = np.stack([f(inp["norm_mix_pre"]), f(inp["norm_mix_post"]), f(inp["norm_ffn_pre"]), f(inp["norm_ffn_post"])], 0)
    shared["gains"] = np.ascontiguousarray(G.reshape(4, 4, 8, 128).transpose(3, 0, 1, 2).reshape(128, 128))
    shared["jmat"] = np.ascontiguousarray(np.eye(128, dtype=np.float32)[::-1])
    cst, am, bm = _consts()
    ar = np.arange(128)
    segs = {}
    segs[("const", 0)] = np.concatenate([cst.ravel(), am.ravel(), bm.ravel()])
    w_in = f(inp["attn_w_in"])
    w_out = f(inp["attn_w_out"])
    kind_base = [0, 512, 1536, 2048]
    cols = [kind_base[kind] + 128 * hp + ar for hp in range(4) for kind in range(4)]
    vcols = []
    for vb in range(2):
        vcols.append(np.concatenate([1024 + 128 * (2 * vb) + ar, 2560 + 128 * (2 * vb) + ar,
                                     1024 + 128 * (2 * vb + 1) + ar, 2560 + 128 * (2 * vb + 1) + ar]))
    rows = np.concatenate([np.concatenate([128 * hp + ar, 512 + 128 * hp + ar]) for hp in range(4)])
    mcols = [128 * m + ar for m in range(8)]
    for j in range(2):
        if ("attn", j) not in SEG_KEYS:
            continue
        segs[("attn", j)] = np.concatenate([_tile_w(w_in[j], cols).ravel(), _tile_w(w_in[j], vcols).ravel(),
                                            _tile_w(w_out[j][rows, :], mcols).ravel()])
    rw_in = f(inp["rg_w_in"])
    rw_out = f(inp["rg_w_out"])
    rcols = [kind * 1024 + n * 256 + half * 128 + ar for n in range(4) for kind in range(2) for half in range(2)]
    for j in range(2):
        if ("rec", j) not in SEG_KEYS:
            continue
        segs[("rec", j)] = np.concatenate([_tile_w(rw_in[j], rcols).ravel(), _tile_w(rw_out[j], mcols).ravel()])
    wg, wu, wd = f(inp["ffn_w_gate"]), f(inp["ffn_w_up"]), f(inp["ffn_w_down"])
    fcols = [128 * t + ar for t in range(FC)]
    for l in range(4):
        if ("ffn", l) not in SEG_KEYS:
            continue
        segs[("ffn", l)] = np.concatenate([_tile_w(wg[l], fcols).ravel(), _tile_w(wu[l], fcols).ravel(),
                                           _tile_w(wd[l], mcols).ravel()])
    shards = [[] for _ in range(8)]
    for si, key in enumerate(SEG_KEYS):
        a = segs[key]
        assert a.size == SEG_ROWS[si] * 1024, (key, a.size)
        a = a.reshape(SEG_ROWS[si], 1024)
        rs = SEG_ROWS[si] // 4
        parts = [[] for _ in range(4)]
        k = 0
        while k < rs:
            cch = min(256, rs - k)
            for r in range(4):
                parts[r].append(a[4 * k + r * cch:4 * k + (r + 1) * cch])
            k += cch
        for c in range(8):
            shards[c].append(np.concatenate(parts[c % 4], 0).ravel())
    rel_bias = f(inp["attn_rel_bias"])
    eidx = np.clip(np.arange(1536) - 511, -256, 256) + 256
    w_a, w_i = f(inp["rg_w_a"]), f(inp["rg_w_i"])
    conv_w, conv_b = f(inp["rg_conv_w"]), f(inp["rg_conv_b"])
    b_a, b_i, lam = f(inp["rg_b_a"]), f(inp["rg_b_i"]), f(inp["rg_lambda"])
    maps = []
    for c in range(8):
        b, r = c // 4, c % 4
        m = dict(shared)
        m["xT"] = np.ascontiguousarray(x[b, r * NT:(r + 1) * NT, :].T)
        wfull = np.concatenate(shards[c]).reshape(WSH_ROWS, 1024)
        m["wsh0"] = np.ascontiguousarray(wfull[:WSH_SPLIT])
        m["wsh1"] = np.ascontiguousarray(wfull[WSH_SPLIT:])
        m["biasext"] = np.ascontiguousarray(
            np.stack([np.stack([rel_bias[j, 2 * r + u][eidx] for u in range(2)], 0) for j in range(2)], 0))
        n = r
        m["r_wa"] = np.stack([_tile_w(w_a[j, n], [np.arange(256)])[0] for j in range(2)], 0)
        m["r_wi"] = np.stack([_tile_w(w_i[j, n], [np.arange(256)])[0] for j in range(2)], 0)
        sm = np.zeros((2, 128, 16), np.float32)
        for j in range(2):
            for hf in range(2):
                ch = n * 256 + hf * 128 + ar
                for jw in range(4):
                    sm[j, :, 4 * hf + jw] = conv_w[j, jw, 0, ch]
                sm[j, :, 8 + hf] = conv_b[j, ch]
                sm[j, :, 10 + hf] = b_a[j].reshape(-1)[ch]
                sm[j, :, 12 + hf] = b_i[j].reshape(-1)[ch]
                sm[j, :, 14 + hf] = lam[j, ch]
        m["r_small"] = sm
        maps.append(m)
    return maps


import os as _os
ATT_STOP = int(_os.environ.get("ATT_STOP", "0"))
FULL_PLAN = [("attn", 0, 0), ("ffn", 0), ("rec", 0, 1), ("ffn", 1), ("attn", 1, 2), ("ffn", 2), ("rec", 1, 3), ("ffn", 3)]
_PROG_CACHE = {}


def run_plan(plan, maps):
    key = tuple(plan)
    if key not in _PROG_CACHE:
        _PROG_CACHE[key] = Prog(plan)
    prog = _PROG_CACHE[key]
    res = run_bass_kernel_spmd(prog.nc, maps, core_ids=list(range(8)))
    return [r["yT"] for r in res.results]


def assemble(ys):
    out = np.zeros((2, S, D), np.float32)
    for c in range(8):
        b, r = c // 4, c % 4
        out[b, r * NT:(r + 1) * NT, :] = ys[c].T
    return out


def kernel(**inputs):
    maps = prepare_inputs(inputs)
    ys = run_plan(FULL_PLAN, maps)
    return assemble(ys)
```

```python
import numpy as np
from contextlib import ExitStack
import concourse.bass as bass
import concourse.mybir as mybir
from concourse.bass_utils import run_bass_kernel_spmd

F32 = mybir.dt.float32
BF16 = mybir.dt.bfloat16
AF = mybir.ActivationFunctionType
ALU = mybir.AluOpType

ENGS = ("pe", "act", "dve", "pool", "sp")


class Tok:
    __slots__ = ("sem", "val", "eng")

    def __init__(self, sem, val, eng=None):
        self.sem = sem
        self.val = val
        self.eng = eng


class Chan:
    def __init__(self, sem):
        self.sem = sem
        self.total = 0


def _bbox(ap):
    t = ap.tensor
    shape = list(t.shape)
    pat = [(int(s), int(n)) for s, n in ap.ap]
    off = int(ap.offset)
    space = str(ap.space)
    if "dram" in space.lower():
        ext = sum((n - 1) * abs(s) for s, n in pat)
        return (0, 1, off, off + ext + 1)
    row = 1
    for d in shape[1:]:
        row *= int(d)
    p0 = off // row
    f0 = off % row
    s0, n0 = pat[0]
    if s0 != 0 and s0 % row == 0:
        pstep = s0 // row
        p1 = p0 + (n0 - 1) * pstep + 1
        rest = pat[1:]
    elif s0 == 0:
        p1 = p0 + 1
        rest = pat[1:]
    else:
        p1 = p0 + 1
        rest = pat
    ext = sum((n - 1) * abs(s) for s, n in rest)
    return (p0, p1, f0, f0 + ext + 1)


class FW:
    def __init__(self, nc, sync_same_engine=True):
        self.nc = nc
        self.ops = {e: [] for e in ENGS}
        self.sem = {}
        self.cnt = {e: 0 for e in ENGS}
        self.waited = {e: {} for e in ENGS}
        self.acc = {}
        self.sync_same = sync_same_engine
        self._stack = []
        self.n_instr = 0
        self.n_wait = 0
        for e in ENGS:
            self.sem[e] = self._sem("c_" + e)
        self.chans = {}

    def _sem(self, name):
        cm = self.nc.semaphore(name)
        s = cm.__enter__()
        self._stack.append(cm)
        return s

    def chan(self, name):
        if name not in self.chans:
            self.chans[name] = Chan(self._sem("d_" + name))
        return self.chans[name]

    def _deps(self, reads, writes):
        deps = []
        for ap, is_w in [(a, False) for a in reads] + [(a, True) for a in writes]:
            name = ap.tensor.name
            bb = _bbox(ap)
            for rec in self.acc.get(name, ()):
                rb, tok, rw = rec
                if not (is_w or rw):
                    continue
                if rb[0] < bb[1] and bb[0] < rb[1] and rb[2] < bb[3] and bb[2] < rb[3]:
                    deps.append((tok, rw, is_w))
        return deps

    def _record(self, reads, writes, tok):
        for ap in writes:
            name = ap.tensor.name
            bb = _bbox(ap)
            lst = self.acc.setdefault(name, [])
            lst[:] = [r for r in lst if not (bb[0] <= r[0][0] and r[0][1] <= bb[1]
                                             and bb[2] <= r[0][2] and r[0][3] <= bb[3])]
            lst.append([bb, tok, True])
        for ap in reads:
            name = ap.tensor.name
            bb = _bbox(ap)
            lst = self.acc.setdefault(name, [])
            lst[:] = [r for r in lst if not ((not r[2]) and r[1].sem is tok.sem
                                             and bb[0] <= r[0][0] and r[0][1] <= bb[1]
                                             and bb[2] <= r[0][2] and r[0][3] <= bb[3])]
            lst.append([bb, tok, False])

    def _emit_waits(self, e, deps, is_dma=False):
        need = {}
        for tok, rw, is_w in deps:
            if tok.eng == e and not is_dma:
                if e == "pe":
                    continue
                if not self.sync_same:
                    continue
                if not rw:
                    continue
            k = id(tok.sem)
            if k not in need or need[k][1] < tok.val:
                need[k] = (tok.sem, tok.val)
        for k, (sem, val) in need.items():
            if self.waited[e].get(k, 0) >= val:
                continue
            self.waited[e][k] = val
            self.n_wait += 1
            self.ops[e].append(("wait", sem, val))

    def op(self, e, fn, reads, writes):
        deps = self._deps(reads, writes)
        self._emit_waits(e, deps)
        self.cnt[e] += 1
        tok = Tok(self.sem[e], self.cnt[e], e)
        self.ops[e].append(("op", fn, self.sem[e]))
        self._record(reads, writes, tok)
        self.n_instr += 1
        return tok

    def dma(self, q, out, in_, chan, rd=None, wr=None, **kw):
        ch = self.chan(chan)
        r = (list(rd) if isinstance(rd, (list, tuple)) else [rd]) if rd is not None else [in_]
        w = [wr if wr is not None else out]
        deps = self._deps(r, w)
        if ch.total > 0:
            deps.append((Tok(ch.sem, ch.total, None), True, True))
        self._emit_waits(q, deps, is_dma=True)
        ch.total += 16
        tok = Tok(ch.sem, ch.total, None)
        self.ops[q].append(("dma", out, in_, ch.sem, kw))
        self._record(r, w, tok)
        self.n_instr += 1
        return tok

    def collective(self, kind, ins, outs, groups, chan):
        ch = self.chan(chan)
        deps = self._deps(ins, outs)
        self._emit_waits("pool", deps, is_dma=True)
        ch.total += 1
        tok = Tok(ch.sem, ch.total, None)
        self.ops["pool"].append(("cc", kind, ins, outs, groups, ch.sem))
        self._record(ins, outs, tok)
        return tok

    def wait_all(self, e, aps):
        deps = self._deps([], aps)
        self._emit_waits(e, deps, is_dma=True)

    def _replay(self, e, h):
        for item in self.ops[e]:
            k = item[0]
            if k == "wait":
                h.wait_ge(item[1], item[2])
            elif k == "op":
                item[1](h).then_inc(item[2], 1)
            elif k == "dma":
                o = item[1](h) if callable(item[1]) else item[1]
                i = item[2](h) if callable(item[2]) else item[2]
                h.dma_start(out=o, in_=i, **item[4]).then_inc(item[3], 16)
            elif k == "cc":
                _, kind, ins, outs, groups, sem = item
                h.collective_compute(kind, ALU.bypass, replica_groups=groups,
                                     ins=[a.opt() for a in ins],
                                     outs=[a.opt() for a in outs]).then_inc(sem, 1)

    def finish(self):
        nc = self.nc
        with nc.Block() as block:
            @block.tensor
            def _(h):
                self._replay("pe", h)

            @block.scalar
            def _(h):
                self._replay("act", h)

            @block.vector
            def _(h):
                self._replay("dve", h)

            @block.gpsimd
            def _(h):
                self._replay("pool", h)

            @block.sync
            def _(h):
                self._replay("sp", h)
        for cm in reversed(self._stack):
            cm.__exit__(None, None, None)
        self._stack = []


D = 1024
KC = 8
S = 8192
NT = 2048
TT = 512
DFF = 2816
FC = 22
DEPTH = 4
EPS = 1e-6
GROUPS = [[0, 1, 2, 3], [4, 5, 6, 7]]

_SEG_R = {"const": 816, "attn": 4096, "ffn": 8448, "rec": 3072}


def seg_layout(plan):
    keys = [("const", 0)]
    for st in plan:
        keys.append((st[0], st[1]))
    rows = [_SEG_R[k] for k, _ in keys]
    tot = sum(rows) // 4
    acc, split = 0, tot
    for i, r in enumerate(rows):
        if acc + r // 4 > 6400:
            split = acc
            break
        acc += r // 4
    if split == tot:
        split = rows[0] // 4
    return keys, rows, tot, split

BA_SIZE = 43008
FA_SIZE = 8192


class Prog:
    def __init__(self, plan, load_x=True, store_x=True):
        self.plan = plan
        global SEG_KEYS, SEG_ROWS, SEG_INDEX, WSH_ROWS, WSH_SPLIT
        SEG_KEYS, SEG_ROWS, WSH_ROWS, WSH_SPLIT = seg_layout(plan)
        SEG_INDEX = {k: i for i, k in enumerate(SEG_KEYS)}
        nc = self.nc = bass.Bass("TRN2", target_bir_lowering=False)
        self.fw = FW(nc)
        self.es = ExitStack()
        dt = nc.dram_tensor

        def ext(name, shape, dtype=F32):
            return dt(name, shape, dtype, kind="ExternalInput").ap()

        self.d_xT = ext("xT", [D, NT])
        self.d_gains = ext("gains", [128, 128])
        self.d_wsh = [ext("wsh0", [WSH_SPLIT, 1024]), ext("wsh1", [WSH_ROWS - WSH_SPLIT, 1024])]
        self.d_bext = ext("biasext", [2, 2, 1536])
        self.d_jmat = ext("jmat", [128, 128])
        self.d_rwa = ext("r_wa", [2, 128, 512])
        self.d_rwi = ext("r_wi", [2, 128, 512])
        self.d_rsm = ext("r_small", [2, 128, 16])
        self.d_wshb = dt("wsh_bf", [WSH_ROWS, 1024], BF16).ap()
        self.d_wall = [dt("wall%d" % i, [SEG_ROWS[i], 1024], BF16).ap() for i in range(len(SEG_ROWS))]
        self.d_yT = dt("yT", [D, NT], F32, kind="ExternalOutput").ap()
        self.d_qkv_s = dt("qkv_s", [3072, 2048], BF16).ap()
        self.d_qkv_r = dt("qkv_r", [4 * 3072, 2048], BF16).ap()
        self.d_myqkv = dt("myqkv", [3072, 2048], BF16).ap()
        self.d_myrec = dt("myrec", [2048, 2048], BF16).ap()
        self.d_att_s = dt("att_s", [1024, 2048], BF16).ap()
        self.d_att_r = dt("att_r", [4096, 2048], BF16).ap()
        self.d_rec_s = dt("rec_s", [2048, 2048], BF16).ap()
        self.d_rec_r = dt("rec_r", [4 * 2048, 2048], BF16).ap()
        self.d_rec2_s = dt("rec2_s", [1024, 2048], BF16).ap()
        self.d_rec2_r = dt("rec2_r", [4096, 2048], BF16).ap()

        sb = lambda name, shape, dtype: self.es.enter_context(nc.sbuf_tensor(name, shape, dtype))
        self.xT = sb("xT_sb", [128, KC, NT], F32)
        self.gains = sb("gains_sb", [128, 128], F32)
        self.ones_f = sb("ones_f", [128, 128], F32)
        self.cst = sb("cst_sb", [128, 384], BF16)
        self.amask = sb("amask_sb", [128, 8, 512], BF16)
        self.bmask = sb("bmask_sb", [128, 4, 512], BF16)
        self.BA = sb("BA", [128, BA_SIZE], BF16)
        self.FA = sb("FA", [128, FA_SIZE], F32)
        self.ps = [self.es.enter_context(nc.psum_tensor("ps%d" % i, [128, 512], F32)) for i in range(8)]
        self.ones_b = self.cst[:, 0:128]
        self.tri_b = self.cst[:, 128:256]
        self.sl_b = self.cst[:, 256:384]
        self.evac_rr = 0

        fw = self.fw
        fw.dma("sp", self.gains[:], self.d_gains, "c0")
        self.jmat = sb("jmat_sb", [128, 128], F32)
        fw.dma("sp", self.jmat[:], self.d_jmat, "c4")
        self.wstage = sb("wstage", [128, 2 * 2048], BF16)
        self.sqb = sb("sqb", [128, 1024], BF16)
        self.ones_bb = sb("ones_bb", [128, 128], BF16)
        fw.op("dve", lambda h: h.memset(self.ones_bb[:], 1.0), [], [self.ones_bb[:]])
        self.seg_done = 0
        self.wst_i = 0
        self.step_i = 0
        self.prefetch_weights(2)
        cseg = self.d_wall[0]
        fw.dma("pool", self.cst[:], self.wtile(cseg, 0, 384), "c1")
        fw.dma("pool", self.amask[:], self.wtile(cseg, 128 * 384, 4096).rearrange("p (a b) -> p a b", a=8), "c2")
        fw.dma("pool", self.bmask[:], self.wtile(cseg, 128 * (384 + 4096), 2048).rearrange("p (a b) -> p a b", a=4), "c3")
        fw.op("dve", lambda h: h.memset(self.ones_f[:], 1.0), [], [self.ones_f[:]])
        if load_x:
            for c in range(KC):
                fw.dma("sp", self.xT[:, c, :], self.d_xT[c * 128:(c + 1) * 128, :], "x%d" % c)
        for si_, step in enumerate(plan):
            self.step_i = si_
            kind = step[0]
            if kind == "attn":
                self.attn_layer(step[1], step[2])
            elif kind == "rec":
                self.rec_layer(step[1], step[2])
            elif kind == "ffn":
                self.ffn_layer(step[1])
            else:
                raise ValueError(kind)
        if store_x:
            for c in range(KC):
                fw.dma("sp", self.d_yT[c * 128:(c + 1) * 128, :], self.xT[:, c, :], "x%d" % c)
            fw.wait_all("sp", [self.d_yT])
        fw.finish()
        self.es.close()

    def mm(self, out, lhsT, rhs, start, stop, skip=False):
        rd = [lhsT, rhs] + ([] if start else [out])
        return self.fw.op("pe", lambda h: h.matmul(out, lhsT, rhs, start=start, stop=stop,
                                                   skip_group_check=skip), rd, [out])

    def act(self, out, in_, func, scale=None, bias=None, eng="act"):
        kw = {}
        rd = [in_]
        if scale is not None:
            kw["scale"] = scale
            if not isinstance(scale, (int, float)):
                rd.append(scale)
        if bias is not None:
            kw["bias"] = bias
            if not isinstance(bias, (int, float)):
                rd.append(bias)
        return self.fw.op("act", lambda h: h.activation(out, in_, func, **kw), rd, [out])

    def tt(self, out, in0, in1, op, eng="dve"):
        return self.fw.op(eng, lambda h: h.tensor_tensor(out, in0, in1, op), [in0, in1], [out])

    def stt(self, out, in0, scalar, in1, op0, op1):
        rd = [in0, in1] + ([] if isinstance(scalar, (int, float)) else [scalar])
        return self.fw.op("dve", lambda h: h.scalar_tensor_tensor(out, in0, scalar, in1, op0, op1), rd, [out])

    def ts(self, out, in0, s1, s2, op0, op1=None, eng="dve"):
        rd = [in0] + [s for s in (s1, s2) if s is not None and not isinstance(s, (int, float))]
        if op1 is None:
            return self.fw.op(eng, lambda h: h.tensor_scalar(out, in0, s1, None, op0), rd, [out])
        return self.fw.op(eng, lambda h: h.tensor_scalar(out, in0, s1, s2, op0, op1), rd, [out])

    def copy(self, out, in_, eng="dve"):
        if eng == "act":
            return self.act(out, in_, AF.Copy)
        return self.fw.op(eng, lambda h: h.tensor_copy(out, in_), [in_], [out])

    def evac(self, out, in_):
        self.evac_rr ^= 1
        return self.copy(out, in_, eng="act" if self.evac_rr else "dve")

    def recip(self, out, in_):
        return self.fw.op("dve", lambda h: h.reciprocal(out, in_), [in_], [out])

    def gidx(self, kind, layer):
        return (kind * 4 + layer) * 8

    def ba(self, off, n):
        return self.BA[:, off:off + n]

    def fa(self, off, n):
        return self.FA[:, off:off + n]

    def dyn_rank(self, h):
        if getattr(self, "_rank_val", None) is None:
            self._rank_val = h.partition_id() % 4
            self._rank_col = self._rank_val * NT
        return self._rank_val

    def dyn_col(self, h):
        self.dyn_rank(h)
        return self._rank_col

    def dyn_val(self, h, mult, add):
        if not hasattr(self, "_dyn_cache"):
            self._dyn_cache = {}
        key = (mult, add)
        if key not in self._dyn_cache:
            self._dyn_cache[key] = self.dyn_rank(h) * mult + add
        return self._dyn_cache[key]

    def gather(self, send, recv, chunk):
        R = int(send.shape[0])
        k = 0
        while k < R:
            c = min(chunk, R - k)
            self.fw.collective("AllGather", [send[k:k + c, :]], [recv[4 * k:4 * k + 4 * c, :]], GROUPS, "cc")
            k += c

    def fetch_rows(self, gathered, dst, nblk, kind, chan):
        def src(h):
            v = gathered.rearrange("(h k r p) c -> h k r p c", h=4, k=nblk, r=4)
            w = v[bass.ds(self.dyn_rank(h), 1), kind]
            return w.rearrange("o r p c -> p (o r) c")
        rd = [gathered[(hh * nblk + kind) * 512:(hh * nblk + kind + 1) * 512, :] for hh in range(4)]
        self.fw.dma("sp", dst, src, chan, rd=rd)

    def fetch_cols(self, gathered, aT):
        for s_ in range(2):
            def src(h, s_=s_):
                v = gathered.rearrange("(r s h p) c -> r s h p c", r=4, s=2, h=4)
                w = v[bass.ds(self.dyn_rank(h), 1), s_]
                return w.rearrange("o h p c -> p (o h) c")
            dst = aT.rearrange("p (h s) c -> p h s c", s=2)[:, :, s_, :]
            self.fw.dma("sp", dst, src, "fc%d" % s_, rd=gathered)

    def wtile(self, seg, off, width, k=None):
        ap = bass.AP(tensor=seg.tensor, offset=int(seg.offset) + off, ap=[[width, 128], [1, width]])
        if k is not None:
            ap = ap.rearrange("p (k m) -> p k m", k=k)
        return ap

    def prefetch_weights(self, upto):
        fw = self.fw
        NB = 2
        upto = min(upto, len(SEG_ROWS))
        while self.seg_done < upto:
            si = self.seg_done
            o = sum(SEG_ROWS[:si]) // 4
            end = o + SEG_ROWS[si] // 4
            row = o
            while row < end:
                n = min(256, end - row)
                npart = n // 2
                i = self.wst_i
                self.wst_i += 1
                st = self.wstage[0:npart, 2048 * (i % NB):2048 * (i % NB + 1)].rearrange("p (a c) -> p a c", a=2)
                if row < WSH_SPLIT:
                    src = self.d_wsh[0][row:row + n, :].rearrange("(p a) c -> p a c", a=2)
                else:
                    src = self.d_wsh[1][row - WSH_SPLIT:row - WSH_SPLIT + n, :].rearrange("(p a) c -> p a c", a=2)
                dst = self.d_wshb[row:row + n, :].rearrange("(p a) c -> p a c", a=2)
                fw.dma("pool", st, src, "wp%d" % (i % NB))
                fw.dma("sp", dst, st, "ws%d" % (i % NB))
                row += n
            self.gather(self.d_wshb[o:end, :], self.d_wall[si], 256)
            self.seg_done += 1

    def seg_of(self, kind, idx):
        return self.d_wall[SEG_INDEX[(kind, idx)]]

    def prenorm(self, g0, tok0, ntiles, hT, hoff):
        sq = [self.sqb[:, 0:512], self.sqb[:, 512:1024]]
        rs = self.fa(1024, 512)
        ss = self.ps[7]
        for ti in range(ntiles):
            t0 = tok0 + ti * TT
            for c in range(KC):
                s = sq[c % 2]
                self.act(s, self.xT[:, c, t0:t0 + TT], AF.Square)
                self.mm(ss[:], self.ones_bb[:], s, c == 0, c == KC - 1)
            self.act(rs, ss[:], AF.Sqrt, scale=1.0 / D, bias=self.eps_ap)
            self.recip(rs, rs)
            for c in range(KC):
                self.stt(hT[:, c, hoff + ti * TT: hoff + (ti + 1) * TT], self.xT[:, c, t0:t0 + TT],
                         self.gains[:, g0 + c:g0 + c + 1], rs, ALU.mult, ALU.mult)

    def postnorm_residual(self, g0, t0, produce):
        mbuf = self.FA[:, 1536:1536 + 4096].rearrange("p (m t) -> p m t", m=8)
        sq = [self.sqb[:, 0:512], self.sqb[:, 512:1024]]
        rs = self.fa(1024, 512)
        tb = [self.fa(5632, 512), self.fa(6144, 512)]
        ss = self.ps[7]
        pend = None
        for m in range(KC):
            pm = self.ps[4 + (m % 3)]
            produce(m, pm)
            if pend is not None:
                pend()
            self.copy(mbuf[:, m, :], pm[:], eng="act")
            s = sq[m % 2]
            self.tt(s, mbuf[:, m, :], mbuf[:, m, :], ALU.mult)

            def _p(s=s, m=m):
                self.mm(ss[:], self.ones_bb[:], s, m == 0, m == KC - 1)
            pend = _p
        pend()
        self.act(rs, ss[:], AF.Sqrt, scale=1.0 / D, bias=self.eps_ap)
        self.recip(rs, rs)
        for m in range(KC):
            t = tb[m % 2]
            self.stt(t, mbuf[:, m, :], self.gains[:, g0 + m:g0 + m + 1], rs, ALU.mult, ALU.mult)
            self.tt(self.xT[:, m, t0:t0 + TT], self.xT[:, m, t0:t0 + TT], t, ALU.add, eng="pool")

    @property
    def eps_ap(self):
        if not hasattr(self, "_eps"):
            t = self.es.enter_context(self.nc.sbuf_tensor("eps_sb", [128, 1], F32))
            self.fw.op("dve", lambda h: h.memset(t[:], EPS), [], [t[:]])
            self._eps = t
        return self._eps[:]

    def ffn_layer(self, L):
        fw = self.fw
        self.prefetch_weights(self.step_i + 4)
        hTh = self.BA[:, 0:8192].rearrange("p (c t) -> p c t", c=8)
        hid = self.BA[:, 8192:8192 + 22528].rearrange("p (f t) -> p f t", f=FC)
        wgu = [[self.BA[:, 30720 + (2 * i + k) * 1024: 30720 + (2 * i + k + 1) * 1024].rearrange("p (k m) -> p k m", k=8)
                for k in range(2)] for i in range(3)]
        wd = [self.BA[:, 36864 + i * DFF: 36864 + (i + 1) * DFF].rearrange("p (k m) -> p k m", k=FC) for i in range(2)]
        sg = [self.fa(6656, 512), self.fa(7168, 512)]
        g_pre = self.gidx(2, L)
        g_post = self.gidx(3, L)
        seg = self.seg_of("ffn", L)
        wdi = 0
        for th in range(2):
            tok0 = th * 1024
            self.prenorm(g_pre, tok0, 2, hTh, 0)
            cnt = 0
            for f in range(FC):
                wg_t, wu_t = wgu[f % 3]
                fw.dma("pool", wg_t, self.wtile(seg, f * 131072, 1024, 8), "wg%d" % (f % 3))
                fw.dma("pool", wu_t, self.wtile(seg, (FC + f) * 131072, 1024, 8), "wu%d" % (f % 3))
                for tt in range(2):
                    pg = self.ps[2 * (cnt % 2)]
                    pu = self.ps[2 * (cnt % 2) + 1]
                    cnt += 1
                    rhs = lambda kc: hTh[:, kc, tt * TT:(tt + 1) * TT]
                    for kc in range(KC):
                        self.mm(pg[:], wg_t[:, kc, :], rhs(kc), kc == 0, kc == KC - 1)
                    for kc in range(KC):
                        self.mm(pu[:], wu_t[:, kc, :], rhs(kc), kc == 0, kc == KC - 1)
                    s = sg[cnt % 2]
                    self.act(s, pg[:], AF.Silu)
                    self.tt(hid[:, f, tt * TT:(tt + 1) * TT], s, pu[:], ALU.mult)
            for tt in range(2):
                def produce(m, pm, tt=tt):
                    nonlocal wdi
                    wt = wd[wdi % 2]
                    fw.dma("sp", wt, self.wtile(seg, 2 * FC * 131072 + m * 128 * DFF, DFF, FC), "wd%d" % (wdi % 2))
                    wdi += 1
                    for kc in range(FC):
                        self.mm(pm[:], wt[:, kc, :], hid[:, kc, tt * TT:(tt + 1) * TT], kc == 0, kc == FC - 1)
                self.postnorm_residual(g_post, tok0 + tt * TT, produce)

    def attn_layer(self, j, L):
        fw = self.fw
        BA = self.BA
        seg = self.seg_of("attn", j)
        hT = BA[:, 0:16384].rearrange("p (c t) -> p c t", c=8)
        wqk = [BA[:, 16384 + i * 1024:16384 + (i + 1) * 1024].rearrange("p (k m) -> p k m", k=8) for i in range(3)]
        wv = [BA[:, 19456 + i * 4096:19456 + (i + 1) * 4096].rearrange("p (k m) -> p k m", k=8) for i in range(2)]
        ost = [BA[:, 27648 + i * 2048:27648 + (i + 1) * 2048] for i in range(2)]
        vst = BA[:, 31744:31744 + 8192].rearrange("p (t f) -> p t f", t=16)
        self.prenorm(self.gidx(0, L), 0, 4, hT, 0)
        def cc_chunk(hp, kind):
            a = (hp * 6 + kind) * 128
            self.gather(self.d_qkv_s[a:a + 128, :], self.d_qkv_r[4 * a:4 * a + 512, :], 128)

        vst2 = BA[:, 31744:31744 + 8192].rearrange("p (g t f) -> p g t f", g=4, t=16)
        for vb in range(2):
            wt = wv[vb]
            fw.dma("pool", wt, self.wtile(seg, 16 * 131072 + vb * 524288, 4096, 8), "wv%d" % vb)
            for tk in range(16):
                pv = self.ps[4 + tk % 3]
                for kc in range(KC):
                    self.mm(pv[:], hT[:, kc, tk * 128:(tk + 1) * 128], wt[:, kc, :], kc == 0, kc == KC - 1)
                self.evac(vst2[:, :, tk, :], pv[:].rearrange("p (g f) -> p g f", g=4))
            for u in range(2):
                hp = 2 * vb + u
                for which in range(2):
                    r0 = hp * 768 + 512 + which * 128
                    fw.dma("sp", self.d_qkv_s[r0:r0 + 128, :].rearrange("p (t f) -> p t f", t=16),
                           vst2[:, 2 * u + which, :, :], "vs%d" % which)
        cnt = 0
        pending = [(hp, 4) for hp in range(4)]
        order = [(kind, hp) for kind in range(4) for hp in range(4)]
        for i, (kind, hp) in enumerate(order):
            mt = hp * 4 + kind
            wt = wqk[i % 3]
            fw.dma("pool", wt, self.wtile(seg, mt * 131072, 1024, 8), "wqk%d" % (i % 3))
            o = ost[i % 2]
            for tt in range(4):
                pz = self.ps[cnt % 4]
                cnt += 1
                for kc in range(KC):
                    self.mm(pz[:], wt[:, kc, :], hT[:, kc, tt * TT:(tt + 1) * TT], kc == 0, kc == KC - 1)
                self.evac(o[:, tt * TT:(tt + 1) * TT], pz[:])
            r0 = hp * 768 + kind * 128
            fw.dma("sp", self.d_qkv_s[r0:r0 + 128, :], o, "qs%d" % (i % 2))
            while pending:
                cc_chunk(*pending.pop(0))
            pending.append((hp, kind))
        while pending:
            cc_chunk(*pending.pop(0))
        for hp in range(4):
            cc_chunk(hp, 5)
        self.prefetch_weights(self.step_i + 3)
        if ATT_STOP == 1:
            fw.wait_all("sp", [self.d_qkv_r])
            return

        Q = BA[:, 0:8192]
        K = BA[:, 8192:16384]
        V = BA[:, 16384:24576].rearrange("p (t f) -> p t f", t=64)
        EB = BA[:, 24576:32768].rearrange("p (e q) -> p e q", e=16)
        pT = [BA[:, 32768 + i * 512:32768 + (i + 1) * 512] for i in range(3)]
        ob = [BA[:, 34304 + i * 512:34304 + (i + 1) * 512] for i in range(2)]
        spb = [BA[:, 35328 + i * 512:35328 + (i + 1) * 512] for i in range(3)]
        wb = [BA[:, 36864 + i * 512:36864 + (i + 1) * 512] for i in range(3)]
        ef = [self.fa(i * 512, 512) for i in range(3)]
        ecf = [self.fa(1536 + i * 512, 512) for i in range(3)]
        stg = [self.fa(3072 + i * 512, 512) for i in range(2)]
        rden = self.fa(4096, 512)

        def load_qkv(which):
            self.fetch_rows(self.d_qkv_r, Q.rearrange("p (r t) -> p r t", r=4), 6, 2 * which, "lq0")
            self.fetch_rows(self.d_qkv_r, K.rearrange("p (r t) -> p r t", r=4), 6, 2 * which + 1, "lq1")
            self.fetch_rows(self.d_qkv_r, BA[:, 16384:24576].rearrange("p (r t) -> p r t", r=4), 6, 4 + which, "lq2")

        load_qkv(0)
        for u in range(2):
            for jj in range(8):
                s = stg[jj % 2]
                s2 = ef[jj % 2]
                src = bass.AP(tensor=self.d_bext.tensor, offset=(j * 2 + u) * 1536 + 896 - 128 * jj,
                              ap=[[1, 128], [1, 512]])
                fw.dma("sp", s, src, "tb%d" % (jj % 2))
                pz = self.ps[jj % 2]
                self.mm(pz[:], self.jmat[:], s, True, True)
                self.act(s2, pz[:], AF.Exp)
                self.tt(EB[:, u * 8 + jj, :], s2, self.amask[:, jj, :], ALU.mult)
        efA = [self.fa(i * 512, 512) for i in range(4)]
        pT4 = [BA[:, 35328 + i * 512:35328 + (i + 1) * 512] for i in range(4)]
        blocks = []
        for qi in range(16):
            jts = [(jj, 4 * qi - 4 + jj) for jj in range(8) if 4 * qi - 4 + jj >= 0]
            for n, (jj, jt) in enumerate(jts):
                blocks.append((qi, jj, jt, n == 0, n == len(jts) - 1))

        def st1(i, u):
            qi, jj, jt, first, last = blocks[i]
            pr = slice(64 * u, 64 * u + 64)
            b = (2 * i + u) % 4
            self.mm(self.ps[b][:], K[pr, jt * 128:(jt + 1) * 128], Q[pr, qi * TT:(qi + 1) * TT], True, True)
            self.act(efA[b], self.ps[b][:], AF.Exp, scale=0.125)

        def st2(i, u):
            qi, jj, jt, first, last = blocks[i]
            b = (2 * i + u) % 4
            po = self.ps[4 + (qi % 2)]
            pden = self.ps[6 + (qi % 2)]
            hs_ = slice(64 * u, 64 * u + 64)
            self.tt(pT4[b], efA[b], EB[:, u * 8 + jj, :], ALU.mult)
            self.mm(po[hs_, :], V[:, jt, 64 * u:64 * u + 64], pT4[b], first, last)
            self.mm(pden[hs_, :], self.ones_b[:, 0:64], pT4[b], first, last)

        for u in range(2):
            st1(0, u)
        for i in range(len(blocks)):
            if i + 1 < len(blocks):
                for u in range(2):
                    st1(i + 1, u)
            for u in range(2):
                st2(i, u)
            qi, jj, jt, first, last = blocks[i]
            if last:
                po = self.ps[4 + (qi % 2)]
                pden = self.ps[6 + (qi % 2)]
                self.recip(rden, pden[:])
                o = ob[qi % 2]
                self.tt(o, po[:], rden, ALU.mult)
                ar0 = (qi // 4) * 256
                fw.dma("sp", self.d_att_s[ar0:ar0 + 128, (qi % 4) * TT:(qi % 4 + 1) * TT], o, "ao%d" % (qi % 2))
        for r_ in range(4):
            a = r_ * 256
            self.gather(self.d_att_s[a:a + 128, :], self.d_att_r[4 * a:4 * a + 512, :], 128)
        if ATT_STOP == 2:
            fw.wait_all("sp", [self.d_att_s])
            return
        load_qkv(1)
        NBUF = 4
        ef = [self.fa(i * 512, 512) for i in range(NBUF)]
        ecf = [self.fa(2048 + i * 512, 512) for i in range(NBUF)]
        spb = [BA[:, 35328 + i * 512:35328 + (i + 1) * 512] for i in range(NBUF)]
        wb = [BA[:, 37376 + i * 512:37376 + (i + 1) * 512] for i in range(NBUF)]
        ob2 = [BA[:, 39424 + i * 512:39424 + (i + 1) * 512] for i in range(2)]
        cnt = 0
        for qi in range(16):
            jts = list(range(4 * qi + 3, -1, -1))
            nb = len(jts)
            po = self.ps[6 + (qi % 2)]
            Ts = [self.ps[4], self.ps[5]]
            slots = {}

            def stageA(n, u):
                nonlocal cnt
                jt = jts[n]
                pr = slice(64 * u, 64 * u + 64)
                b = cnt % NBUF
                cnt += 1
                pz = self.ps[b]
                slots[(n, u)] = b
                self.mm(pz[:], K[pr, jt * 128:(jt + 1) * 128], Q[pr, qi * TT:(qi + 1) * TT], True, True)
                self.act(ef[b], pz[:], AF.Exp, scale=0.125)
                dj = jt - 4 * qi
                if dj >= 0:
                    self.tt(ef[b], ef[b], self.bmask[:, dj, :], ALU.mult)

            def stageA2(n, u):
                b = slots[(n, u)]
                self.act(spb[b], ef[b], AF.Ln, bias=1.0)

            def stageB1(n, u):
                b = slots[(n, u)]
                self.mm(Ts[u][:], self.tri_b, spb[b], n == 0, True, skip=(n != 0))
                self.act(ecf[b], Ts[u][:], AF.Exp, scale=-1.0)

            def stageB2(n, u):
                b = slots[(n, u)]
                jt = jts[n]
                self.mm(Ts[u][:], self.sl_b, spb[b], False, True, skip=True)
                self.tt(wb[b], ef[b], ecf[b], ALU.mult)
                self.mm(po[64 * u:64 * u + 64, :], V[:, jt, 64 * u:64 * u + 64], wb[b], n == 0, n == nb - 1)

            for u in range(2):
                stageA(0, u)
            for u in range(2):
                stageA2(0, u)
            for n in range(nb):
                for u in range(2):
                    stageB1(n, u)
                if n + 1 < nb:
                    for u in range(2):
                        stageA(n + 1, u)
                    for u in range(2):
                        stageA2(n + 1, u)
                for u in range(2):
                    stageB2(n, u)
            o = ob2[qi % 2]
            self.evac(o[:], po[:])
            ar0 = (qi // 4) * 256 + 128
            fw.dma("sp", self.d_att_s[ar0:ar0 + 128, (qi % 4) * TT:(qi % 4 + 1) * TT], o[:], "ao%d" % (qi % 2))
            if qi % 4 == 3:
                self.gather(self.d_att_s[ar0:ar0 + 128, :], self.d_att_r[4 * ar0:4 * ar0 + 512, :], 128)
        self.prefetch_weights(self.step_i + 4)
        if ATT_STOP == 3:
            fw.wait_all("sp", [self.d_att_r])
            return

        aT = BA[:, 0:16384].rearrange("p (c t) -> p c t", c=8)
        wo = [BA[:, 16384 + m * 1024:16384 + (m + 1) * 1024].rearrange("p (k m) -> p k m", k=8) for m in range(8)]
        self.fetch_cols(self.d_att_r, aT)
        for m in range(8):
            fw.dma("pool", wo[m], self.wtile(seg, (16 + 8 + m) * 131072, 1024, 8), "wo%d" % (m % 4))
        g_post = self.gidx(1, L)
        for tt in range(4):
            def produce(m, pm, tt=tt):
                for kc in range(KC):
                    self.mm(pm[:], wo[m][:, kc, :], aT[:, kc, tt * TT:(tt + 1) * TT], kc == 0, kc == KC - 1)
            self.postnorm_residual(g_post, tt * TT, produce)

    def rec_layer(self, j, L):
        fw = self.fw
        BA = self.BA
        seg = self.seg_of("rec", j)
        hT = BA[:, 0:16384].rearrange("p (c t) -> p c t", c=8)
        win = [BA[:, 16384 + i * 1024:16384 + (i + 1) * 1024].rearrange("p (k m) -> p k m", k=8) for i in range(3)]
        ost = [BA[:, 27648 + i * 2048:27648 + (i + 1) * 2048] for i in range(2)]
        self.prenorm(self.gidx(0, L), 0, 4, hT, 0)
        cnt = 0
        for mt in range(16):
            n, kind, half = mt // 4, (mt // 2) % 2, mt % 2
            wt = win[mt % 3]
            fw.dma("pool", wt, self.wtile(seg, mt * 131072, 1024, 8), "wqk%d" % (mt % 3))
            o = ost[mt % 2]
            for tt in range(4):
                pz = self.ps[cnt % 4]
                cnt += 1
                for kc in range(KC):
                    self.mm(pz[:], wt[:, kc, :], hT[:, kc, tt * TT:(tt + 1) * TT], kc == 0, kc == KC - 1)
                if kind == 0:
                    self.act(o[:, tt * TT:(tt + 1) * TT], pz[:], AF.Gelu_apprx_tanh)
                else:
                    self.copy(o[:, tt * TT:(tt + 1) * TT], pz[:], eng="dve")
            r0 = n * 512 + kind * 256 + half * 128
            fw.dma("sp", self.d_rec_s[r0:r0 + 128, :], o, "qs%d" % (mt % 2))
            if mt >= 1:
                a = rprev
                self.gather(self.d_rec_s[a:a + 128, :], self.d_rec_r[4 * a:4 * a + 512, :], 128)
            rprev = r0
        self.gather(self.d_rec_s[rprev:rprev + 128, :], self.d_rec_r[4 * rprev:4 * rprev + 512, :], 128)
        self.prefetch_weights(self.step_i + 3)

        G = BA[:, 0:16384].rearrange("p (c t) -> p c t", c=2)
        XR = BA[:, 16384:16384 + 16400].rearrange("p (c t) -> p c t", c=2)
        PAD = 8
        xcb = [BA[:, 32784 + i * 1024:32784 + (i + 1) * 1024].rearrange("p (c t) -> p c t", c=2) for i in range(2)]
        wa = BA[:, 34832:34832 + 512].rearrange("p (k m) -> p k m", k=2)
        wi = BA[:, 35344:35344 + 512].rearrange("p (k m) -> p k m", k=2)
        ob = [BA[:, 35856 + i * 512:35856 + (i + 1) * 512] for i in range(4)]
        xc = [self.FA[:, i * 1024:(i + 1) * 1024].rearrange("p (c t) -> p c t", c=2) for i in range(2)]
        rb = [self.fa(2048 + i * 512, 512) for i in range(2)]
        ib = [self.fa(3072 + i * 512, 512) for i in range(2)]
        a2b = [self.fa(4096 + i * 512, 512) for i in range(2)]
        hsb = [[self.fa(5120 + (2 * hf + i) * 512, 512) for i in range(2)] for hf in range(2)]
        sm = self.fa(7168, 16)
        cv = self.fa(7184, 4)
        fw.dma("sp", sm, self.d_rsm[j], "c0")
        fw.dma("pool", wa, self.d_rwa[j].rearrange("p (k m) -> p k m", k=2), "c1")
        fw.dma("pool", wi, self.d_rwi[j].rearrange("p (k m) -> p k m", k=2), "c2")
        self.act(cv[:, 0:2], sm[:, 14:16], AF.Exp, scale=-1.0)
        self.act(cv[:, 0:2], cv[:, 0:2], AF.Ln, bias=1.0)
        self.ts(cv[:, 2:4], cv[:, 0:2], -16.0, None, ALU.mult)
        self.ts(cv[:, 0:2], cv[:, 0:2], -8.0, None, ALU.mult)
        for hf in range(2):
            fw.op("dve", lambda h, hf=hf: h.memset(XR[:, hf, 0:PAD], 0.0), [], [XR[:, hf, 0:PAD]])
        for kind in range(2):
            for hf in range(2):
                dstt = (G[:, hf, :] if kind == 0 else XR[:, hf, PAD:PAD + S]).rearrange("p (r t) -> p r t", r=4)
                self.fetch_rows(self.d_rec_r, dstt, 4, kind * 2 + hf, "lq%d" % (2 * kind + hf))
        for ti in range(16):
            t0 = ti * TT
            x_c = xc[ti % 2]
            x_b = xcb[ti % 2]
            base = PAD + t0 - 3
            for hf in range(2):
                self.ts(x_c[:, hf, :], XR[:, hf, base:base + TT], sm[:, 4 * hf:4 * hf + 1], sm[:, 8 + hf:9 + hf],
                        ALU.mult, ALU.add)
            for jw in range(1, 4):
                for hf in range(2):
                    self.stt(x_c[:, hf, :], XR[:, hf, base + jw:base + jw + TT], sm[:, 4 * hf + jw:4 * hf + jw + 1],
                             x_c[:, hf, :], ALU.mult, ALU.add)
            for hf in range(2):
                self.copy(x_b[:, hf, :], x_c[:, hf, :], eng="pool")
            for jt in range(2):
                for ic in range(2):
                    self.mm(self.ps[2 * jt][:], wa[:, ic, jt * 128:(jt + 1) * 128], x_b[:, ic, :], ic == 0, ic == 1)
                for ic in range(2):
                    self.mm(self.ps[2 * jt + 1][:], wi[:, ic, jt * 128:(jt + 1) * 128], x_b[:, ic, :], ic == 0, ic == 1)
            for jt in range(2):
                self.act(rb[jt], self.ps[2 * jt][:], AF.Sigmoid, bias=sm[:, 10 + jt:11 + jt])
            for jt in range(2):
                self.act(ib[jt], self.ps[2 * jt + 1][:], AF.Sigmoid, bias=sm[:, 12 + jt:13 + jt])
            for jt in range(2):
                self.act(a2b[jt], rb[jt], AF.Exp, scale=cv[:, 2 + jt:3 + jt])
            for jt in range(2):
                self.act(rb[jt], rb[jt], AF.Exp, scale=cv[:, jt:jt + 1])
            for jt in range(2):
                self.act(a2b[jt], a2b[jt], AF.Sqrt, scale=-1.0, bias=1.0)
            for jt in range(2):
                self.tt(ib[jt], ib[jt], x_c[:, jt, :], ALU.mult)
            for jt in range(2):
                self.tt(ib[jt], ib[jt], a2b[jt], ALU.mult)
            for jt in range(2):
                r_, i_ = rb[jt], ib[jt]
                hs = hsb[jt][ti % 2]
                hprev = hsb[jt][(ti + 1) % 2]
                init = 0.0 if ti == 0 else hprev[:, TT - 1:TT]
                rd = [r_, i_] + ([] if ti == 0 else [init])
                fw.op("dve", lambda h, hs=hs, r_=r_, i_=i_, init=init: h.tensor_tensor_scan(
                    hs, r_, i_, init, ALU.mult, ALU.add), rd, [hs])
            for jt in range(2):
                hs = hsb[jt][ti % 2]
                o = ob[(2 * ti + jt) % 4]
                self.tt(o, hs, G[:, jt, t0:t0 + TT], ALU.mult, eng="pool")
                rr0 = (ti // 4) * 256 + jt * 128
                fw.dma("sp", self.d_rec2_s[rr0:rr0 + 128, (ti % 4) * TT:(ti % 4 + 1) * TT], o, "ao%d" % (jt))
            if ti % 4 == 3:
                g0 = (ti // 4) * 256
                self.gather(self.d_rec2_s[g0:g0 + 256, :], self.d_rec2_r[4 * g0:4 * g0 + 1024, :], 128)
        self.prefetch_weights(self.step_i + 4)

        aT = BA[:, 0:16384].rearrange("p (c t) -> p c t", c=8)
        wo = [BA[:, 16384 + m * 1024:16384 + (m + 1) * 1024].rearrange("p (k m) -> p k m", k=8) for m in range(8)]
        self.fetch_cols(self.d_rec2_r, aT)
        for m in range(8):
            fw.dma("pool", wo[m], self.wtile(seg, (16 + m) * 131072, 1024, 8), "wo%d" % (m % 4))
        g_post = self.gidx(1, L)
        for tt in range(4):
            def produce(m, pm, tt=tt):
                for kc in range(KC):
                    self.mm(pm[:], wo[m][:, kc, :], aT[:, kc, tt * TT:(tt + 1) * TT], kc == 0, kc == KC - 1)
            self.postnorm_residual(g_post, tt * TT, produce)


def _tile_w(w, cols):
    K = w.shape[0]
    kc = K // 128
    out = []
    for c in cols:
        blk = w[:, c]
        m = blk.shape[1]
        out.append(blk.reshape(kc, 128, m).transpose(1, 0, 2).reshape(128, kc * m))
    return np.ascontiguousarray(np.stack(out, 0))


def _consts():
    k = np.arange(128)[:, None]
    q = np.arange(512)[None, :]
    cst = np.zeros((128, 384), np.float32)
    cst[:, 0:128] = 1.0
    kk = np.arange(128)[:, None]
    kp = np.arange(128)[None, :]
    cst[:, 128:256] = (kk >= kp).astype(np.float32)
    cst[:, 256:384] = (kk < kp).astype(np.float32)
    am = np.zeros((128, 8, 512), np.float32)
    for jj in range(8):
        kc = 2 * jj + k // 64
        qc = q // 64
        am[:, jj, :] = ((kc >= qc) & (kc <= qc + 8)).astype(np.float32)
    bm = np.zeros((128, 4, 512), np.float32)
    for dj in range(4):
        bm[:, dj, :] = ((k + 128 * dj) < q).astype(np.float32)
    return cst, am.reshape(128, 4096), bm.reshape(128, 2048)


def _toeplitz_idx():
    k = np.arange(128)[:, None]
    q = np.arange(512)[None, :]
    idx = np.zeros((8, 128, 512), np.int64)
    for jj in range(8):
        idx[jj] = np.clip(512 - 128 * jj + q - k, -256, 256) + 256
    return idx


def prepare_inputs(inp, plan=None):
    plan = FULL_PLAN if plan is None else plan
    SEG_KEYS, SEG_ROWS, WSH_ROWS, WSH_SPLIT = seg_layout(plan)
    f = lambda a: np.asarray(a, dtype=np.float32)
    x = f(inp["x"])
    shared = {}
    G = np.stack([f(inp["norm_mix_pre"]), f(inp["norm_mix_post"]), f(inp["norm_ffn_pre"]), f(inp["norm_ffn_post"])], 0)
    shared["gains"] = np.ascontiguousarray(G.reshape(4, 4, 8, 128).transpose(3, 0, 1, 2).reshape(128, 128))
    shared["jmat"] = np.ascontiguousarray(np.eye(128, dtype=np.float32)[::-1])
    cst, am, bm = _consts()
    ar = np.arange(128)
    segs = {}
    segs[("const", 0)] = np.concatenate([cst.ravel(), am.ravel(), bm.ravel()])
    w_in = f(inp["attn_w_in"])
    w_out = f(inp["attn_w_out"])
    kind_base = [0, 512, 1536, 2048]
    cols = [kind_base[kind] + 128 * hp + ar for hp in range(4) for kind in range(4)]
    vcols = []
    for vb in range(2):
        vcols.append(np.concatenate([1024 + 128 * (2 * vb) + ar, 2560 + 128 * (2 * vb) + ar,
                                     1024 + 128 * (2 * vb + 1) + ar, 2560 + 128 * (2 * vb + 1) + ar]))
    rows = np.concatenate([np.concatenate([128 * hp + ar, 512 + 128 * hp + ar]) for hp in range(4)])
    mcols = [128 * m + ar for m in range(8)]
    for j in range(2):
        if ("attn", j) not in SEG_KEYS:
            continue
        segs[("attn", j)] = np.concatenate([_tile_w(w_in[j], cols).ravel(), _tile_w(w_in[j], vcols).ravel(),
                                            _tile_w(w_out[j][rows, :], mcols).ravel()])
    rw_in = f(inp["rg_w_in"])
    rw_out = f(inp["rg_w_out"])
    rcols = [kind * 1024 + n * 256 + half * 128 + ar for n in range(4) for kind in range(2) for half in range(2)]
    for j in range(2):
        if ("rec", j) not in SEG_KEYS:
            continue
        segs[("rec", j)] = np.concatenate([_tile_w(rw_in[j], rcols).ravel(), _tile_w(rw_out[j], mcols).ravel()])
    wg, wu, wd = f(inp["ffn_w_gate"]), f(inp["ffn_w_up"]), f(inp["ffn_w_down"])
    fcols = [128 * t + ar for t in range(FC)]
    for l in range(4):
        if ("ffn", l) not in SEG_KEYS:
            continue
        segs[("ffn", l)] = np.concatenate([_tile_w(wg[l], fcols).ravel(), _tile_w(wu[l], fcols).ravel(),
                                           _tile_w(wd[l], mcols).ravel()])
    shards = [[] for _ in range(8)]
    for si, key in enumerate(SEG_KEYS):
        a = segs[key]
        assert a.size == SEG_ROWS[si] * 1024, (key, a.size)
        a = a.reshape(SEG_ROWS[si], 1024)
        rs = SEG_ROWS[si] // 4
        parts = [[] for _ in range(4)]
        k = 0
        while k < rs:
            cch = min(256, rs - k)
            for r in range(4):
                parts[r].append(a[4 * k + r * cch:4 * k + (r + 1) * cch])
            k += cch
        for c in range(8):
            shards[c].append(np.concatenate(parts[c % 4], 0).ravel())
    rel_bias = f(inp["attn_rel_bias"])
    eidx = np.clip(np.arange(1536) - 511, -256, 256) + 256
    w_a, w_i = f(inp["rg_w_a"]), f(inp["rg_w_i"])
    conv_w, conv_b = f(inp["rg_conv_w"]), f(inp["rg_conv_b"])
    b_a, b_i, lam = f(inp["rg_b_a"]), f(inp["rg_b_i"]), f(inp["rg_lambda"])
    maps = []
    for c in range(8):
        b, r = c // 4, c % 4
        m = dict(shared)
        m["xT"] = np.ascontiguousarray(x[b, r * NT:(r + 1) * NT, :].T)
        wfull = np.concatenate(shards[c]).reshape(WSH_ROWS, 1024)
        m["wsh0"] = np.ascontiguousarray(wfull[:WSH_SPLIT])
        m["wsh1"] = np.ascontiguousarray(wfull[WSH_SPLIT:])
        m["biasext"] = np.ascontiguousarray(
            np.stack([np.stack([rel_bias[j, 2 * r + u][eidx] for u in range(2)], 0) for j in range(2)], 0))
        n = r
        m["r_wa"] = np.stack([_tile_w(w_a[j, n], [np.arange(256)])[0] for j in range(2)], 0)
        m["r_wi"] = np.stack([_tile_w(w_i[j, n], [np.arange(256)])[0] for j in range(2)], 0)
        sm = np.zeros((2, 128, 16), np.float32)
        for j in range(2):
            for hf in range(2):
                ch = n * 256 + hf * 128 + ar
                for jw in range(4):
                    sm[j, :, 4 * hf + jw] = conv_w[j, jw, 0, ch]
                sm[j, :, 8 + hf] = conv_b[j, ch]
                sm[j, :, 10 + hf] = b_a[j].reshape(-1)[ch]
                sm[j, :, 12 + hf] = b_i[j].reshape(-1)[ch]
                sm[j, :, 14 + hf] = lam[j, ch]
        m["r_small"] = sm
        maps.append(m)
    return maps


import os as _os
ATT_STOP = int(_os.environ.get("ATT_STOP", "0"))
FULL_PLAN = [("attn", 0, 0), ("ffn", 0), ("rec", 0, 1), ("ffn", 1), ("attn", 1, 2), ("ffn", 2), ("rec", 1, 3), ("ffn", 3)]
_PROG_CACHE = {}


def run_plan(plan, maps):
    key = tuple(plan)
    if key not in _PROG_CACHE:
        _PROG_CACHE[key] = Prog(plan)
    prog = _PROG_CACHE[key]
    res = run_bass_kernel_spmd(prog.nc, maps, core_ids=list(range(8)))
    return [r["yT"] for r in res.results]


def assemble(ys):
    out = np.zeros((2, S, D), np.float32)
    for c in range(8):
        b, r = c // 4, c % 4
        out[b, r * NT:(r + 1) * NT, :] = ys[c].T
    return out


def kernel(**inputs):
    maps = prepare_inputs(inputs)
    ys = run_plan(FULL_PLAN, maps)
    return assemble(ys)
```
